# Optimizing a Trainium2 kernel written in Bass

```python
import numpy as np
import jax
import jax.numpy as jnp
from jax import lax

D_MODEL = 1024
BATCH = 16
SEQ = 2048
DEPTH = 2

CTX_LEN = 256
GRID_W = 64

NA_HEADS = 4
NA_HEAD_DIM = 64
NA_WIN_ROWS = 8
NA_WIN_COLS = 16
NA_QBLOCK_COLS = 16
NA_KBLOCK_COLS = 32
MLA_HEADS = 4
MLA_NOPE_DIM = 64
MLA_ROPE_DIM = 32
MLA_V_DIM = 64
MLA_Q_RANK = 256
MLA_KV_RANK = 128
MLA_QBLOCK = 128
CM_GROUPS = 4
CM_GROUP_DIM = 64
CM_CHUNK = 128
SC_WIDTH = 256
SC_TAPS = 3

NA_WIDTH = NA_HEADS * NA_HEAD_DIM
MLA_WIDTH = MLA_HEADS * MLA_V_DIM
CM_WIDTH = CM_GROUPS * CM_GROUP_DIM
D_MIX = NA_WIDTH + MLA_WIDTH + CM_WIDTH + SC_WIDTH

IN_SIZES = (NA_WIDTH, NA_WIDTH, NA_WIDTH, NA_WIDTH,
            MLA_Q_RANK, MLA_KV_RANK, MLA_ROPE_DIM, MLA_WIDTH,
            CM_WIDTH, CM_WIDTH, CM_WIDTH,
            SC_WIDTH, SC_WIDTH, SC_WIDTH, SC_WIDTH)
D_IN = sum(IN_SIZES)
IN_SPLIT_POINTS = tuple(int(s) for s in np.cumsum(IN_SIZES)[:-1])

ROPE_BASE = 10000.0
NORM_EPS = 1e-6
NEG_INF = -1e9

kernel_name = "hybrid_headgroup_diffusion_block"


def rms_norm(x, g):
    xf = x.astype(jnp.float32)
    y = xf * lax.rsqrt(jnp.mean(xf * xf, axis=-1, keepdims=True) + NORM_EPS)
    return (y * g.astype(jnp.float32)).astype(x.dtype)


def layer_norm(x, g):
    xf = x.astype(jnp.float32)
    mu = jnp.mean(xf, axis=-1, keepdims=True)
    var = jnp.mean(jnp.square(xf - mu), axis=-1, keepdims=True)
    return ((xf - mu) * lax.rsqrt(var + NORM_EPS) * g.astype(jnp.float32)).astype(x.dtype)


def modulate(h, shift, scale):
    return h * (1 + scale) + shift


def heads(t, n_heads):
    return t.reshape(*t.shape[:-1], n_heads, t.shape[-1] // n_heads)


def attend_dense(q, k, v):
    s = jnp.einsum('bqhd,bkhd->bhqk', q, k).astype(jnp.float32) * (q.shape[-1] ** -0.5)
    p = jax.nn.softmax(s, axis=-1).astype(v.dtype)
    return jnp.einsum('bhqk,bkhd->bqhd', p, v)


def axial_rope(x):
    length = x.shape[1]
    nf = x.shape[-1] // 4
    t = jnp.arange(length)
    row = (t // GRID_W).astype(jnp.float32)
    col = (t % GRID_W).astype(jnp.float32)
    inv = ROPE_BASE ** (-jnp.arange(nf, dtype=jnp.float32) / nf)
    ang = jnp.stack([row[:, None] * inv, col[:, None] * inv], axis=1)
    cos = jnp.cos(ang)[:, None].astype(x.dtype)
    sin = jnp.sin(ang)[:, None].astype(x.dtype)
    xa = x.reshape(*x.shape[:-1], 2, 2, nf)
    x1, x2 = xa[..., 0, :], xa[..., 1, :]
    out = jnp.stack([x1 * cos - x2 * sin, x2 * cos + x1 * sin], axis=-2)
    return out.reshape(x.shape)


def na_tables(rpb, rows, kr):
    ncb = GRID_W // NA_QBLOCK_COLS
    r = np.arange(rows)
    rs = np.clip(r - kr // 2, 0, rows - kr)
    roff = rs[:, None] + np.arange(kr)[None, :] - r[:, None]
    qcol = np.arange(ncb)[:, None] * NA_QBLOCK_COLS + np.arange(NA_QBLOCK_COLS)[None, :]
    kcs = np.clip(np.arange(ncb) * NA_QBLOCK_COLS - NA_WIN_COLS // 2, 0, GRID_W - NA_KBLOCK_COLS)
    kcol = kcs[:, None] + np.arange(NA_KBLOCK_COLS)[None, :]
    cs = np.clip(qcol - NA_WIN_COLS // 2, 0, GRID_W - NA_WIN_COLS)
    kc3 = kcol[:, None, :]
    inwin = (kc3 >= cs[..., None]) & (kc3 < cs[..., None] + NA_WIN_COLS)
    coff = np.clip(kc3 - qcol[..., None], -(NA_WIN_COLS - 1), NA_WIN_COLS - 1)
    ri = (roff + NA_WIN_ROWS - 1)[:, None, None, :, None]
    ci = (coff + NA_WIN_COLS - 1)[None, :, :, None, :]
    bias = rpb[:, ri, ci].astype(jnp.float32)
    bias = jnp.where(inwin[None, None, :, :, None, :], bias, NEG_INF)
    return jnp.moveaxis(bias, 1, 0), kcol


def neighbourhood_attention(q, k, v, kc, vc, rpb):
    b_, length, n_h, d = q.shape
    rows = length // GRID_W
    kr = min(NA_WIN_ROWS, rows)
    ncb = GRID_W // NA_QBLOCK_COLS
    nloc = kr * NA_KBLOCK_COLS
    scale = d ** -0.5
    bias, kcol = na_tables(rpb, rows, kr)
    kg = k.reshape(b_, rows, GRID_W, n_h, d)
    vg = v.reshape(b_, rows, GRID_W, n_h, d)
    qr = jnp.moveaxis(q.reshape(b_, rows, ncb, NA_QBLOCK_COLS, n_h, d), 1, 0)

    def row_fn(args):
        r, q_r, b_r = args
        rs = jnp.clip(r - kr // 2, 0, rows - kr)
        k_blk = lax.dynamic_slice_in_dim(kg, rs, kr, axis=1)[:, :, kcol]
        v_blk = lax.dynamic_slice_in_dim(vg, rs, kr, axis=1)[:, :, kcol]
        s_loc = jnp.einsum('bjmhd,bijnhd->bhjmin', q_r, k_blk).astype(jnp.float32) * scale + b_r[None]
        s_ctx = jnp.einsum('bjmhd,bchd->bhjmc', q_r, kc).astype(jnp.float32) * scale
        s = jnp.concatenate([s_loc.reshape(*s_loc.shape[:4], nloc), s_ctx], axis=-1)
        p = jax.nn.softmax(s, axis=-1).astype(v.dtype)
        p_loc = p[..., :nloc].reshape(s_loc.shape)
        p_ctx = p[..., nloc:]
        return (jnp.einsum('bhjmin,bijnhd->bjmhd', p_loc, v_blk)
                + jnp.einsum('bhjmc,bchd->bjmhd', p_ctx, vc))

    out = lax.map(row_fn, (jnp.arange(rows), qr, bias))
    return jnp.moveaxis(out, 0, 1).reshape(b_, length, n_h * d)


def mla_queries(c_q, qn_g, w_uq):
    q = rms_norm(c_q, qn_g) @ w_uq
    return heads(q, MLA_HEADS)


def mla_rope_query(q):
    return jnp.concatenate([q[..., :MLA_NOPE_DIM], axial_rope(q[..., MLA_NOPE_DIM:])], axis=-1)


def mla_keys_values(c_kv, k_rope, kvn_g, w_ukv):
    kv = heads(rms_norm(c_kv, kvn_g) @ w_ukv, MLA_HEADS)
    return kv[..., :MLA_NOPE_DIM], k_rope[:, :, None, :], kv[..., MLA_NOPE_DIM:]


def mla_join_key(k_nope, k_pe):
    k_pe = jnp.broadcast_to(k_pe, k_nope.shape[:-1] + (MLA_ROPE_DIM,))
    return jnp.concatenate([k_nope, k_pe], axis=-1)


def mla_block_attention(q, k, v, kc, vc):
    k_all = jnp.concatenate([kc, k], axis=1)
    v_all = jnp.concatenate([vc, v], axis=1)
    b_, length, n_h, dq = q.shape
    nb = length // MLA_QBLOCK
    qb = jnp.moveaxis(q.reshape(b_, nb, MLA_QBLOCK, n_h, dq), 1, 0)
    out = lax.map(lambda q_blk: attend_dense(q_blk, k_all, v_all), qb)
    return jnp.moveaxis(out, 0, 1).reshape(b_, length, n_h * MLA_V_DIM)


def spatial_gating(u, v, ln_g, w_s, b_s):
    b_, length, _ = u.shape
    nc = length // CM_CHUNK
    vg = v.reshape(b_, nc, CM_CHUNK, CM_GROUPS, CM_GROUP_DIM)
    vn = layer_norm(vg, ln_g.reshape(CM_GROUPS, CM_GROUP_DIM))
    s = jnp.einsum('gij,bnjgc->bnigc', w_s, vn) + b_s.T[:, :, None]
    return u * s.reshape(b_, length, CM_WIDTH)


def short_conv(gb, gc, h, w):
    length = h.shape[1]
    pad = SC_TAPS // 2
    zp = jnp.pad(gc * h, ((0, 0), (pad, pad), (0, 0)))
    y = sum(zp[:, i:i + length] * w[i] for i in range(SC_TAPS))
    return gb * y


def merge_branches(outs, gates, w_out):
    return jnp.concatenate([o * jax.nn.silu(g) for o, g in zip(outs, gates)], axis=-1) @ w_out


def hybrid_layer(x, ctx, c, c_ctx, norm_g, w_mod, b_mod, w_in, na_rpb, mla_qn_g, mla_w_uq,
                 mla_kvn_g, mla_w_ukv, cm_ln_g, cm_w_s, cm_b_s, sc_w, w_out, update_ctx):
    mod_x = jax.nn.silu(c) @ w_mod + b_mod
    mod_c = jax.nn.silu(c_ctx) @ w_mod + b_mod
    sh_x, sc_x, g_x = jnp.split(mod_x[:, None, :], 3, axis=-1)
    sh_c, sc_c, g_c = jnp.split(mod_c, 3, axis=-1)
    hx = modulate(rms_norm(x, norm_g), sh_x, sc_x)
    hc = modulate(rms_norm(ctx, norm_g), sh_c, sc_c)
    (a_q, a_k, a_v, a_g, b_cq, b_ckv, b_kr, b_g,
     c_u, c_v, c_g, d_b, d_c, d_h, d_g) = jnp.split(hx @ w_in, IN_SPLIT_POINTS, axis=-1)
    (a_qc, a_kc, a_vc, a_gc, b_cqc, b_ckvc, b_krc, b_gc,
     c_uc, c_vc, c_gc, d_bc, d_cc, d_hc, d_gc) = jnp.split(hc @ w_in, IN_SPLIT_POINTS, axis=-1)

    a_kc_h = heads(a_kc, NA_HEADS)
    a_vc_h = heads(a_vc, NA_HEADS)
    b_kn_c, b_kpe_c, b_v_c = mla_keys_values(b_ckvc, b_krc, mla_kvn_g, mla_w_ukv)
    b_k_c = mla_join_key(b_kn_c, b_kpe_c)

    out_a = neighbourhood_attention(heads(a_q, NA_HEADS), heads(a_k, NA_HEADS), heads(a_v, NA_HEADS),
                                    a_kc_h, a_vc_h, na_rpb)
    b_q = mla_rope_query(mla_queries(b_cq, mla_qn_g, mla_w_uq))
    b_kn, b_kpe, b_v = mla_keys_values(b_ckv, b_kr, mla_kvn_g, mla_w_ukv)
    b_k = mla_join_key(b_kn, axial_rope(b_kpe))
    out_b = mla_block_attention(b_q, b_k, b_v, b_k_c, b_v_c)
    out_c = spatial_gating(jax.nn.gelu(c_u, approximate=False), jax.nn.gelu(c_v, approximate=False),
                           cm_ln_g, cm_w_s, cm_b_s)
    out_d = short_conv(d_b, d_c, d_h, sc_w)
    x_new = x + g_x * merge_branches((out_a, out_b, out_c, out_d), (a_g, b_g, c_g, d_g), w_out)

    if update_ctx:
        bc_, lc_, _ = ctx.shape
        ctx_a = attend_dense(heads(a_qc, NA_HEADS), a_kc_h, a_vc_h).reshape(bc_, lc_, NA_WIDTH)
        ctx_b = attend_dense(mla_queries(b_cqc, mla_qn_g, mla_w_uq), b_k_c, b_v_c).reshape(bc_, lc_, MLA_WIDTH)
        ctx_c = spatial_gating(jax.nn.gelu(c_uc, approximate=False), jax.nn.gelu(c_vc, approximate=False),
                               cm_ln_g, cm_w_s, cm_b_s)
        ctx_d = short_conv(d_bc, d_cc, d_hc, sc_w)
        ctx = ctx + g_c * merge_branches((ctx_a, ctx_b, ctx_c, ctx_d), (a_gc, b_gc, c_gc, d_gc), w_out)
    return x_new, ctx


def setup_inputs(seed: int = 0) -> dict:
    key = jax.random.key(seed)
    ks = jax.random.split(key, 24)
    f32 = jnp.float32

    def nrm(k, shape, scale):
        return jax.random.normal(k, shape, f32) * scale

    return {
        "x": nrm(ks[0], (BATCH, SEQ, D_MODEL), 1.0),
        "c": nrm(ks[1], (BATCH, D_MODEL), 1.0),
        "ctx": nrm(ks[2], (BATCH, CTX_LEN, D_MODEL), 1.0),
        "c_ctx": nrm(ks[3], (D_MODEL,), 1.0),
        "norm_g": 1.0 + nrm(ks[4], (DEPTH, D_MODEL), 0.02),
        "w_mod": nrm(ks[5], (DEPTH, D_MODEL, 3 * D_MODEL), 0.5 * D_MODEL ** -0.5),
        "b_mod": nrm(ks[6], (DEPTH, 3 * D_MODEL), 0.01),
        "w_in": nrm(ks[7], (DEPTH, D_MODEL, D_IN), D_MODEL ** -0.5),
        "na_rpb": nrm(ks[8], (DEPTH, NA_HEADS, 2 * NA_WIN_ROWS - 1, 2 * NA_WIN_COLS - 1), 0.1),
        "mla_qn_g": 1.0 + nrm(ks[9], (DEPTH, MLA_Q_RANK), 0.02),
        "mla_w_uq": nrm(ks[10], (DEPTH, MLA_Q_RANK, MLA_HEADS * (MLA_NOPE_DIM + MLA_ROPE_DIM)), MLA_Q_RANK ** -0.5),
        "mla_kvn_g": 1.0 + nrm(ks[11], (DEPTH, MLA_KV_RANK), 0.02),
        "mla_w_ukv": nrm(ks[12], (DEPTH, MLA_KV_RANK, MLA_HEADS * (MLA_NOPE_DIM + MLA_V_DIM)), MLA_KV_RANK ** -0.5),
        "cm_ln_g": 1.0 + nrm(ks[13], (DEPTH, CM_WIDTH), 0.02),
        "cm_w_s": nrm(ks[14], (DEPTH, CM_GROUPS, CM_CHUNK, CM_CHUNK), CM_CHUNK ** -0.5),
        "cm_b_s": 1.0 + nrm(ks[15], (DEPTH, CM_GROUPS, CM_CHUNK), 0.02),
        "sc_w": nrm(ks[16], (DEPTH, SC_TAPS, SC_WIDTH), SC_TAPS ** -0.5),
        "w_out": nrm(ks[17], (DEPTH, D_MIX, D_MODEL), D_MIX ** -0.5),
        "final_g": 1.0 + nrm(ks[18], (D_MODEL,), 0.02),
    }


def reference(x, c, ctx, c_ctx, norm_g, w_mod, b_mod, w_in, na_rpb, mla_qn_g, mla_w_uq, mla_kvn_g,
              mla_w_ukv, cm_ln_g, cm_w_s, cm_b_s, sc_w, w_out, final_g):
    for l in range(DEPTH):
        x, ctx = hybrid_layer(x, ctx, c, c_ctx, norm_g[l], w_mod[l], b_mod[l], w_in[l], na_rpb[l],
                              mla_qn_g[l], mla_w_uq[l], mla_kvn_g[l], mla_w_ukv[l], cm_ln_g[l],
                              cm_w_s[l], cm_b_s[l], sc_w[l], w_out[l], update_ctx=(l < DEPTH - 1))
    return rms_norm(x, final_g)
```

```python
import contextlib
import os
DBG = os.environ.get('DBG', '')
import numpy as np
import concourse.bass as bass
import concourse.mybir as mybir
from concourse.bass_utils import run_bass_kernel_spmd

F32 = mybir.dt.float32
BF16 = mybir.dt.bfloat16
AF = mybir.ActivationFunctionType
ALU = mybir.AluOpType
AX = mybir.AxisListType

NCORES = 8
D = 1024
KT = 8
SEQ = 2048
CTXL = 256
T = SEQ + CTXL
NT = T // 128
DIN = 3488
EPS = 1e-6
MASK_FILL = -100.0
NTAB = 21


class _Op:
    __slots__ = ("eng", "fn", "deps", "raw", "is_dma", "dsem", "dcount", "ms", "need_inc", "waits")


def _foot(ap):
    t = ap.tensor
    kind = type(t).__name__
    pat = ap.ap
    off = int(ap.offset)
    if kind.startswith("DRam"):
        ext = 1
        for st, cnt in pat:
            ext += (cnt - 1) * abs(st)
        return (t.name, 0, 1, off, off + ext)
    row = 1
    for d in list(t.shape)[1:]:
        row *= int(d)
    p0 = off // row
    lo = off % row
    npart = pat[0][1]
    ext = 1
    for st, cnt in pat[1:]:
        ext += (cnt - 1) * abs(st)
    return (t.name, p0, p0 + npart, lo, lo + ext)


class Prog:
    ENG = ("pe", "act", "dve", "pool", "sp")
    NDS = 8

    def __init__(self):
        self.ops = []
        self.acc = {}
        self.dcnt = {q: [0] * self.NDS for q in ("sp", "pool", "act")}
        self.drr = {q: 0 for q in ("sp", "pool", "act")}

    def _access(self, ap, idx, is_write, deps, eng, is_dma, raw):
        name, p0, p1, lo, hi = _foot(ap)
        rw = is_write
        q0, q1, l0, h0 = p0, p1, lo, hi
        if type(ap.tensor).__name__.startswith("PSum"):
            be = 512 if ap.dtype == F32 else 1024
            p0, p1, is_write = 0, 128, True
            lo, hi = (lo // be) * be, -(-hi // be) * be
        lst = self.acc.setdefault(name, [])
        keep = []
        for e in lst:
            ov = not (e[1] <= p0 or p1 <= e[0] or e[3] <= lo or hi <= e[2])
            if ov and (is_write or e[5]):
                deps.add(e[4])
                if (not rw) and e[8] and not (e[10] <= q0 or q1 <= e[9] or e[12] <= l0 or h0 <= e[11]):
                    raw.add(e[4])
            if is_write and ov and e[0] >= p0 and e[1] <= p1 and e[2] >= lo and e[3] <= hi:
                continue
            if (not is_write) and (not e[5]) and (not is_dma) and e[6] == eng and (not e[7]) \
                    and e[0] == p0 and e[1] == p1 and e[2] == lo and e[3] == hi:
                continue
            keep.append(e)
        keep.append((p0, p1, lo, hi, idx, is_write, eng, is_dma, rw, q0, q1, l0, h0))
        self.acc[name] = keep

    def add(self, eng, fn, reads=(), writes=(), is_dma=False):
        op = _Op()
        op.eng = eng
        op.fn = fn
        op.is_dma = is_dma
        op.need_inc = is_dma
        op.ms = 0
        idx = len(self.ops)
        deps = set()
        raw = set()
        for ap in reads:
            if ap is not None and not isinstance(ap, (int, float)):
                self._access(ap, idx, False, deps, eng, is_dma, raw)
        for ap in writes:
            self._access(ap, idx, True, deps, eng, is_dma, raw)
        deps.discard(idx)
        raw.discard(idx)
        op.deps = deps
        op.raw = raw
        if is_dma:
            k = self.drr[eng]
            self.drr[eng] = (k + 1) % self.NDS
            op.dsem = (eng, k)
            self.dcnt[eng][k] += 1
            op.dcount = self.dcnt[eng][k]
        self.ops.append(op)
        return idx

    def mm(self, out, lhsT, rhs, start=True, stop=True, skip=False):
        self.add("pe", lambda e: e.matmul(out, lhsT=lhsT, rhs=rhs, start=start, stop=stop, skip_group_check=skip),
                 reads=[lhsT, rhs], writes=[out])

    def tr(self, out, in_, ident):
        self.add("pe", lambda e: e.transpose(out, in_, ident), reads=[in_, ident], writes=[out])

    def act(self, out, in_, func, bias=None, scale=None, accum_out=None):
        kw = {}
        if bias is not None:
            kw["bias"] = bias
        if scale is not None:
            kw["scale"] = scale
        if accum_out is not None:
            kw["accum_out"] = accum_out
        w = [out] + ([accum_out] if accum_out is not None else [])
        self.add("act", lambda e: e.activation(out, in_, func, **kw), reads=[in_, bias, scale], writes=w)

    def tt(self, eng, out, in0, in1, op):
        self.add(eng, lambda e: e.tensor_tensor(out, in0, in1, op), reads=[in0, in1], writes=[out])

    def ts(self, eng, out, in0, s1, s2=None, op0=ALU.mult, op1=None):
        if op1 is None:
            self.add(eng, lambda e: e.tensor_scalar(out, in0, s1, None, op0), reads=[in0, s1], writes=[out])
        else:
            self.add(eng, lambda e: e.tensor_scalar(out, in0, s1, s2, op0, op1), reads=[in0, s1, s2], writes=[out])

    def stt(self, eng, out, in0, scalar, in1, op0, op1):
        self.add(eng, lambda e: e.scalar_tensor_tensor(out, in0, scalar, in1, op0, op1),
                 reads=[in0, scalar, in1], writes=[out])

    def cp(self, eng, out, in_):
        if eng == "act":
            self.add("act", lambda e: e.activation(out, in_, AF.Copy), reads=[in_], writes=[out])
        else:
            self.add(eng, lambda e: e.tensor_copy(out, in_), reads=[in_], writes=[out])

    def memset(self, eng, out, val):
        self.add(eng, lambda e: e.memset(out, val), writes=[out])

    def recip(self, out, in_):
        self.add("dve", lambda e: e.reciprocal(out, in_), reads=[in_], writes=[out])

    def red(self, out, in_, op=ALU.add):
        self.add("dve", lambda e: e.tensor_reduce(out, in_, AX.X, op), reads=[in_], writes=[out])

    def dma(self, q, out, in_):
        self.add(q, lambda e: e.dma_start(out=out, in_=in_), reads=[in_], writes=[out], is_dma=True)

    def emit(self, nc):
        ops = self.ops
        for op in ops:
            for d in op.deps:
                D = ops[d]
                if not D.is_dma and (D.eng != op.eng or op.eng != "pe"):
                    D.need_inc = True
        cnt = {e: 0 for e in self.ENG}
        for op in ops:
            if not op.is_dma and op.need_inc:
                cnt[op.eng] += 1
                op.ms = cnt[op.eng]
        waited = {e: {} for e in self.ENG}
        for op in ops:
            need = {}
            for d in op.deps:
                D = ops[d]
                if D.is_dma:
                    key, val = ("d",) + D.dsem, 16 * D.dcount
                elif D.eng == op.eng and op.eng == "pe":
                    continue
                else:
                    key, val = ("e", D.eng), D.ms
                if need.get(key, 0) < val:
                    need[key] = val
            if op.is_dma and op.dcount > 1:
                key = ("d",) + op.dsem
                need[key] = max(need.get(key, 0), 16 * (op.dcount - 1))
            w = waited[op.eng]
            op.waits = []
            for key, val in need.items():
                if w.get(key, 0) < val:
                    w[key] = val
                    op.waits.append((key, val))
        per = {e: [op for op in ops if op.eng == e] for e in self.ENG}
        with contextlib.ExitStack() as es:
            sems = {}
            for e in self.ENG:
                sems[("e", e)] = es.enter_context(nc.semaphore("s_" + e))
            for q in self.dcnt:
                for k in range(self.NDS):
                    sems[("d", q, k)] = es.enter_context(nc.semaphore("d_%s%d" % (q, k)))
            block = es.enter_context(nc.Block())

            def runner(name, final_wait=False):
                def f(e):
                    for op in per[name]:
                        for key, val in op.waits:
                            e.wait_ge(sems[key], val)
                        ins = op.fn(e)
                        if op.is_dma:
                            ins.then_inc(sems[("d",) + op.dsem], 16)
                        elif op.need_inc:
                            ins.then_inc(sems[("e", name)], 1)
                    if final_wait:
                        for q in self.dcnt:
                            for k in range(self.NDS):
                                if self.dcnt[q][k] > 0:
                                    e.wait_ge(sems[("d", q, k)], 16 * self.dcnt[q][k])
                return f

            block.sync(runner("sp", True))
            block.scalar(runner("act"))
            block.vector(runner("dve"))
            block.gpsimd(runner("pool"))
            block.tensor(runner("pe"))


def _rope_perm32():
    p = np.zeros(32, dtype=np.int64)
    for a in range(2):
        for h in range(2):
            for f in range(8):
                p[a * 16 + h * 8 + f] = a * 16 + (1 - h) * 8 + f
    return p


def _rope_tables():
    t = np.arange(SEQ)
    row = (t // 64).astype(np.float32)
    col = (t % 64).astype(np.float32)
    nf = 8
    inv = (np.float32(10000.0) ** (-np.arange(nf, dtype=np.float32) / np.float32(nf))).astype(np.float32)
    C = np.zeros((32, SEQ), dtype=np.float32)
    S = np.zeros((32, SEQ), dtype=np.float32)
    for a in range(2):
        pos = row if a == 0 else col
        ang = (pos[None, :] * inv[:, None]).astype(np.float32)
        c = np.cos(ang).astype(np.float32)
        s = np.sin(ang).astype(np.float32)
        C[a * 16:a * 16 + 8] = c
        C[a * 16 + 8:a * 16 + 16] = c
        S[a * 16:a * 16 + 8] = -s
        S[a * 16 + 8:a * 16 + 16] = s
    return C, S


def _na_local_tiles(tq):
    rows = [2 * tq, 2 * tq + 1]
    ks = set()
    for qr in rows:
        rs = min(max(qr - 4, 0), 24)
        for kr in range(rs, rs + 8):
            ks.add(kr // 2)
    return sorted(ks)


def _na_table_base(tq):
    if 2 <= tq <= 13:
        return 0
    return {0: 5, 1: 9, 14: 13, 15: 17}[tq]


def _na_index_tables():
    ri = np.zeros((NTAB, 128, 128), dtype=np.int64)
    ci = np.zeros((NTAB, 128, 128), dtype=np.int64)
    inw = np.zeros((NTAB, 128, 128), dtype=bool)
    done = set()
    for tq in range(16):
        base = _na_table_base(tq)
        if base in done:
            continue
        done.add(base)
        for j, tk in enumerate(_na_local_tiles(tq)):
            ki = np.arange(128)[:, None]
            qi = np.arange(128)[None, :]
            kr = 2 * tk + ki // 64
            kc = ki % 64
            qr = 2 * tq + qi // 64
            qc = qi % 64
            rs = np.clip(qr - 4, 0, 24)
            cs = np.clip(qc - 8, 0, 48)
            win = (kr >= rs) & (kr < rs + 8) & (kc >= cs) & (kc < cs + 16)
            ri[base + j] = np.clip(kr - qr + 7, 0, 14)
            ci[base + j] = np.clip(kc - qc, -15, 15) + 15
            inw[base + j] = win
    return ri, ci, inw


class _Stop(Exception):
    pass


def build_program(n_layers=2, taps=(), stop=None):
    nc = bass.Bass("TRN2", target_bir_lowering=False)
    P = Prog()
    es = contextlib.ExitStack()

    def din(name, shape, dt=F32):
        return nc.dram_tensor(name, list(shape), dt, kind="ExternalInput").ap()

    x_d = din("x", [2, SEQ, D])
    ctx_d = din("ctx", [2, CTXL, D])
    cT_d = din("cT", [128, 8, 3])
    ident_d = din("ident", [128, 128])
    sel2_d = din("sel2", [2, 2])
    ropeC_d = din("ropeC", [32, SEQ])
    ropeS_d = din("ropeS", [32, SEQ])
    normg_d = din("norm_gT", [2, 128, 8])
    finalg_d = din("final_g", [D])
    wmod_d = din("w_mod", [2, D, 3 * D])
    bmod_d = din("b_mod", [2, 3 * D])
    win_d = din("w_in", [2, D, DIN])
    krsw_d = din("w_krsw", [2, D, 96])
    wout_d = din("w_out", [2, D, D])
    wuq_d = din("w_uq", [2, 256, 384])
    wuqsw_d = din("w_uqsw", [2, 256, 384])
    qng_d = din("qn_gT", [2, 128, 2])
    wukv_d = din("w_ukv", [2, 128, 512])
    kvng_d = din("kvn_gT", [2, 128, 1])
    wsT_d = din("w_sT", [2, 128, 4, 128])
    bs_d = din("b_s", [2, 512])
    lng_d = din("ln_g", [2, 256])
    scw_d = din("sc_wT", [2, 128, 2, 3])
    nab_d = din("na_bias", [2, 128, 4 * NTAB, 128])
    out_d = nc.dram_tensor("out", [2, SEQ, D], F32, kind="ExternalOutput").ap()
    xs_d = nc.dram_tensor("xs_scr", [2, SEQ, D], F32).ap()
    cs_d = nc.dram_tensor("cs_scr", [2, CTXL, D], F32).ap()
    grow_d = nc.dram_tensor("grow_scr", [2, 3, D], F32).ap()
    expm_d = nc.dram_tensor("expm_scr", [2, 128, 4 * NTAB, 128], BF16).ap()
    tap_out = {}

    def sb(name, shape, dt):
        return es.enter_context(nc.sbuf_tensor(name, list(shape), dt))

    def pst(name, shape, dt):
        return es.enter_context(nc.psum_tensor(name, list(shape), dt))

    hxT = sb("hxT", [128, KT, T], BF16)
    mixT = sb("mixT", [128, KT, T], BF16)
    wg = [sb("wg0", [128, KT, 1024], BF16), sb("wg1", [128, KT, 1024], BF16)]
    ident = sb("identb", [128, 128], BF16)
    ones_f = sb("ones_f", [128, 128], F32)
    epsT = sb("epsT", [128, 1], F32)
    siluT = sb("siluT", [128, 8, 3], F32)
    modT = sb("modT", [128, 24, 3], F32)
    sce = sb("sce", [128, 8, 3], F32)
    normgT = sb("normgT", [128, 8], F32)
    identf = sb("identf", [3, 4], F32)
    wuq = sb("wuq", [128, 2, 384], BF16)
    wuqsw = sb("wuqsw", [128, 2, 384], BF16)
    wukv = sb("wukv", [128, 512], BF16)
    krsw = sb("krsw", [128, KT, 96], BF16)
    wsT = sb("wsT", [128, 4, 128], BF16)
    bs2 = sb("bs2", [2, 512], BF16)
    bsh2 = sb("bsh2", [2, 512], BF16)
    ones_b = sb("ones_b", [2, 128], BF16)
    sel2 = sb("sel2s", [2, 2], F32)
    zeroT = sb("zeroT", [128, 1], F32)
    lngbc = sb("lngbc", [128, 256], F32)
    scw = sb("scw", [128, 2, 3], F32)
    qng = sb("qng", [128, 2], F32)
    kvng = sb("kvng", [128, 1], F32)
    stat = sb("stat", [128, 256], F32)
    AB = 28 * 1024 + 512
    AFN = 8 * 1024
    arb = sb("arena_b", [128, AB], BF16)
    arf = sb("arena_f", [128, AFN], F32)
    psD = [pst("psD%d" % i, [128, 1024], F32) for i in range(2)]
    psF = [psD[0][:, 0:512], psD[0][:, 512:1024], psD[1][:, 0:512], psD[1][:, 512:1024],
           pst("psF4", [128, 512], F32)[:, :], pst("psF5", [128, 512], F32)[:, :]]
    psT = [pst("psT%d" % i, [128, 1024], BF16) for i in range(2)]
    cnt = {"f": 0, "t": 0}

    def nbF():
        cnt["f"] += 1
        return psF[cnt["f"] % 4]

    def nbO():
        cnt["o"] = cnt.get("o", 0) + 1
        return psF[4 + cnt["o"] % 2]

    def nbT():
        cnt["t"] += 1
        return psT[cnt["t"] % 2]

    class Carver:
        def __init__(self, t, size):
            self.t, self.size, self.off = t, size, 0

        def reset(self, off=0):
            self.off = off

        def take(self, shape):
            n = 1
            for d_ in shape:
                n *= d_
            assert self.off + n <= self.size, (self.off, n, self.size)
            ap = self.t[:, self.off:self.off + n]
            self.off += n
            if len(shape) == 2:
                return ap.rearrange("p (a b) -> p a b", a=shape[0])
            if len(shape) == 3:
                return ap.rearrange("p (a b c) -> p a b c", a=shape[0], b=shape[1])
            return ap

    CB = Carver(arb, AB)
    CF = Carver(arf, AFN)

    def tap(name, ap):
        if name not in taps:
            return
        shp = list(ap.shape)
        dt = ap.dtype
        dtn = nc.dram_tensor("tap_" + name, shp, dt, kind="ExternalOutput").ap()
        tap_out[name] = dtn
        P.dma("sp", dtn, ap)

    TBLK = [(0, 256)] + [(256 + 512 * i, 512) for i in range(4)]
    evac_rr = {"i": 0}

    def evac_eng():
        evac_rr["i"] += 1
        return "act" if evac_rr["i"] % 2 else "dve"

    def fm_proj(w, c0, nft, evac, blks=TBLK):
        for ft in range(nft):
            for (t0, n) in blks:
                ps = nbF()
                for kt in range(KT):
                    P.mm(ps[:, 0:n], w[:, kt, c0 + ft * 128:c0 + (ft + 1) * 128], hxT[:, kt, t0:t0 + n],
                         start=(kt == 0), stop=(kt == KT - 1))
                evac(ft, t0, n, ps)

    def tm_proj(w, c0, ncols, tt, ps):
        for kt in range(KT):
            P.mm(ps[:, 0:ncols], hxT[:, kt, tt * 128:(tt + 1) * 128], w[:, kt, c0:c0 + ncols],
                 start=(kt == 0), stop=(kt == KT - 1))

    P.dma("pool", ident[:], ident_d[:, :])
    P.dma("sp", identf[0:3, 0:3], ident_d[0:3, 0:3])
    P.dma("sp", sel2[:], sel2_d[:, :])
    P.memset("dve", ones_f[:], 1.0)
    P.memset("dve", epsT[:], EPS)
    P.memset("dve", zeroT[:], 0.0)
    P.memset("dve", ones_b[:], 1.0)
    CF.reset()
    cTs = CF.take([8, 3])
    P.dma("sp", cTs, cT_d[:, :, :])
    P.act(siluT[:], cTs, AF.Silu)

    GRP = [(0, 1024), (1024, 672), (1696, 768), (2464, 1024)]
    wg_state = {"i": 0}

    def load_group(l, g):
        buf = wg[wg_state["i"] % 2]
        wg_state["i"] += 1
        c0, n = GRP[g]
        src = win_d[l].rearrange("(kt p) c -> p kt c", p=128)
        for k2 in range(0, KT, 2):
            P.dma("pool", buf[:, k2:k2 + 2, 0:n], src[:, k2:k2 + 2, c0:c0 + n])
        return buf

    def pipeline(n, stages):
        ns = len(stages)
        for step in range(n + ns - 1):
            for k, st in enumerate(stages):
                i = step - k
                if 0 <= i < n:
                    st(i)

    def rstd_from_ms(dst, src, n):
        P.act(dst, src, AF.Sqrt, bias=epsT[:, 0:1], scale=1.0)
        P.recip(dst, dst)

    try:
      for l in range(n_layers):
          upd = (l == 0)
          last = (l == n_layers - 1)
          CF.reset()
          CB.reset()
          wA_pref = [load_group(l, 0)]
          P.dma("sp", normgT[:], normg_d[l])
          CF.reset()
          grow = CF.take([1024])
          wm = [CF.take([8, 256]) for _ in range(2)]
          rowb = [CF.take([256]) for _ in range(2)]
          bch = [CF.take([256]) for _ in range(2)]
          mst = [CF.take([7, 128]) for _ in range(2)]
          mbf = [CB.take([7, 128]) for _ in range(2)]
          psm = nbF()
          wsrc = wmod_d[l].rearrange("(kt p) c -> p kt c", p=128)
          bsrc = bmod_d[l].rearrange("(o n) -> o n", o=1)
          for j in range(12):
              wmj = wm[j % 2]
              P.dma("sp", wmj, wsrc[:, :, j * 256:(j + 1) * 256])
              P.dma("sp", bch[j % 2][0:1, :], bsrc[:, j * 256:(j + 1) * 256])
              P.dma("sp", mst[j % 2], nab_d[l][:, j * 7:(j + 1) * 7, :])
              P.act(mbf[j % 2], mst[j % 2], AF.Exp)
              P.dma("act", expm_d[l][:, j * 7:(j + 1) * 7, :], mbf[j % 2])
              psr = nbO()
              for kt in range(KT):
                  P.mm(psr[0:3, 0:256], siluT[:, kt, :], wmj[:, kt, :], start=(kt == 0), stop=False)
              P.mm(psr[0:3, 0:256], ones_f[0:1, 0:3], bch[j % 2][0:1, :], start=False, stop=True)
              rb = rowb[j % 2]
              P.cp("dve", rb[0:3, :], psr[0:3, 0:256])
              if j >= 8:
                  P.cp("dve", grow[0:3, (j - 8) * 256:(j - 7) * 256], psr[0:3, 0:256])
              for m in range(2):
                  mt = j * 2 + m
                  P.tr(psm[:, mt * 3:mt * 3 + 3], rb[0:3, m * 128:(m + 1) * 128], identf[0:3, 0:3])
          P.dma("sp", grow_d[l], grow[0:3, :])
          psm3 = psm[:, 0:72].rearrange("p (m b) -> p m b", b=3)
          P.cp("dve", modT[:, :, :], psm3)
          for b in range(3):
              P.stt("dve", sce[:, :, b], modT[:, 8:16, b], 1.0, normgT[:], ALU.add, ALU.mult)
          tap("modT%d" % l, modT[:])
          CF.reset()
          P.dma("act", qng[:], qng_d[l])
          P.dma("act", kvng[:], kvng_d[l])
          P.dma("act", lngbc[:], lng_d[l].partition_broadcast(128))
          P.dma("act", scw[:], scw_d[l])
          P.dma("pool", wsT[:], wsT_d[l])
          P.dma("pool", krsw[:], krsw_d[l].rearrange("(kt p) c -> p kt c", p=128))
          wst = CF.take([2, 384])
          for (dst, src) in ((wuq, wuq_d), (wuqsw, wuqsw_d)):
              P.dma("act", wst, src[l].rearrange("(kt p) c -> p kt c", p=128))
              for k2 in range(2):
                  P.ts("dve", dst[:, k2, :], wst[:, k2, :], qng[:, k2:k2 + 1])
          wst2 = CF.take([512])
          P.dma("act", wst2, wukv_d[l])
          P.ts("dve", wukv[:], wst2, kvng[:, 0:1])

          bs2f = CF.take([512])
          bs2t = CF.take([512])
          P.dma("act", bs2f[0:2, :], bs_d[l].partition_broadcast(2))
          P.cp("dve", bsh2[0:2, :], bs2f[0:2, :])
          P.tt("dve", bs2f[0:2, :], bs2f[0:2, :], bsh2[0:2, :], ALU.subtract)
          P.ts("dve", bs2t[0:2, :], bsh2[0:2, :], sel2[0:2, 0:1])
          P.stt("dve", bs2[0:2, :], bs2f[0:2, :], sel2[0:2, 1:2], bs2t[0:2, :], ALU.mult, ALU.add)

          if stop == 'M':
              raise _Stop()
          for s in range(2):
              CF.reset()
              CB.reset()
              aqT = CB.take([2, T])
              akT = CB.take([2, T])
              Va = CB.take([NT, 4, 65])
              maskb = [CB.take([NTAB, 128]) for _ in range(2)]
              PT = [CB.take([7, 128]) for _ in range(3)]
              oa_off = CB.off
              oa = CB.take([NT, 256])
              NXS = 7
              xst = [CF.take([1024]) for _ in range(NXS)]
              xn = [arb[:, oa_off + i * 1024:oa_off + (i + 1) * 1024] for i in range(3)]
              junk = arb[:, oa_off + 3072:oa_off + 4096]
              ssq = stat[:, 0:NT]
              rsd = stat[:, 32:32 + NT]
              P.memset("dve", ssq, 0.0)
              xsrc = x_d if l == 0 else xs_d
              csrc = ctx_d if l == 0 else cs_d
              wcur = wA_pref[0]
              pTn = {}

              def n_sd(tt):
                  src = csrc[s, tt * 128:(tt + 1) * 128, :] if tt < 2 else xsrc[s, (tt - 2) * 128:(tt - 1) * 128, :]
                  P.dma("sp", xst[tt % NXS], src)

              def n_s0(tt):
                  P.act(junk, xst[tt % NXS], AF.Square, scale=1.0 / 32.0, accum_out=ssq[:, tt:tt + 1])

              def n_s1a(tt):
                  P.act(rsd[:, tt:tt + 1], ssq[:, tt:tt + 1], AF.Sqrt, bias=epsT[:, 0:1], scale=1.0)

              def n_s1b(tt):
                  P.recip(rsd[:, tt:tt + 1], rsd[:, tt:tt + 1])

              def n_s1(tt):
                  P.tt("pool", xn[tt % 3][:, 0:640], xst[tt % NXS][:, 0:640], rsd[:, tt:tt + 1].to_broadcast([128, 640]), ALU.mult)
                  P.ts("dve", xn[tt % 3][:, 640:1024], xst[tt % NXS][:, 640:1024], rsd[:, tt:tt + 1])

              def n_s2(tt):
                  pT = nbT()
                  pTn[tt] = pT
                  xb = xn[tt % 3]
                  for kt in range(KT):
                      P.tr(pT[:, kt * 128:(kt + 1) * 128], xb[:, kt * 128:(kt + 1) * 128], ident[:])

              def n_s3(tt):
                  b = 2 if tt < 2 else s
                  pT = pTn.pop(tt)
                  for kt in range(KT):
                      o = hxT[:, kt, tt * 128:(tt + 1) * 128]
                      i_ = pT[:, kt * 128:(kt + 1) * 128]
                      if tt % 3 == 0:
                          P.act(o, i_, AF.Identity, bias=modT[:, kt, b:b + 1], scale=sce[:, kt, b:b + 1])
                      else:
                          P.ts("dve", o, i_, sce[:, kt, b:b + 1], modT[:, kt, b:b + 1], ALU.mult, ALU.add)

              blk_done = {(t0 + n) // 128 - 1: (t0, n) for (t0, n) in TBLK}
              P.memset("pool", Va[:, :, :, 64:65], 1.0)

              aitems = []

              def a_item_fm(ft, c0, kind, t0, n):
                  def f():
                      ps = nbF()
                      for kt in range(KT):
                          P.mm(ps[:, 0:n], wcur[:, kt, c0 + ft * 128:c0 + (ft + 1) * 128], hxT[:, kt, t0:t0 + n],
                               start=(kt == 0), stop=(kt == KT - 1))
                      if kind == "g":
                          P.act(mixT[:, ft, t0:t0 + n], ps[:, 0:n], AF.Silu)
                      elif kind == "q":
                          P.cp("dve", aqT[:, ft, t0:t0 + n], ps[:, 0:n])
                      else:
                          P.cp("act", akT[:, ft, t0:t0 + n], ps[:, 0:n])
                  return f

              def a_item_v(t2):
                  def f():
                      ps = nbF()
                      tm_proj(wcur, 512, 256, t2, ps)
                      P.cp("dve", Va[:, t2, :, 0:64], ps[:, 0:256].rearrange("p (h d) -> p h d", h=4))
                  return f

              def n_s4(tt):
                  if tt in blk_done:
                      t0, n = blk_done[tt]
                      for ft in range(2):
                          for (c0, kind) in ((768, "g"), (0, "q"), (256, "k")):
                              aitems.append(a_item_fm(ft, c0, kind, t0, n))
                      for t2 in range(t0 // 128, (t0 + n) // 128):
                          aitems.append(a_item_v(t2))
                  for _ in range(3):
                      if aitems:
                          aitems.pop(0)()

              pipeline(NT, [n_sd, n_s0, n_s1a, n_s1b, n_s1, n_s2, n_s3, n_s4])
              while aitems:
                  aitems.pop(0)()
              if s == 0:
                  tap("hxT%d" % l, hxT[:])

              if stop == 'N':
                  raise _Stop()
              rden = stat[:, 64:72]
              wnext = load_group(l, 1)

              def ev_q(ft, t0, n, ps):
                  P.cp(evac_eng(), aqT[:, ft, t0:t0 + n], ps[:, 0:n])

              def ev_k(ft, t0, n, ps):
                  P.cp(evac_eng(), akT[:, ft, t0:t0 + n], ps[:, 0:n])

              def ev_gate(slot):
                  def f(ft, t0, n, ps):
                      P.act(mixT[:, slot + ft, t0:t0 + n], ps[:, 0:n], AF.Silu)
                  return f


              def na_stage1(h, qt, PTt, wi=0):
                  ft, pb = h // 2, (h % 2) * 64
                  if qt >= 2:
                      loc = _na_local_tiles(qt - 2)
                      slots = [0, 1] + [t_ + 2 for t_ in loc]
                      mask = (2, _na_table_base(qt - 2), len(loc))
                  else:
                      slots = [0, 1]
                      mask = None
                  ns = len(slots)
                  banks = [nbF(), nbF()] if ns > 4 else [nbF()]
                  for j, ktile in enumerate(slots):
                      bk = banks[j // 4]
                      P.mm(bk[:, (j % 4) * 128:(j % 4 + 1) * 128],
                           akT[pb:pb + 64, ft, ktile * 128:(ktile + 1) * 128],
                           aqT[pb:pb + 64, ft, qt * 128:(qt + 1) * 128])
                  for bi, bk in enumerate(banks):
                      n_here = min(4, ns - bi * 4)
                      P.act(PTt[:, bi * 4:bi * 4 + n_here, :],
                            bk[:, 0:n_here * 128].rearrange("p (j q) -> p j q", j=n_here),
                            AF.Exp, scale=0.125)
                  if mask is not None:
                      mj, mt0, mn = mask
                      P.tt("pool", PTt[:, mj:mj + mn, :], PTt[:, mj:mj + mn, :], maskb[h % 2][:, mt0:mt0 + mn, :], ALU.mult)
                  return slots

              def na_stage2(h, qt, PTt, slots):
                  ns = len(slots)
                  po = nbO()
                  for j, ktile in enumerate(slots):
                      P.mm(po[:, 0:65], PTt[:, j, :], Va[:, ktile, h, :], start=(j == 0), stop=(j == ns - 1))
                  P.recip(rden[:, 0:1], po[:, 64:65])
                  P.ts("dve", oa[:, qt, h * 64:(h + 1) * 64], po[:, 0:64], rden[:, 0:1])

              work = []
              for h in range(4):
                  for qt in (list(range(2, NT)) + ([0, 1] if upd else [])):
                      work.append((h, qt))
              pend = []
              lasth = -1
              for wi, (h, qt) in enumerate(work):
                  if h != lasth:
                      P.dma("sp", maskb[h % 2], expm_d[l][:, h * NTAB:(h + 1) * NTAB, :])
                      lasth = h
                  PTt = PT[wi % 3]
                  slots = na_stage1(h, qt, PTt, wi)
                  pend.append((h, qt, PTt, slots))
                  if len(pend) > 2:
                      na_stage2(*pend.pop(0))
              while pend:
                  na_stage2(*pend.pop(0))
              for qt in (list(range(2, NT)) + ([0, 1] if upd else [])):
                  pT = nbT()
                  for ft in range(2):
                      P.tr(pT[:, ft * 128:(ft + 1) * 128], oa[:, qt, ft * 128:(ft + 1) * 128], ident[:])
                  for ft in range(2):
                      o = mixT[:, ft, qt * 128:(qt + 1) * 128]
                      P.tt("dve", o, pT[:, ft * 128:(ft + 1) * 128], o, ALU.mult)

              if stop == 'A':
                  raise _Stop()
              wcur = wnext
              CB.reset()
              CF.reset()
              cT3 = CB.take([3, T])
              KTh = [CB.take([T]) for _ in range(2)]
              QTh = [CB.take([T]) for _ in range(2)]
              krT = KTh[0]
              Vb = CB.take([NT, 4, 65])
              PTb2 = [CB.take([1024]) for _ in range(2)]
              ob = CB.take([NT, 256])
              cst = [CB.take([384]) for _ in range(3)]
              junkb = CB.take([256])
              junkb2 = CB.take([128])
              ropeC = CF.take([SEQ])
              ropeS = CF.take([SEQ])
              tmp1 = [CF.take([512]) for _ in range(2)]
              tmp2 = [CF.take([512]) for _ in range(2)]
              ms2 = stat[:, 80:82]
              rq = stat[:, 84:86]
              rden4 = stat[:, 88:92]
              rdenb = [stat[:, 88:92], stat[:, 92:96]]
              rbi = [0]
              P.dma("sp", ropeC[64:96, :], ropeC_d[:, :])
              P.dma("sp", ropeS[64:96, :], ropeS_d[:, :])
              wnext = load_group(l, 2)
              fm_proj(wcur, 416, 2, ev_gate(2))
              P.memset("pool", Vb[:, :, :, 64:65], 1.0)
              msb = stat[:, 96:96 + 2 * NT].rearrange("p (t k) -> p t k", k=2)
              rqb = stat[:, 136:136 + 2 * NT].rearrange("p (t k) -> p t k", k=2)
              P.memset("dve", stat[:, 96:96 + 2 * NT], 0.0)
              psb = {}
              pTb = {}

              def b_s0(tt):
                  ps = nbF()
                  psb[tt] = ps
                  tm_proj(wcur, 0, 384, tt, ps)
                  P.act(junkb[:, 0:256], ps[:, 0:256], AF.Square, scale=1.0 / 16.0, accum_out=msb[:, tt, 0:1])
                  P.act(junkb2[:, 0:128], ps[:, 256:384], AF.Square, scale=float(128.0 ** -0.5), accum_out=msb[:, tt, 1:2])

              def b_s1a(tt):
                  P.act(rqb[:, tt, :], msb[:, tt, :], AF.Sqrt, bias=epsT[:, 0:1], scale=1.0)

              def b_s1(tt):
                  ps = psb.pop(tt)
                  P.recip(rqb[:, tt, :], rqb[:, tt, :])
                  c_ = cst[tt % 3]
                  P.ts("dve", c_[:, 0:256], ps[:, 0:256], rqb[:, tt, 0:1])
                  P.ts("dve", c_[:, 256:384], ps[:, 256:384], rqb[:, tt, 1:2])

              def b_s2(tt):
                  pT = nbT()
                  pTb[tt] = pT
                  c_ = cst[tt % 3]
                  for j in range(3):
                      P.tr(pT[:, j * 128:(j + 1) * 128], c_[:, j * 128:(j + 1) * 128], ident[:])

              def b_s3(tt):
                  pT = pTb.pop(tt)
                  P.cp(evac_eng(), cT3[:, :, tt * 128:(tt + 1) * 128], pT[:, 0:384].rearrange("p (j t) -> p j t", j=3))

              pipeline(NT, [b_s0, b_s1a, b_s1, b_s2, b_s3])
              for tt in range(NT):
                  ps = nbF()
                  P.mm(ps[:, 0:512], cT3[:, 2, tt * 128:(tt + 1) * 128], wukv[:, :])
                  P.cp(evac_eng(), Vb[:, tt, :, 0:64], ps[:, 0:512].rearrange("p (h d) -> p h d", h=4)[:, :, 64:128])

              def rope_evac(dst, psA, psB, t0, n, ri):
                  if t0 < CTXL:
                      P.cp("dve", dst[64:96, t0:t0 + n], psA[64:96, 0:n])
                      return
                  p0 = t0 - CTXL
                  a, b_ = tmp1[ri % 2], tmp2[ri % 2]
                  P.tt("dve", a[64:96, 0:n], psA[64:96, 0:n], ropeC[64:96, p0:p0 + n], ALU.mult)
                  P.tt("dve", b_[64:96, 0:n], psB[64:96, 0:n], ropeS[64:96, p0:p0 + n], ALU.mult)
                  P.tt("pool", dst[64:96, t0:t0 + n], a[64:96, 0:n], b_[64:96, 0:n], ALU.add)

              for bi, (t0, n) in enumerate(TBLK):
                  psA, psB = nbF(), nbF()
                  for kt in range(KT):
                      P.mm(psA[0:96, 0:n], wcur[:, kt, 320:416], hxT[:, kt, t0:t0 + n], start=(kt == 0), stop=(kt == KT - 1))
                  if t0 >= CTXL:
                      for kt in range(KT):
                          P.mm(psB[0:96, 0:n], krsw[:, kt, :], hxT[:, kt, t0:t0 + n], start=(kt == 0), stop=(kt == KT - 1))
                  rope_evac(krT, psA, psB, t0, n, bi)
              P.cp("dve", KTh[1][64:96, :], KTh[0][64:96, :])

              sc_b = float(96.0 ** -0.5)

              def b_proj_gen(h):
                  Kh = KTh[h % 2]
                  Qh = QTh[h % 2]
                  for bi, (t0, n) in enumerate(TBLK):
                      ps = nbF()
                      P.mm(ps[0:64, 0:n], wukv[:, h * 128:h * 128 + 64], cT3[:, 2, t0:t0 + n])
                      P.cp("dve", Kh[0:64, t0:t0 + n], ps[0:64, 0:n])
                      yield
                  for bi, (t0, n) in enumerate(TBLK):
                      if t0 < CTXL and not upd:
                          continue
                      psA, psB = nbF(), nbF()
                      for k2 in range(2):
                          P.mm(psA[0:96, 0:n], wuq[:, k2, h * 96:(h + 1) * 96], cT3[:, k2, t0:t0 + n], start=(k2 == 0), stop=(k2 == 1))
                      if t0 >= CTXL:
                          for k2 in range(2):
                              P.mm(psB[0:96, 0:n], wuqsw[:, k2, h * 96:(h + 1) * 96], cT3[:, k2, t0:t0 + n], start=(k2 == 0), stop=(k2 == 1))
                      P.cp("dve", Qh[0:64, t0:t0 + n], psA[0:64, 0:n])
                      rope_evac(Qh, psA, psB, t0, n, bi)
                      yield

              def b_proj(h):
                  for _ in b_proj_gen(h):
                      pass

              qblks = [(256 + 512 * i, 512, NT) for i in range(4)] + ([(0, 256, 2)] if upd else [])
              items = []
              for h in range(4):
                  for bq, (q0, nq, nk) in enumerate(qblks):
                      for p_ in range(nk // 2):
                          items.append((h, bq, q0, nq, nk, p_))
              LA = 1
              pos = {}
              gen = [None]
              b_proj(0)
              for idx in range(len(items) + LA):
                  if idx < len(items):
                      h, bq, q0, nq, nk, p_ = items[idx]
                      if bq == 0 and p_ == 0:
                          if gen[0] is not None:
                              for _ in gen[0]:
                                  pass
                          gen[0] = b_proj_gen(h + 1) if h + 1 < 4 else None
                      elif gen[0] is not None and idx % 3 == 1:
                          try:
                              next(gen[0])
                          except StopIteration:
                              gen[0] = None
                      if p_ == 0:
                          pos[(h, bq)] = nbO()
                      S2 = psD[idx % 2]
                      for j in range(2):
                          kt_ = 2 * p_ + j
                          P.mm(S2[:, j * 512:j * 512 + nq], KTh[h % 2][0:96, kt_ * 128:(kt_ + 1) * 128], QTh[h % 2][0:96, q0:q0 + nq])
                      if nq == 512:
                          P.act(PTb2[idx % 2][:, 0:1024], S2[:, 0:1024], AF.Exp, scale=sc_b)
                      else:
                          P.act(PTb2[idx % 2][:, 0:1024].rearrange("p (j q) -> p j q", j=2)[:, :, 0:nq],
                                S2[:, 0:1024].rearrange("p (j q) -> p j q", j=2)[:, :, 0:nq], AF.Exp, scale=sc_b)
                  if idx >= LA:
                      h, bq, q0, nq, nk, p_ = items[idx - LA]
                      nqs = nq // 128
                      po = pos[(h, bq)]
                      Pb = PTb2[(idx - LA) % 2]
                      for j in range(2):
                          k_ = 2 * p_ + j
                          for qs in range(nqs):
                              P.mm(po[:, qs * 65:(qs + 1) * 65], Pb[:, j * 512 + qs * 128:j * 512 + (qs + 1) * 128], Vb[:, k_, h, :],
                                   start=(k_ == 0 and qs == 0), stop=(k_ == nk - 1), skip=True)
                      if 2 * p_ + 1 == nk - 1:
                          po3 = po[:, 0:nqs * 65].rearrange("p (q d) -> p q d", d=65)
                          rd = rdenb[rbi[0] % 2]
                          rbi[0] += 1
                          P.recip(rd[:, 0:nqs], po3[:, :, 64])
                          for qs in range(nqs):
                              qt = q0 // 128 + qs
                              P.ts("dve", ob[:, qt, h * 64:(h + 1) * 64], po[:, qs * 65:qs * 65 + 64], rd[:, qs:qs + 1])
                          del pos[(h, bq)]
              for qt in (list(range(2, NT)) + ([0, 1] if upd else [])):
                  pT = nbT()
                  for ft in range(2):
                      P.tr(pT[:, ft * 128:(ft + 1) * 128], ob[:, qt, ft * 128:(ft + 1) * 128], ident[:])
                  for ft in range(2):
                      o = mixT[:, 2 + ft, qt * 128:(qt + 1) * 128]
                      P.tt("dve", o, pT[:, ft * 128:(ft + 1) * 128], o, ALU.mult)

              if stop == 'B':
                  raise _Stop()
              wcur = wnext
              CB.reset()
              CF.reset()
              ugT = CB.take([2, T])
              vn = CB.take([NT, 256])
              sgt = [CB.take([512]) for _ in range(2)]
              gv = CF.take([NT, 256])
              sq = CF.take([256])
              vt = [CF.take([256]) for _ in range(2)]
              s1 = stat[:, 96:96 + 72].rearrange("p (t g) -> p t g", g=4)
              s2 = stat[:, 168:168 + 72].rearrange("p (t g) -> p t g", g=4)
              wnext = load_group(l, 3)
              cblks = TBLK if upd else TBLK[1:]
              ctiles = list(range(NT)) if upd else list(range(2, NT))

              def ev_u(ft, t0, n, ps):
                  P.act(ugT[:, ft, t0:t0 + n], ps[:, 0:n], AF.Gelu)

              fm_proj(wcur, 0, 2, ev_u, cblks)
              for tt in ctiles:
                  ps = nbF()
                  tm_proj(wcur, 256, 256, tt, ps)
                  P.act(gv[:, tt, :], ps[:, 0:256], AF.Gelu)
                  P.red(s1[:, tt, :], gv[:, tt, :].rearrange("p (g c) -> p g c", g=4))
                  P.tt("pool", sq, gv[:, tt, :], gv[:, tt, :], ALU.mult)
                  P.red(s2[:, tt, :], sq.rearrange("p (g c) -> p g c", g=4))
              def ev_g(ft, t0, n, ps):
                  nonlocal_ri = ev_g.ri
                  ev_g.ri += 1
                  sg_ = sgt[nonlocal_ri % 2]
                  P.act(sg_[:, 0:n], ps[:, 0:n], AF.Silu)
                  P.tt("pool", ugT[:, ft, t0:t0 + n], ugT[:, ft, t0:t0 + n], sg_[:, 0:n], ALU.mult)
              ev_g.ri = 0
              fm_proj(wcur, 512, 2, ev_g, cblks)
              s1f = stat[:, 96:96 + 72]
              s2f = stat[:, 168:168 + 72]
              m2 = CF.take([72])
              P.ts("dve", s1f, s1f, 1.0 / 64.0)
              P.tt("dve", m2, s1f, s1f, ALU.mult)
              P.stt("dve", s2f, s2f, 1.0 / 64.0, m2, ALU.mult, ALU.subtract)
              rstd_from_ms(s2f, s2f, 72)
              P.stt("dve", s1f, s1f, -1.0, s2f, ALU.mult, ALU.mult)
              for tt in ctiles:
                  v_ = vt[tt % 2]
                  for g in range(4):
                      P.act(v_[:, g * 64:(g + 1) * 64], gv[:, tt, g * 64:(g + 1) * 64], AF.Identity,
                            bias=s1[:, tt, g:g + 1], scale=s2[:, tt, g:g + 1])
                  P.tt("pool", vn[:, tt, :], v_, lngbc[:], ALU.mult)
              for tt in ctiles:
                  ps = nbF()
                  for ft in range(2):
                      for hf in range(2):
                          g = 2 * ft + hf
                          o = ps[:, (ft * 2 + hf) * 128:(ft * 2 + hf + 1) * 128]
                          P.mm(o, vn[:, tt, ft * 128:(ft + 1) * 128], wsT[:, g, :], start=True, stop=False)
                          P.mm(o, ones_b[0:2, :], bs2[0:2, g * 128:(g + 1) * 128], start=False, stop=True)
                  for ft in range(2):
                      for hf in range(2):
                          pr = slice(hf * 64, (hf + 1) * 64)
                          P.tt("dve", mixT[pr, 4 + ft, tt * 128:(tt + 1) * 128],
                               ps[pr, (ft * 2 + hf) * 128:(ft * 2 + hf + 1) * 128],
                               ugT[pr, ft, tt * 128:(tt + 1) * 128], ALU.mult)

              if stop == 'C':
                  raise _Stop()
              wcur = wnext
              CB.reset()
              CF.reset()
              bg = CB.take([2, T])
              zl = CF.take([2, SEQ + 2])
              zc = CF.take([2, CTXL + 2])
              dcs = [CF.take([512]) for _ in range(2)]
              sgd = [CF.take([512]) for _ in range(2)]
              yt = [CF.take([512]) for _ in range(2)]
              P.memset("pool", zl[:, :, 0:1], 0.0)
              P.memset("pool", zl[:, :, SEQ + 1:SEQ + 2], 0.0)
              P.memset("pool", zc[:, :, 0:1], 0.0)
              P.memset("pool", zc[:, :, CTXL + 1:CTXL + 2], 0.0)
              if s == 0:
                  wA_pref[0] = load_group(l, 0)
              CBw = Carver(arb, AB)
              CBw.reset(2 * T)
              wo = CBw.take([KT, 1024])
              for k2 in range(0, KT, 2):
                  P.dma("pool", wo[:, k2:k2 + 2, :], wout_d[l].rearrange("(kt p) c -> p kt c", p=128)[:, k2:k2 + 2, :])
              di = 0
              for ft in range(2):
                  for (t0, n) in cblks:
                      ps_c, ps_h, ps_b, ps_g = nbF(), nbF(), nbF(), nbF()
                      for (ps, c0) in ((ps_c, 256), (ps_h, 512), (ps_b, 0), (ps_g, 768)):
                          for kt in range(KT):
                              P.mm(ps[:, 0:n], wcur[:, kt, c0 + ft * 128:c0 + (ft + 1) * 128], hxT[:, kt, t0:t0 + n],
                                   start=(kt == 0), stop=(kt == KT - 1))
                      d_, g_ = dcs[di % 2], sgd[di % 2]
                      di += 1
                      P.cp("act", d_[:, 0:n], ps_c[:, 0:n])
                      P.act(g_[:, 0:n], ps_g[:, 0:n], AF.Silu)
                      zdst = zc[:, ft, 1 + t0:1 + t0 + n] if t0 < CTXL else zl[:, ft, 1 + t0 - CTXL:1 + t0 - CTXL + n]
                      P.tt("dve", zdst, ps_h[:, 0:n], d_[:, 0:n], ALU.mult)
                      P.tt("dve", bg[:, ft, t0:t0 + n], ps_b[:, 0:n], g_[:, 0:n], ALU.mult)
              yi = 0
              for ft in range(2):
                  for (t0, n) in cblks:
                      zb, zo = (zc, t0) if t0 < CTXL else (zl, t0 - CTXL)
                      y = yt[yi % 2]
                      yi += 1
                      P.act(y[:, 0:n], zb[:, ft, 1 + zo:1 + zo + n], AF.Identity, bias=zeroT[:, 0:1], scale=scw[:, ft, 1:2])
                      P.stt("dve", y[:, 0:n], zb[:, ft, zo:zo + n], scw[:, ft, 0:1], y[:, 0:n], ALU.mult, ALU.add)
                      P.stt("dve", y[:, 0:n], zb[:, ft, 2 + zo:2 + zo + n], scw[:, ft, 2:3], y[:, 0:n], ALU.mult, ALU.add)
                      P.tt("pool", mixT[:, 6 + ft, t0:t0 + n], y[:, 0:n], bg[:, ft, t0:t0 + n], ALU.mult)
              if s == 0:
                  tap("mixT%d" % l, mixT[:])

              if stop == 'D':
                  raise _Stop()
              CF.reset()
              gbc = CF.take([1024])
              gbcc = CF.take([1024])
              fg = CF.take([1024])
              xo = [CF.take([1024]) for _ in range(4)]
              tmps2 = [CF.take([512]) for _ in range(2)]
              P.dma("sp", gbc, grow_d[l, s].partition_broadcast(128))
              if upd:
                  P.dma("sp", gbcc, grow_d[l, 2].partition_broadcast(128))
              if last:
                  P.dma("sp", fg, finalg_d.partition_broadcast(128))
              sso = stat[:, 0:NT]
              rso = stat[:, 32:32 + NT]
              P.memset("dve", sso, 0.0)
              junko = CB.take([1024]) if False else arb[:, 0:1024]
              otiles = list(range(2, NT)) + ([0, 1] if upd else [])
              xo4 = xo
              pso = {}

              def o_src(tt):
                  return (xsrc[s, (tt - 2) * 128:(tt - 1) * 128, :] if tt >= 2 else csrc[s, tt * 128:(tt + 1) * 128, :])

              def o_s0(oi):
                  tt = otiles[oi]
                  P.dma("sp", xo4[oi % 4], o_src(tt))
                  banks = []
                  for hf in range(2):
                      ps = nbF()
                      banks.append(ps)
                      for kt in range(KT):
                          P.mm(ps[:, 0:512], mixT[:, kt, tt * 128:(tt + 1) * 128], wo[:, kt, hf * 512:(hf + 1) * 512],
                               start=(kt == 0), stop=(kt == KT - 1))
                  pso[oi] = banks

              def o_s1(oi):
                  tt = otiles[oi]
                  g_ = gbc if tt >= 2 else gbcc
                  xt = xo4[oi % 4]
                  banks = pso.pop(oi)
                  for hf in range(2):
                      tmpo = tmps2[hf]
                      P.tt("dve", tmpo[:, 0:512], banks[hf][:, 0:512], g_[:, hf * 512:(hf + 1) * 512], ALU.mult)
                      P.tt("pool", xt[:, hf * 512:(hf + 1) * 512], tmpo[:, 0:512], xt[:, hf * 512:(hf + 1) * 512], ALU.add)
                  if last and tt >= 2:
                      P.act(junko, xt, AF.Square, scale=1.0 / 32.0, accum_out=sso[:, tt:tt + 1])

              def o_s2(oi):
                  tt = otiles[oi]
                  xt = xo4[oi % 4]
                  if not last:
                      dst = (xs_d[s, (tt - 2) * 128:(tt - 1) * 128, :] if tt >= 2 else cs_d[s, tt * 128:(tt + 1) * 128, :])
                      P.dma("act", dst, xt)
                  elif tt >= 2:
                      rstd_from_ms(rso[:, tt:tt + 1], sso[:, tt:tt + 1], 1)
                      P.stt("dve", xt, xt, rso[:, tt:tt + 1], fg, ALU.mult, ALU.mult)
                      P.dma("act", out_d[s, (tt - 2) * 128:(tt - 1) * 128, :], xt)

              pipeline(len(otiles), [o_s0, o_s1, o_s2])

    except _Stop:
        pass
    P.emit(nc)
    es.close()
    return nc, list(tap_out.keys())


def _prep_shared(inp):
    f = lambda a: np.ascontiguousarray(np.asarray(a, dtype=np.float32))
    C, S = _rope_tables()
    pr = _rope_perm32()
    perm_q = np.concatenate([np.concatenate([np.arange(64), 64 + pr]) + 96 * h for h in range(4)])
    perm_k = np.concatenate([np.arange(64), 64 + pr])
    ri, ci, inw = _na_index_tables()
    rpb = f(inp["na_rpb"])
    nab = rpb[:, :, ri, ci]
    nab = np.where(inw[None, None], nab, np.float32(MASK_FILL)).astype(np.float32)
    nab = np.ascontiguousarray(nab.transpose(0, 3, 1, 2, 4).reshape(2, 128, 4 * NTAB, 128))
    w_in = f(inp["w_in"])
    w_uq = f(inp["mla_w_uq"])
    sh = {
        "ident": np.eye(128, dtype=np.float32),
        "sel2": np.eye(2, dtype=np.float32),
        "ropeC": C, "ropeS": S,
        "norm_gT": f(f(inp["norm_g"]).reshape(2, 8, 128).transpose(0, 2, 1)),
        "final_g": f(inp["final_g"]),
        "w_mod": f(inp["w_mod"]),
        "b_mod": f(inp["b_mod"]),
        "w_in": w_in,
        "w_krsw": f(w_in[:, :, 1344:1440][:, :, perm_k]),
        "w_out": f(inp["w_out"]),
        "w_uq": w_uq,
        "w_uqsw": f(w_uq[:, :, perm_q]),
        "qn_gT": f(f(inp["mla_qn_g"]).reshape(2, 2, 128).transpose(0, 2, 1)),
        "w_ukv": f(inp["mla_w_ukv"]),
        "kvn_gT": f(f(inp["mla_kvn_g"]).reshape(2, 1, 128).transpose(0, 2, 1)),
        "w_sT": f(f(inp["cm_w_s"]).transpose(0, 3, 1, 2)),
        "b_s": f(f(inp["cm_b_s"]).reshape(2, 512)),
        "ln_g": f(inp["cm_ln_g"]),
        "sc_wT": f(f(inp["sc_w"]).reshape(2, 3, 2, 128).transpose(0, 3, 2, 1)),
        "na_bias": nab,
    }
    return sh


_CACHE = {}


def kernel(x, c, ctx, c_ctx, norm_g, w_mod, b_mod, w_in, na_rpb, mla_qn_g, mla_w_uq, mla_kvn_g,
           mla_w_ukv, cm_ln_g, cm_w_s, cm_b_s, sc_w, w_out, final_g):
    inp = dict(x=x, c=c, ctx=ctx, c_ctx=c_ctx, norm_g=norm_g, w_mod=w_mod, b_mod=b_mod, w_in=w_in,
               na_rpb=na_rpb, mla_qn_g=mla_qn_g, mla_w_uq=mla_w_uq, mla_kvn_g=mla_kvn_g,
               mla_w_ukv=mla_w_ukv, cm_ln_g=cm_ln_g, cm_w_s=cm_w_s, cm_b_s=cm_b_s, sc_w=sc_w,
               w_out=w_out, final_g=final_g)
    sh = _prep_shared(inp)
    x = np.asarray(x, dtype=np.float32)
    ctx = np.asarray(ctx, dtype=np.float32)
    c = np.asarray(c, dtype=np.float32)
    c_ctx = np.asarray(c_ctx, dtype=np.float32)
    if "nc" not in _CACHE:
        _CACHE["nc"] = build_program()[0]
    nc = _CACHE["nc"]
    in_maps = []
    for i in range(NCORES):
        cc = np.stack([c[2 * i], c[2 * i + 1], c_ctx], axis=0)
        cT = np.ascontiguousarray(cc.reshape(3, 8, 128).transpose(2, 1, 0))
        m = dict(sh)
        m["x"] = np.ascontiguousarray(x[2 * i:2 * i + 2])
        m["ctx"] = np.ascontiguousarray(ctx[2 * i:2 * i + 2])
        m["cT"] = cT
        in_maps.append(m)
    res = run_bass_kernel_spmd(nc, in_maps, core_ids=list(range(NCORES)))
    return np.concatenate([np.asarray(r["out"], dtype=np.float32) for r in res.results], axis=0)
```

```python
import contextlib
import os
DBG = os.environ.get('DBG', '')
import numpy as np
import concourse.bass as bass
import concourse.mybir as mybir
from concourse.bass_utils import run_bass_kernel_spmd

F32 = mybir.dt.float32
BF16 = mybir.dt.bfloat16
AF = mybir.ActivationFunctionType
ALU = mybir.AluOpType
AX = mybir.AxisListType

NCORES = 8
D = 1024
KT = 8
SEQ = 2048
CTXL = 256
T = SEQ + CTXL
NT = T // 128
DIN = 3488
EPS = 1e-6
MASK_FILL = -100.0
NTAB = 21


class _Op:
    __slots__ = ("eng", "fn", "deps", "raw", "is_dma", "dsem", "dcount", "ms", "need_inc", "waits")


def _foot(ap):
    t = ap.tensor
    kind = type(t).__name__
    pat = ap.ap
    off = int(ap.offset)
    if kind.startswith("DRam"):
        ext = 1
        for st, cnt in pat:
            ext += (cnt - 1) * abs(st)
        return (t.name, 0, 1, off, off + ext)
    row = 1
    for d in list(t.shape)[1:]:
        row *= int(d)
    p0 = off // row
    lo = off % row
    npart = pat[0][1]
    ext = 1
    for st, cnt in pat[1:]:
        ext += (cnt - 1) * abs(st)
    return (t.name, p0, p0 + npart, lo, lo + ext)


class Prog:
    ENG = ("pe", "act", "dve", "pool", "sp")
    NDS = 8

    def __init__(self):
        self.ops = []
        self.acc = {}
        self.dcnt = {q: [0] * self.NDS for q in ("sp", "pool", "act")}
        self.drr = {q: 0 for q in ("sp", "pool", "act")}

    def _access(self, ap, idx, is_write, deps, eng, is_dma, raw):
        name, p0, p1, lo, hi = _foot(ap)
        rw = is_write
        q0, q1, l0, h0 = p0, p1, lo, hi
        if type(ap.tensor).__name__.startswith("PSum"):
            p0, p1, lo, hi, is_write = 0, 128, 0, 1 << 30, True
        lst = self.acc.setdefault(name, [])
        keep = []
        for e in lst:
            ov = not (e[1] <= p0 or p1 <= e[0] or e[3] <= lo or hi <= e[2])
            if ov and (is_write or e[5]):
                deps.add(e[4])
                if (not rw) and e[8] and not (e[10] <= q0 or q1 <= e[9] or e[12] <= l0 or h0 <= e[11]):
                    raw.add(e[4])
            if is_write and ov and e[0] >= p0 and e[1] <= p1 and e[2] >= lo and e[3] <= hi:
                continue
            if (not is_write) and (not e[5]) and (not is_dma) and e[6] == eng and (not e[7]) \
                    and e[0] == p0 and e[1] == p1 and e[2] == lo and e[3] == hi:
                continue
            keep.append(e)
        keep.append((p0, p1, lo, hi, idx, is_write, eng, is_dma, rw, q0, q1, l0, h0))
        self.acc[name] = keep

    def add(self, eng, fn, reads=(), writes=(), is_dma=False):
        op = _Op()
        op.eng = eng
        op.fn = fn
        op.is_dma = is_dma
        op.need_inc = is_dma
        op.ms = 0
        idx = len(self.ops)
        deps = set()
        raw = set()
        for ap in reads:
            if ap is not None and not isinstance(ap, (int, float)):
                self._access(ap, idx, False, deps, eng, is_dma, raw)
        for ap in writes:
            self._access(ap, idx, True, deps, eng, is_dma, raw)
        deps.discard(idx)
        raw.discard(idx)
        op.deps = deps
        op.raw = raw
        if is_dma:
            k = self.drr[eng]
            self.drr[eng] = (k + 1) % self.NDS
            op.dsem = (eng, k)
            self.dcnt[eng][k] += 1
            op.dcount = self.dcnt[eng][k]
        self.ops.append(op)
        return idx

    def mm(self, out, lhsT, rhs, start=True, stop=True, skip=False):
        self.add("pe", lambda e: e.matmul(out, lhsT=lhsT, rhs=rhs, start=start, stop=stop, skip_group_check=skip),
                 reads=[lhsT, rhs], writes=[out])

    def tr(self, out, in_, ident):
        self.add("pe", lambda e: e.transpose(out, in_, ident), reads=[in_, ident], writes=[out])

    def act(self, out, in_, func, bias=None, scale=None, accum_out=None):
        kw = {}
        if bias is not None:
            kw["bias"] = bias
        if scale is not None:
            kw["scale"] = scale
        if accum_out is not None:
            kw["accum_out"] = accum_out
        w = [out] + ([accum_out] if accum_out is not None else [])
        self.add("act", lambda e: e.activation(out, in_, func, **kw), reads=[in_, bias, scale], writes=w)

    def tt(self, eng, out, in0, in1, op):
        self.add(eng, lambda e: e.tensor_tensor(out, in0, in1, op), reads=[in0, in1], writes=[out])

    def ts(self, eng, out, in0, s1, s2=None, op0=ALU.mult, op1=None):
        if op1 is None:
            self.add(eng, lambda e: e.tensor_scalar(out, in0, s1, None, op0), reads=[in0, s1], writes=[out])
        else:
            self.add(eng, lambda e: e.tensor_scalar(out, in0, s1, s2, op0, op1), reads=[in0, s1, s2], writes=[out])

    def stt(self, eng, out, in0, scalar, in1, op0, op1):
        self.add(eng, lambda e: e.scalar_tensor_tensor(out, in0, scalar, in1, op0, op1),
                 reads=[in0, scalar, in1], writes=[out])

    def cp(self, eng, out, in_):
        if eng == "act":
            self.add("act", lambda e: e.activation(out, in_, AF.Copy), reads=[in_], writes=[out])
        else:
            self.add(eng, lambda e: e.tensor_copy(out, in_), reads=[in_], writes=[out])

    def memset(self, eng, out, val):
        self.add(eng, lambda e: e.memset(out, val), writes=[out])

    def recip(self, out, in_):
        self.add("dve", lambda e: e.reciprocal(out, in_), reads=[in_], writes=[out])

    def red(self, out, in_, op=ALU.add):
        self.add("dve", lambda e: e.tensor_reduce(out, in_, AX.X, op), reads=[in_], writes=[out])

    def dma(self, q, out, in_):
        self.add(q, lambda e: e.dma_start(out=out, in_=in_), reads=[in_], writes=[out], is_dma=True)

    def emit(self, nc):
        ops = self.ops
        for op in ops:
            for d in op.deps:
                D = ops[d]
                if not D.is_dma and (D.eng != op.eng or op.eng != "pe"):
                    D.need_inc = True
        cnt = {e: 0 for e in self.ENG}
        for op in ops:
            if not op.is_dma and op.need_inc:
                cnt[op.eng] += 1
                op.ms = cnt[op.eng]
        waited = {e: {} for e in self.ENG}
        for op in ops:
            need = {}
            for d in op.deps:
                D = ops[d]
                if D.is_dma:
                    key, val = ("d",) + D.dsem, 16 * D.dcount
                elif D.eng == op.eng and op.eng == "pe":
                    continue
                else:
                    key, val = ("e", D.eng), D.ms
                if need.get(key, 0) < val:
                    need[key] = val
            if op.is_dma and op.dcount > 1:
                key = ("d",) + op.dsem
                need[key] = max(need.get(key, 0), 16 * (op.dcount - 1))
            w = waited[op.eng]
            op.waits = []
            for key, val in need.items():
                if w.get(key, 0) < val:
                    w[key] = val
                    op.waits.append((key, val))
        per = {e: [op for op in ops if op.eng == e] for e in self.ENG}
        with contextlib.ExitStack() as es:
            sems = {}
            for e in self.ENG:
                sems[("e", e)] = es.enter_context(nc.semaphore("s_" + e))
            for q in self.dcnt:
                for k in range(self.NDS):
                    sems[("d", q, k)] = es.enter_context(nc.semaphore("d_%s%d" % (q, k)))
            block = es.enter_context(nc.Block())

            def runner(name, final_wait=False):
                def f(e):
                    for op in per[name]:
                        for key, val in op.waits:
                            e.wait_ge(sems[key], val)
                        ins = op.fn(e)
                        if op.is_dma:
                            ins.then_inc(sems[("d",) + op.dsem], 16)
                        elif op.need_inc:
                            ins.then_inc(sems[("e", name)], 1)
                    if final_wait:
                        for q in self.dcnt:
                            for k in range(self.NDS):
                                if self.dcnt[q][k] > 0:
                                    e.wait_ge(sems[("d", q, k)], 16 * self.dcnt[q][k])
                return f

            block.sync(runner("sp", True))
            block.scalar(runner("act"))
            block.vector(runner("dve"))
            block.gpsimd(runner("pool"))
            block.tensor(runner("pe"))


def _rope_perm32():
    p = np.zeros(32, dtype=np.int64)
    for a in range(2):
        for h in range(2):
            for f in range(8):
                p[a * 16 + h * 8 + f] = a * 16 + (1 - h) * 8 + f
    return p


def _rope_tables():
    t = np.arange(SEQ)
    row = (t // 64).astype(np.float32)
    col = (t % 64).astype(np.float32)
    nf = 8
    inv = (np.float32(10000.0) ** (-np.arange(nf, dtype=np.float32) / np.float32(nf))).astype(np.float32)
    C = np.zeros((32, SEQ), dtype=np.float32)
    S = np.zeros((32, SEQ), dtype=np.float32)
    for a in range(2):
        pos = row if a == 0 else col
        ang = (pos[None, :] * inv[:, None]).astype(np.float32)
        c = np.cos(ang).astype(np.float32)
        s = np.sin(ang).astype(np.float32)
        C[a * 16:a * 16 + 8] = c
        C[a * 16 + 8:a * 16 + 16] = c
        S[a * 16:a * 16 + 8] = -s
        S[a * 16 + 8:a * 16 + 16] = s
    return C, S


def _na_local_tiles(tq):
    rows = [2 * tq, 2 * tq + 1]
    ks = set()
    for qr in rows:
        rs = min(max(qr - 4, 0), 24)
        for kr in range(rs, rs + 8):
            ks.add(kr // 2)
    return sorted(ks)


def _na_table_base(tq):
    if 2 <= tq <= 13:
        return 0
    return {0: 5, 1: 9, 14: 13, 15: 17}[tq]


def _na_index_tables():
    ri = np.zeros((NTAB, 128, 128), dtype=np.int64)
    ci = np.zeros((NTAB, 128, 128), dtype=np.int64)
    inw = np.zeros((NTAB, 128, 128), dtype=bool)
    done = set()
    for tq in range(16):
        base = _na_table_base(tq)
        if base in done:
            continue
        done.add(base)
        for j, tk in enumerate(_na_local_tiles(tq)):
            ki = np.arange(128)[:, None]
            qi = np.arange(128)[None, :]
            kr = 2 * tk + ki // 64
            kc = ki % 64
            qr = 2 * tq + qi // 64
            qc = qi % 64
            rs = np.clip(qr - 4, 0, 24)
            cs = np.clip(qc - 8, 0, 48)
            win = (kr >= rs) & (kr < rs + 8) & (kc >= cs) & (kc < cs + 16)
            ri[base + j] = np.clip(kr - qr + 7, 0, 14)
            ci[base + j] = np.clip(kc - qc, -15, 15) + 15
            inw[base + j] = win
    return ri, ci, inw


class _Stop(Exception):
    pass


def build_program(n_layers=2, taps=(), stop=None):
    nc = bass.Bass("TRN2", target_bir_lowering=False)
    P = Prog()
    es = contextlib.ExitStack()

    def din(name, shape, dt=F32):
        return nc.dram_tensor(name, list(shape), dt, kind="ExternalInput").ap()

    x_d = din("x", [2, SEQ, D])
    ctx_d = din("ctx", [2, CTXL, D])
    cT_d = din("cT", [128, 8, 3])
    ident_d = din("ident", [128, 128])
    sel2_d = din("sel2", [2, 2])
    ropeC_d = din("ropeC", [32, SEQ])
    ropeS_d = din("ropeS", [32, SEQ])
    normg_d = din("norm_gT", [2, 128, 8])
    finalg_d = din("final_g", [D])
    wmod_d = din("w_mod", [2, D, 3 * D])
    bmod_d = din("b_mod", [2, 3 * D])
    win_d = din("w_in", [2, D, DIN])
    krsw_d = din("w_krsw", [2, D, 96])
    wout_d = din("w_out", [2, D, D])
    wuq_d = din("w_uq", [2, 256, 384])
    wuqsw_d = din("w_uqsw", [2, 256, 384])
    qng_d = din("qn_gT", [2, 128, 2])
    wukv_d = din("w_ukv", [2, 128, 512])
    kvng_d = din("kvn_gT", [2, 128, 1])
    wsT_d = din("w_sT", [2, 128, 4, 128])
    bs_d = din("b_s", [2, 512])
    lng_d = din("ln_g", [2, 256])
    scw_d = din("sc_wT", [2, 128, 2, 3])
    nab_d = din("na_bias", [2, 128, 4 * NTAB, 128])
    out_d = nc.dram_tensor("out", [2, SEQ, D], F32, kind="ExternalOutput").ap()
    xs_d = nc.dram_tensor("xs_scr", [2, SEQ, D], F32).ap()
    cs_d = nc.dram_tensor("cs_scr", [2, CTXL, D], F32).ap()
    grow_d = nc.dram_tensor("grow_scr", [2, 3, D], F32).ap()
    expm_d = nc.dram_tensor("expm_scr", [2, 128, 4 * NTAB, 128], BF16).ap()
    tap_out = {}

    def sb(name, shape, dt):
        return es.enter_context(nc.sbuf_tensor(name, list(shape), dt))

    def pst(name, shape, dt):
        return es.enter_context(nc.psum_tensor(name, list(shape), dt))

    hxT = sb("hxT", [128, KT, T], BF16)
    mixT = sb("mixT", [128, KT, T], BF16)
    wg = [sb("wg0", [128, KT, 1024], BF16), sb("wg1", [128, KT, 1024], BF16)]
    ident = sb("identb", [128, 128], BF16)
    ones_f = sb("ones_f", [128, 128], F32)
    epsT = sb("epsT", [128, 1], F32)
    siluT = sb("siluT", [128, 8, 3], F32)
    modT = sb("modT", [128, 24, 3], F32)
    sce = sb("sce", [128, 8, 3], F32)
    normgT = sb("normgT", [128, 8], F32)
    identf = sb("identf", [3, 4], F32)
    wuq = sb("wuq", [128, 2, 384], BF16)
    wuqsw = sb("wuqsw", [128, 2, 384], BF16)
    wukv = sb("wukv", [128, 512], BF16)
    krsw = sb("krsw", [128, KT, 96], BF16)
    wsT = sb("wsT", [128, 4, 128], BF16)
    bs2 = sb("bs2", [2, 512], BF16)
    bsh2 = sb("bsh2", [2, 512], BF16)
    ones_b = sb("ones_b", [2, 128], BF16)
    sel2 = sb("sel2s", [2, 2], F32)
    zeroT = sb("zeroT", [128, 1], F32)
    lngbc = sb("lngbc", [128, 256], F32)
    scw = sb("scw", [128, 2, 3], F32)
    qng = sb("qng", [128, 2], F32)
    kvng = sb("kvng", [128, 1], F32)
    stat = sb("stat", [128, 256], F32)
    AB = 28 * 1024 + 512
    AFN = 8 * 1024
    arb = sb("arena_b", [128, AB], BF16)
    arf = sb("arena_f", [128, AFN], F32)
    psF = [pst("psF%d" % i, [128, 512], F32) for i in range(6)]
    psT = [pst("psT%d" % i, [128, 1024], BF16) for i in range(2)]
    cnt = {"f": 0, "t": 0}

    def nbF():
        cnt["f"] += 1
        return psF[cnt["f"] % 4]

    def nbO():
        cnt["o"] = cnt.get("o", 0) + 1
        return psF[4 + cnt["o"] % 2]

    def nbT():
        cnt["t"] += 1
        return psT[cnt["t"] % 2]

    class Carver:
        def __init__(self, t, size):
            self.t, self.size, self.off = t, size, 0

        def reset(self, off=0):
            self.off = off

        def take(self, shape):
            n = 1
            for d_ in shape:
                n *= d_
            assert self.off + n <= self.size, (self.off, n, self.size)
            ap = self.t[:, self.off:self.off + n]
            self.off += n
            if len(shape) == 2:
                return ap.rearrange("p (a b) -> p a b", a=shape[0])
            if len(shape) == 3:
                return ap.rearrange("p (a b c) -> p a b c", a=shape[0], b=shape[1])
            return ap

    CB = Carver(arb, AB)
    CF = Carver(arf, AFN)

    def tap(name, ap):
        if name not in taps:
            return
        shp = list(ap.shape)
        dt = ap.dtype
        dtn = nc.dram_tensor("tap_" + name, shp, dt, kind="ExternalOutput").ap()
        tap_out[name] = dtn
        P.dma("sp", dtn, ap)

    TBLK = [(0, 256)] + [(256 + 512 * i, 512) for i in range(4)]
    evac_rr = {"i": 0}

    def evac_eng():
        evac_rr["i"] += 1
        return "act" if evac_rr["i"] % 2 else "dve"

    def fm_proj(w, c0, nft, evac, blks=TBLK):
        for ft in range(nft):
            for (t0, n) in blks:
                ps = nbF()
                for kt in range(KT):
                    P.mm(ps[:, 0:n], w[:, kt, c0 + ft * 128:c0 + (ft + 1) * 128], hxT[:, kt, t0:t0 + n],
                         start=(kt == 0), stop=(kt == KT - 1))
                evac(ft, t0, n, ps)

    def tm_proj(w, c0, ncols, tt, ps):
        for kt in range(KT):
            P.mm(ps[:, 0:ncols], hxT[:, kt, tt * 128:(tt + 1) * 128], w[:, kt, c0:c0 + ncols],
                 start=(kt == 0), stop=(kt == KT - 1))

    P.dma("pool", ident[:], ident_d[:, :])
    P.dma("sp", identf[0:3, 0:3], ident_d[0:3, 0:3])
    P.dma("sp", sel2[:], sel2_d[:, :])
    P.memset("dve", ones_f[:], 1.0)
    P.memset("dve", epsT[:], EPS)
    P.memset("dve", zeroT[:], 0.0)
    P.memset("dve", ones_b[:], 1.0)
    CF.reset()
    cTs = CF.take([8, 3])
    P.dma("sp", cTs, cT_d[:, :, :])
    P.act(siluT[:], cTs, AF.Silu)

    GRP = [(0, 1024), (1024, 672), (1696, 768), (2464, 1024)]
    wg_state = {"i": 0}

    def load_group(l, g):
        buf = wg[wg_state["i"] % 2]
        wg_state["i"] += 1
        c0, n = GRP[g]
        src = win_d[l].rearrange("(kt p) c -> p kt c", p=128)
        for k2 in range(0, KT, 2):
            P.dma("pool", buf[:, k2:k2 + 2, 0:n], src[:, k2:k2 + 2, c0:c0 + n])
        return buf

    def pipeline(n, stages):
        ns = len(stages)
        for step in range(n + ns - 1):
            for k, st in enumerate(stages):
                i = step - k
                if 0 <= i < n:
                    st(i)

    def rstd_from_ms(dst, src, n):
        P.act(dst, src, AF.Sqrt, bias=epsT[:, 0:1], scale=1.0)
        P.recip(dst, dst)

    try:
      for l in range(n_layers):
          upd = (l == 0)
          last = (l == n_layers - 1)
          CF.reset()
          CB.reset()
          wA_pref = [load_group(l, 0)]
          P.dma("sp", normgT[:], normg_d[l])
          CF.reset()
          grow = CF.take([1024])
          wm = [CF.take([8, 256]) for _ in range(2)]
          rowb = [CF.take([256]) for _ in range(2)]
          bch = [CF.take([256]) for _ in range(2)]
          mst = [CF.take([7, 128]) for _ in range(2)]
          mbf = [CB.take([7, 128]) for _ in range(2)]
          psm = nbF()
          wsrc = wmod_d[l].rearrange("(kt p) c -> p kt c", p=128)
          bsrc = bmod_d[l].rearrange("(o n) -> o n", o=1)
          for j in range(12):
              wmj = wm[j % 2]
              P.dma("sp", wmj, wsrc[:, :, j * 256:(j + 1) * 256])
              P.dma("sp", bch[j % 2][0:1, :], bsrc[:, j * 256:(j + 1) * 256])
              P.dma("sp", mst[j % 2], nab_d[l][:, j * 7:(j + 1) * 7, :])
              P.act(mbf[j % 2], mst[j % 2], AF.Exp)
              P.dma("act", expm_d[l][:, j * 7:(j + 1) * 7, :], mbf[j % 2])
              psr = nbO()
              for kt in range(KT):
                  P.mm(psr[0:3, 0:256], siluT[:, kt, :], wmj[:, kt, :], start=(kt == 0), stop=False)
              P.mm(psr[0:3, 0:256], ones_f[0:1, 0:3], bch[j % 2][0:1, :], start=False, stop=True)
              rb = rowb[j % 2]
              P.cp("dve", rb[0:3, :], psr[0:3, 0:256])
              if j >= 8:
                  P.cp("dve", grow[0:3, (j - 8) * 256:(j - 7) * 256], psr[0:3, 0:256])
              for m in range(2):
                  mt = j * 2 + m
                  P.tr(psm[:, mt * 3:mt * 3 + 3], rb[0:3, m * 128:(m + 1) * 128], identf[0:3, 0:3])
          P.dma("sp", grow_d[l], grow[0:3, :])
          psm3 = psm[:, 0:72].rearrange("p (m b) -> p m b", b=3)
          P.cp("dve", modT[:, :, :], psm3)
          for b in range(3):
              P.stt("dve", sce[:, :, b], modT[:, 8:16, b], 1.0, normgT[:], ALU.add, ALU.mult)
          tap("modT%d" % l, modT[:])
          CF.reset()
          P.dma("act", qng[:], qng_d[l])
          P.dma("act", kvng[:], kvng_d[l])
          P.dma("act", lngbc[:], lng_d[l].partition_broadcast(128))
          P.dma("act", scw[:], scw_d[l])
          P.dma("pool", wsT[:], wsT_d[l])
          P.dma("pool", krsw[:], krsw_d[l].rearrange("(kt p) c -> p kt c", p=128))
          wst = CF.take([2, 384])
          for (dst, src) in ((wuq, wuq_d), (wuqsw, wuqsw_d)):
              P.dma("act", wst, src[l].rearrange("(kt p) c -> p kt c", p=128))
              for k2 in range(2):
                  P.ts("dve", dst[:, k2, :], wst[:, k2, :], qng[:, k2:k2 + 1])
          wst2 = CF.take([512])
          P.dma("act", wst2, wukv_d[l])
          P.ts("dve", wukv[:], wst2, kvng[:, 0:1])

          bs2f = CF.take([512])
          bs2t = CF.take([512])
          P.dma("act", bs2f[0:2, :], bs_d[l].partition_broadcast(2))
          P.cp("dve", bsh2[0:2, :], bs2f[0:2, :])
          P.tt("dve", bs2f[0:2, :], bs2f[0:2, :], bsh2[0:2, :], ALU.subtract)
          P.ts("dve", bs2t[0:2, :], bsh2[0:2, :], sel2[0:2, 0:1])
          P.stt("dve", bs2[0:2, :], bs2f[0:2, :], sel2[0:2, 1:2], bs2t[0:2, :], ALU.mult, ALU.add)

          if stop == 'M':
              raise _Stop()
          for s in range(2):
              CF.reset()
              CB.reset()
              aqT = CB.take([2, T])
              akT = CB.take([2, T])
              Va = CB.take([NT, 4, 65])
              maskb = [CB.take([NTAB, 128]) for _ in range(2)]
              PT = [CB.take([7, 128]) for _ in range(3)]
              oa_off = CB.off
              oa = CB.take([NT, 256])
              NXS = 7
              xst = [CF.take([1024]) for _ in range(NXS)]
              xn = [arb[:, oa_off + i * 1024:oa_off + (i + 1) * 1024] for i in range(3)]
              junk = arb[:, oa_off + 3072:oa_off + 4096]
              ssq = stat[:, 0:NT]
              rsd = stat[:, 32:32 + NT]
              P.memset("dve", ssq, 0.0)
              xsrc = x_d if l == 0 else xs_d
              csrc = ctx_d if l == 0 else cs_d
              wcur = wA_pref[0]
              pTn = {}

              def n_sd(tt):
                  src = csrc[s, tt * 128:(tt + 1) * 128, :] if tt < 2 else xsrc[s, (tt - 2) * 128:(tt - 1) * 128, :]
                  P.dma("sp", xst[tt % NXS], src)

              def n_s0(tt):
                  P.act(junk, xst[tt % NXS], AF.Square, scale=1.0 / 32.0, accum_out=ssq[:, tt:tt + 1])

              def n_s1a(tt):
                  P.act(rsd[:, tt:tt + 1], ssq[:, tt:tt + 1], AF.Sqrt, bias=epsT[:, 0:1], scale=1.0)

              def n_s1b(tt):
                  P.recip(rsd[:, tt:tt + 1], rsd[:, tt:tt + 1])

              def n_s1(tt):
                  P.tt("pool", xn[tt % 3][:, 0:640], xst[tt % NXS][:, 0:640], rsd[:, tt:tt + 1].to_broadcast([128, 640]), ALU.mult)
                  P.ts("dve", xn[tt % 3][:, 640:1024], xst[tt % NXS][:, 640:1024], rsd[:, tt:tt + 1])

              def n_s2(tt):
                  pT = nbT()
                  pTn[tt] = pT
                  xb = xn[tt % 3]
                  for kt in range(KT):
                      P.tr(pT[:, kt * 128:(kt + 1) * 128], xb[:, kt * 128:(kt + 1) * 128], ident[:])

              def n_s3(tt):
                  b = 2 if tt < 2 else s
                  pT = pTn.pop(tt)
                  for kt in range(KT):
                      o = hxT[:, kt, tt * 128:(tt + 1) * 128]
                      i_ = pT[:, kt * 128:(kt + 1) * 128]
                      if tt % 3 == 0:
                          P.act(o, i_, AF.Identity, bias=modT[:, kt, b:b + 1], scale=sce[:, kt, b:b + 1])
                      else:
                          P.ts("dve", o, i_, sce[:, kt, b:b + 1], modT[:, kt, b:b + 1], ALU.mult, ALU.add)

              blk_done = {(t0 + n) // 128 - 1: (t0, n) for (t0, n) in TBLK}
              P.memset("pool", Va[:, :, :, 64:65], 1.0)

              aitems = []

              def a_item_fm(ft, c0, kind, t0, n):
                  def f():
                      ps = nbF()
                      for kt in range(KT):
                          P.mm(ps[:, 0:n], wcur[:, kt, c0 + ft * 128:c0 + (ft + 1) * 128], hxT[:, kt, t0:t0 + n],
                               start=(kt == 0), stop=(kt == KT - 1))
                      if kind == "g":
                          P.act(mixT[:, ft, t0:t0 + n], ps[:, 0:n], AF.Silu)
                      elif kind == "q":
                          P.cp("dve", aqT[:, ft, t0:t0 + n], ps[:, 0:n])
                      else:
                          P.cp("act", akT[:, ft, t0:t0 + n], ps[:, 0:n])
                  return f

              def a_item_v(t2):
                  def f():
                      ps = nbF()
                      tm_proj(wcur, 512, 256, t2, ps)
                      P.cp("dve", Va[:, t2, :, 0:64], ps[:, 0:256].rearrange("p (h d) -> p h d", h=4))
                  return f

              def n_s4(tt):
                  if tt in blk_done:
                      t0, n = blk_done[tt]
                      for ft in range(2):
                          for (c0, kind) in ((768, "g"), (0, "q"), (256, "k")):
                              aitems.append(a_item_fm(ft, c0, kind, t0, n))
                      for t2 in range(t0 // 128, (t0 + n) // 128):
                          aitems.append(a_item_v(t2))
                  for _ in range(4):
                      if aitems:
                          aitems.pop(0)()

              pipeline(NT, [n_sd, n_s0, n_s1a, n_s1b, n_s1, n_s2, n_s3, n_s4])
              while aitems:
                  aitems.pop(0)()
              if s == 0:
                  tap("hxT%d" % l, hxT[:])

              if stop == 'N':
                  raise _Stop()
              rden = stat[:, 64:72]
              wnext = load_group(l, 1)

              def ev_q(ft, t0, n, ps):
                  P.cp(evac_eng(), aqT[:, ft, t0:t0 + n], ps[:, 0:n])

              def ev_k(ft, t0, n, ps):
                  P.cp(evac_eng(), akT[:, ft, t0:t0 + n], ps[:, 0:n])

              def ev_gate(slot):
                  def f(ft, t0, n, ps):
                      P.act(mixT[:, slot + ft, t0:t0 + n], ps[:, 0:n], AF.Silu)
                  return f


              def na_stage1(h, qt, PTt, wi=0):
                  ft, pb = h // 2, (h % 2) * 64
                  if qt >= 2:
                      loc = _na_local_tiles(qt - 2)
                      slots = [0, 1] + [t_ + 2 for t_ in loc]
                      mask = (2, _na_table_base(qt - 2), len(loc))
                  else:
                      slots = [0, 1]
                      mask = None
                  ns = len(slots)
                  banks = [nbF(), nbF()] if ns > 4 else [nbF()]
                  for j, ktile in enumerate(slots):
                      bk = banks[j // 4]
                      P.mm(bk[:, (j % 4) * 128:(j % 4 + 1) * 128],
                           akT[pb:pb + 64, ft, ktile * 128:(ktile + 1) * 128],
                           aqT[pb:pb + 64, ft, qt * 128:(qt + 1) * 128])
                  for bi, bk in enumerate(banks):
                      n_here = min(4, ns - bi * 4)
                      P.act(PTt[:, bi * 4:bi * 4 + n_here, :],
                            bk[:, 0:n_here * 128].rearrange("p (j q) -> p j q", j=n_here),
                            AF.Exp, scale=0.125)
                  if mask is not None:
                      mj, mt0, mn = mask
                      P.tt("pool", PTt[:, mj:mj + mn, :], PTt[:, mj:mj + mn, :], maskb[h % 2][:, mt0:mt0 + mn, :], ALU.mult)
                  return slots

              def na_stage2(h, qt, PTt, slots):
                  ns = len(slots)
                  po = nbO()
                  for j, ktile in enumerate(slots):
                      P.mm(po[:, 0:65], PTt[:, j, :], Va[:, ktile, h, :], start=(j == 0), stop=(j == ns - 1))
                  P.recip(rden[:, 0:1], po[:, 64:65])
                  P.ts("dve", oa[:, qt, h * 64:(h + 1) * 64], po[:, 0:64], rden[:, 0:1])

              work = []
              for h in range(4):
                  for qt in (list(range(2, NT)) + ([0, 1] if upd else [])):
                      work.append((h, qt))
              pend = []
              lasth = -1
              for wi, (h, qt) in enumerate(work):
                  if h != lasth:
                      P.dma("sp", maskb[h % 2], expm_d[l][:, h * NTAB:(h + 1) * NTAB, :])
                      lasth = h
                  PTt = PT[wi % 3]
                  slots = na_stage1(h, qt, PTt, wi)
                  pend.append((h, qt, PTt, slots))
                  if len(pend) > 2:
                      na_stage2(*pend.pop(0))
              while pend:
                  na_stage2(*pend.pop(0))
              for qt in (list(range(2, NT)) + ([0, 1] if upd else [])):
                  pT = nbT()
                  for ft in range(2):
                      P.tr(pT[:, ft * 128:(ft + 1) * 128], oa[:, qt, ft * 128:(ft + 1) * 128], ident[:])
                  for ft in range(2):
                      o = mixT[:, ft, qt * 128:(qt + 1) * 128]
                      P.tt("dve", o, pT[:, ft * 128:(ft + 1) * 128], o, ALU.mult)

              if stop == 'A':
                  raise _Stop()
              wcur = wnext
              CB.reset()
              CF.reset()
              cT3 = CB.take([3, T])
              KTh = [CB.take([T]) for _ in range(2)]
              QTh = [CB.take([T]) for _ in range(2)]
              krT = KTh[0]
              Vb = CB.take([NT, 4, 65])
              PTb = [CB.take([512]) for _ in range(4)]
              ob = CB.take([NT, 256])
              cst = [CB.take([384]) for _ in range(3)]
              junkb = CB.take([256])
              junkb2 = CB.take([128])
              ropeC = CF.take([SEQ])
              ropeS = CF.take([SEQ])
              tmp1 = [CF.take([512]) for _ in range(2)]
              tmp2 = [CF.take([512]) for _ in range(2)]
              ms2 = stat[:, 80:82]
              rq = stat[:, 84:86]
              rden4 = stat[:, 88:92]
              rdenb = [stat[:, 88:92], stat[:, 92:96]]
              rbi = [0]
              P.dma("sp", ropeC[64:96, :], ropeC_d[:, :])
              P.dma("sp", ropeS[64:96, :], ropeS_d[:, :])
              wnext = load_group(l, 2)
              fm_proj(wcur, 416, 2, ev_gate(2))
              P.memset("pool", Vb[:, :, :, 64:65], 1.0)
              msb = stat[:, 96:96 + 2 * NT].rearrange("p (t k) -> p t k", k=2)
              rqb = stat[:, 136:136 + 2 * NT].rearrange("p (t k) -> p t k", k=2)
              P.memset("dve", stat[:, 96:96 + 2 * NT], 0.0)
              psb = {}
              pTb = {}

              def b_s0(tt):
                  ps = nbF()
                  psb[tt] = ps
                  tm_proj(wcur, 0, 384, tt, ps)
                  P.act(junkb[:, 0:256], ps[:, 0:256], AF.Square, scale=1.0 / 16.0, accum_out=msb[:, tt, 0:1])
                  P.act(junkb2[:, 0:128], ps[:, 256:384], AF.Square, scale=float(128.0 ** -0.5), accum_out=msb[:, tt, 1:2])

              def b_s1a(tt):
                  P.act(rqb[:, tt, :], msb[:, tt, :], AF.Sqrt, bias=epsT[:, 0:1], scale=1.0)

              def b_s1(tt):
                  ps = psb.pop(tt)
                  P.recip(rqb[:, tt, :], rqb[:, tt, :])
                  c_ = cst[tt % 3]
                  P.ts("dve", c_[:, 0:256], ps[:, 0:256], rqb[:, tt, 0:1])
                  P.ts("dve", c_[:, 256:384], ps[:, 256:384], rqb[:, tt, 1:2])

              def b_s2(tt):
                  pT = nbT()
                  pTb[tt] = pT
                  c_ = cst[tt % 3]
                  for j in range(3):
                      P.tr(pT[:, j * 128:(j + 1) * 128], c_[:, j * 128:(j + 1) * 128], ident[:])

              def b_s3(tt):
                  pT = pTb.pop(tt)
                  P.cp(evac_eng(), cT3[:, :, tt * 128:(tt + 1) * 128], pT[:, 0:384].rearrange("p (j t) -> p j t", j=3))

              pipeline(NT, [b_s0, b_s1a, b_s1, b_s2, b_s3])
              for tt in range(NT):
                  ps = nbF()
                  P.mm(ps[:, 0:512], cT3[:, 2, tt * 128:(tt + 1) * 128], wukv[:, :])
                  P.cp(evac_eng(), Vb[:, tt, :, 0:64], ps[:, 0:512].rearrange("p (h d) -> p h d", h=4)[:, :, 64:128])

              def rope_evac(dst, psA, psB, t0, n, ri):
                  if t0 < CTXL:
                      P.cp("dve", dst[64:96, t0:t0 + n], psA[64:96, 0:n])
                      return
                  p0 = t0 - CTXL
                  a, b_ = tmp1[ri % 2], tmp2[ri % 2]
                  P.tt("dve", a[64:96, 0:n], psA[64:96, 0:n], ropeC[64:96, p0:p0 + n], ALU.mult)
                  P.tt("dve", b_[64:96, 0:n], psB[64:96, 0:n], ropeS[64:96, p0:p0 + n], ALU.mult)
                  P.tt("pool", dst[64:96, t0:t0 + n], a[64:96, 0:n], b_[64:96, 0:n], ALU.add)

              for bi, (t0, n) in enumerate(TBLK):
                  psA, psB = nbF(), nbF()
                  for kt in range(KT):
                      P.mm(psA[0:96, 0:n], wcur[:, kt, 320:416], hxT[:, kt, t0:t0 + n], start=(kt == 0), stop=(kt == KT - 1))
                  if t0 >= CTXL:
                      for kt in range(KT):
                          P.mm(psB[0:96, 0:n], krsw[:, kt, :], hxT[:, kt, t0:t0 + n], start=(kt == 0), stop=(kt == KT - 1))
                  rope_evac(krT, psA, psB, t0, n, bi)
              P.cp("dve", KTh[1][64:96, :], KTh[0][64:96, :])

              sc_b = float(96.0 ** -0.5)

              def b_proj_gen(h):
                  Kh = KTh[h % 2]
                  Qh = QTh[h % 2]
                  for bi, (t0, n) in enumerate(TBLK):
                      ps = nbF()
                      P.mm(ps[0:64, 0:n], wukv[:, h * 128:h * 128 + 64], cT3[:, 2, t0:t0 + n])
                      P.cp("dve", Kh[0:64, t0:t0 + n], ps[0:64, 0:n])
                      yield
                  for bi, (t0, n) in enumerate(TBLK):
                      if t0 < CTXL and not upd:
                          continue
                      psA, psB = nbF(), nbF()
                      for k2 in range(2):
                          P.mm(psA[0:96, 0:n], wuq[:, k2, h * 96:(h + 1) * 96], cT3[:, k2, t0:t0 + n], start=(k2 == 0), stop=(k2 == 1))
                      if t0 >= CTXL:
                          for k2 in range(2):
                              P.mm(psB[0:96, 0:n], wuqsw[:, k2, h * 96:(h + 1) * 96], cT3[:, k2, t0:t0 + n], start=(k2 == 0), stop=(k2 == 1))
                      P.cp("dve", Qh[0:64, t0:t0 + n], psA[0:64, 0:n])
                      rope_evac(Qh, psA, psB, t0, n, bi)
                      yield

              def b_proj(h):
                  for _ in b_proj_gen(h):
                      pass

              qblks = [(256 + 512 * i, 512, NT) for i in range(4)] + ([(0, 256, 2)] if upd else [])
              items = []
              for h in range(4):
                  for bq, (q0, nq, nk) in enumerate(qblks):
                      for i in range(nk):
                          items.append((h, bq, q0, nq, nk, i))
              LA = 3
              pos = {}
              gen = [None]
              b_proj(0)
              for idx in range(len(items) + LA):
                  if idx < len(items):
                      h, bq, q0, nq, nk, i = items[idx]
                      if bq == 0 and i == 0:
                          if gen[0] is not None:
                              for _ in gen[0]:
                                  pass
                          gen[0] = b_proj_gen(h + 1) if h + 1 < 4 else None
                      elif gen[0] is not None and idx % 6 == 3:
                          try:
                              next(gen[0])
                          except StopIteration:
                              gen[0] = None
                      if i == 0:
                          pos[(h, bq)] = nbO()
                      ps = nbF()
                      P.mm(ps[:, 0:nq], KTh[h % 2][0:96, i * 128:(i + 1) * 128], QTh[h % 2][0:96, q0:q0 + nq])
                      P.act(PTb[idx % 4][:, 0:nq], ps[:, 0:nq], AF.Exp, scale=sc_b)
                  if idx >= LA:
                      h, bq, q0, nq, nk, k_ = items[idx - LA]
                      nqs = nq // 128
                      po = pos[(h, bq)]
                      for qs in range(nqs):
                          P.mm(po[:, qs * 65:(qs + 1) * 65], PTb[(idx - LA) % 4][:, qs * 128:(qs + 1) * 128], Vb[:, k_, h, :],
                               start=(k_ == 0 and qs == 0), stop=(k_ == nk - 1), skip=True)
                      if k_ == nk - 1:
                          po3 = po[:, 0:nqs * 65].rearrange("p (q d) -> p q d", d=65)
                          rd = rdenb[rbi[0] % 2]
                          rbi[0] += 1
                          P.recip(rd[:, 0:nqs], po3[:, :, 64])
                          for qs in range(nqs):
                              qt = q0 // 128 + qs
                              P.ts("dve", ob[:, qt, h * 64:(h + 1) * 64], po[:, qs * 65:qs * 65 + 64], rd[:, qs:qs + 1])
                          del pos[(h, bq)]
              for qt in (list(range(2, NT)) + ([0, 1] if upd else [])):
                  pT = nbT()
                  for ft in range(2):
                      P.tr(pT[:, ft * 128:(ft + 1) * 128], ob[:, qt, ft * 128:(ft + 1) * 128], ident[:])
                  for ft in range(2):
                      o = mixT[:, 2 + ft, qt * 128:(qt + 1) * 128]
                      P.tt("dve", o, pT[:, ft * 128:(ft + 1) * 128], o, ALU.mult)

              if stop == 'B':
                  raise _Stop()
              wcur = wnext
              CB.reset()
              CF.reset()
              ugT = CB.take([2, T])
              vn = CB.take([NT, 256])
              sgt = [CB.take([512]) for _ in range(2)]
              gv = CF.take([NT, 256])
              sq = CF.take([256])
              vt = [CF.take([256]) for _ in range(2)]
              s1 = stat[:, 96:96 + 72].rearrange("p (t g) -> p t g", g=4)
              s2 = stat[:, 168:168 + 72].rearrange("p (t g) -> p t g", g=4)
              wnext = load_group(l, 3)
              cblks = TBLK if upd else TBLK[1:]
              ctiles = list(range(NT)) if upd else list(range(2, NT))

              def ev_u(ft, t0, n, ps):
                  P.act(ugT[:, ft, t0:t0 + n], ps[:, 0:n], AF.Gelu)

              fm_proj(wcur, 0, 2, ev_u, cblks)
              for tt in ctiles:
                  ps = nbF()
                  tm_proj(wcur, 256, 256, tt, ps)
                  P.act(gv[:, tt, :], ps[:, 0:256], AF.Gelu)
                  P.red(s1[:, tt, :], gv[:, tt, :].rearrange("p (g c) -> p g c", g=4))
                  P.tt("pool", sq, gv[:, tt, :], gv[:, tt, :], ALU.mult)
                  P.red(s2[:, tt, :], sq.rearrange("p (g c) -> p g c", g=4))
              def ev_g(ft, t0, n, ps):
                  nonlocal_ri = ev_g.ri
                  ev_g.ri += 1
                  sg_ = sgt[nonlocal_ri % 2]
                  P.act(sg_[:, 0:n], ps[:, 0:n], AF.Silu)
                  P.tt("pool", ugT[:, ft, t0:t0 + n], ugT[:, ft, t0:t0 + n], sg_[:, 0:n], ALU.mult)
              ev_g.ri = 0
              fm_proj(wcur, 512, 2, ev_g, cblks)
              s1f = stat[:, 96:96 + 72]
              s2f = stat[:, 168:168 + 72]
              m2 = CF.take([72])
              P.ts("dve", s1f, s1f, 1.0 / 64.0)
              P.tt("dve", m2, s1f, s1f, ALU.mult)
              P.stt("dve", s2f, s2f, 1.0 / 64.0, m2, ALU.mult, ALU.subtract)
              rstd_from_ms(s2f, s2f, 72)
              P.stt("dve", s1f, s1f, -1.0, s2f, ALU.mult, ALU.mult)
              for tt in ctiles:
                  v_ = vt[tt % 2]
                  for g in range(4):
                      P.act(v_[:, g * 64:(g + 1) * 64], gv[:, tt, g * 64:(g + 1) * 64], AF.Identity,
                            bias=s1[:, tt, g:g + 1], scale=s2[:, tt, g:g + 1])
                  P.tt("pool", vn[:, tt, :], v_, lngbc[:], ALU.mult)
              for tt in ctiles:
                  ps = nbF()
                  for ft in range(2):
                      for hf in range(2):
                          g = 2 * ft + hf
                          o = ps[:, (ft * 2 + hf) * 128:(ft * 2 + hf + 1) * 128]
                          P.mm(o, vn[:, tt, ft * 128:(ft + 1) * 128], wsT[:, g, :], start=True, stop=False)
                          P.mm(o, ones_b[0:2, :], bs2[0:2, g * 128:(g + 1) * 128], start=False, stop=True)
                  for ft in range(2):
                      for hf in range(2):
                          pr = slice(hf * 64, (hf + 1) * 64)
                          P.tt("dve", mixT[pr, 4 + ft, tt * 128:(tt + 1) * 128],
                               ps[pr, (ft * 2 + hf) * 128:(ft * 2 + hf + 1) * 128],
                               ugT[pr, ft, tt * 128:(tt + 1) * 128], ALU.mult)

              if stop == 'C':
                  raise _Stop()
              wcur = wnext
              CB.reset()
              CF.reset()
              bg = CB.take([2, T])
              zl = CF.take([2, SEQ + 2])
              zc = CF.take([2, CTXL + 2])
              dcs = [CF.take([512]) for _ in range(2)]
              sgd = [CF.take([512]) for _ in range(2)]
              yt = [CF.take([512]) for _ in range(2)]
              P.memset("pool", zl[:, :, 0:1], 0.0)
              P.memset("pool", zl[:, :, SEQ + 1:SEQ + 2], 0.0)
              P.memset("pool", zc[:, :, 0:1], 0.0)
              P.memset("pool", zc[:, :, CTXL + 1:CTXL + 2], 0.0)
              if s == 0:
                  wA_pref[0] = load_group(l, 0)
              CBw = Carver(arb, AB)
              CBw.reset(2 * T)
              wo = CBw.take([KT, 1024])
              for k2 in range(0, KT, 2):
                  P.dma("pool", wo[:, k2:k2 + 2, :], wout_d[l].rearrange("(kt p) c -> p kt c", p=128)[:, k2:k2 + 2, :])
              di = 0
              for ft in range(2):
                  for (t0, n) in cblks:
                      ps_c, ps_h, ps_b, ps_g = nbF(), nbF(), nbF(), nbF()
                      for (ps, c0) in ((ps_c, 256), (ps_h, 512), (ps_b, 0), (ps_g, 768)):
                          for kt in range(KT):
                              P.mm(ps[:, 0:n], wcur[:, kt, c0 + ft * 128:c0 + (ft + 1) * 128], hxT[:, kt, t0:t0 + n],
                                   start=(kt == 0), stop=(kt == KT - 1))
                      d_, g_ = dcs[di % 2], sgd[di % 2]
                      di += 1
                      P.cp("act", d_[:, 0:n], ps_c[:, 0:n])
                      P.act(g_[:, 0:n], ps_g[:, 0:n], AF.Silu)
                      zdst = zc[:, ft, 1 + t0:1 + t0 + n] if t0 < CTXL else zl[:, ft, 1 + t0 - CTXL:1 + t0 - CTXL + n]
                      P.tt("dve", zdst, ps_h[:, 0:n], d_[:, 0:n], ALU.mult)
                      P.tt("dve", bg[:, ft, t0:t0 + n], ps_b[:, 0:n], g_[:, 0:n], ALU.mult)
              yi = 0
              for ft in range(2):
                  for (t0, n) in cblks:
                      zb, zo = (zc, t0) if t0 < CTXL else (zl, t0 - CTXL)
                      y = yt[yi % 2]
                      yi += 1
                      P.act(y[:, 0:n], zb[:, ft, 1 + zo:1 + zo + n], AF.Identity, bias=zeroT[:, 0:1], scale=scw[:, ft, 1:2])
                      P.stt("dve", y[:, 0:n], zb[:, ft, zo:zo + n], scw[:, ft, 0:1], y[:, 0:n], ALU.mult, ALU.add)
                      P.stt("dve", y[:, 0:n], zb[:, ft, 2 + zo:2 + zo + n], scw[:, ft, 2:3], y[:, 0:n], ALU.mult, ALU.add)
                      P.tt("pool", mixT[:, 6 + ft, t0:t0 + n], y[:, 0:n], bg[:, ft, t0:t0 + n], ALU.mult)
              if s == 0:
                  tap("mixT%d" % l, mixT[:])

              if stop == 'D':
                  raise _Stop()
              CF.reset()
              gbc = CF.take([1024])
              gbcc = CF.take([1024])
              fg = CF.take([1024])
              xo = [CF.take([1024]) for _ in range(4)]
              tmps2 = [CF.take([512]) for _ in range(2)]
              P.dma("sp", gbc, grow_d[l, s].partition_broadcast(128))
              if upd:
                  P.dma("sp", gbcc, grow_d[l, 2].partition_broadcast(128))
              if last:
                  P.dma("sp", fg, finalg_d.partition_broadcast(128))
              sso = stat[:, 0:NT]
              rso = stat[:, 32:32 + NT]
              P.memset("dve", sso, 0.0)
              junko = CB.take([1024]) if False else arb[:, 0:1024]
              otiles = list(range(2, NT)) + ([0, 1] if upd else [])
              xo4 = xo
              pso = {}

              def o_src(tt):
                  return (xsrc[s, (tt - 2) * 128:(tt - 1) * 128, :] if tt >= 2 else csrc[s, tt * 128:(tt + 1) * 128, :])

              def o_s0(oi):
                  tt = otiles[oi]
                  P.dma("sp", xo4[oi % 4], o_src(tt))
                  banks = []
                  for hf in range(2):
                      ps = nbF()
                      banks.append(ps)
                      for kt in range(KT):
                          P.mm(ps[:, 0:512], mixT[:, kt, tt * 128:(tt + 1) * 128], wo[:, kt, hf * 512:(hf + 1) * 512],
                               start=(kt == 0), stop=(kt == KT - 1))
                  pso[oi] = banks

              def o_s1(oi):
                  tt = otiles[oi]
                  g_ = gbc if tt >= 2 else gbcc
                  xt = xo4[oi % 4]
                  banks = pso.pop(oi)
                  for hf in range(2):
                      tmpo = tmps2[hf]
                      P.tt("dve", tmpo[:, 0:512], banks[hf][:, 0:512], g_[:, hf * 512:(hf + 1) * 512], ALU.mult)
                      P.tt("pool", xt[:, hf * 512:(hf + 1) * 512], tmpo[:, 0:512], xt[:, hf * 512:(hf + 1) * 512], ALU.add)
                  if last and tt >= 2:
                      P.act(junko, xt, AF.Square, scale=1.0 / 32.0, accum_out=sso[:, tt:tt + 1])

              def o_s2(oi):
                  tt = otiles[oi]
                  xt = xo4[oi % 4]
                  if not last:
                      dst = (xs_d[s, (tt - 2) * 128:(tt - 1) * 128, :] if tt >= 2 else cs_d[s, tt * 128:(tt + 1) * 128, :])
                      P.dma("act", dst, xt)
                  elif tt >= 2:
                      rstd_from_ms(rso[:, tt:tt + 1], sso[:, tt:tt + 1], 1)
                      P.stt("dve", xt, xt, rso[:, tt:tt + 1], fg, ALU.mult, ALU.mult)
                      P.dma("act", out_d[s, (tt - 2) * 128:(tt - 1) * 128, :], xt)

              pipeline(len(otiles), [o_s0, o_s1, o_s2])

    except _Stop:
        pass
    P.emit(nc)
    es.close()
    return nc, list(tap_out.keys())


def _prep_shared(inp):
    f = lambda a: np.ascontiguousarray(np.asarray(a, dtype=np.float32))
    C, S = _rope_tables()
    pr = _rope_perm32()
    perm_q = np.concatenate([np.concatenate([np.arange(64), 64 + pr]) + 96 * h for h in range(4)])
    perm_k = np.concatenate([np.arange(64), 64 + pr])
    ri, ci, inw = _na_index_tables()
    rpb = f(inp["na_rpb"])
    nab = rpb[:, :, ri, ci]
    nab = np.where(inw[None, None], nab, np.float32(MASK_FILL)).astype(np.float32)
    nab = np.ascontiguousarray(nab.transpose(0, 3, 1, 2, 4).reshape(2, 128, 4 * NTAB, 128))
    w_in = f(inp["w_in"])
    w_uq = f(inp["mla_w_uq"])
    sh = {
        "ident": np.eye(128, dtype=np.float32),
        "sel2": np.eye(2, dtype=np.float32),
        "ropeC": C, "ropeS": S,
        "norm_gT": f(f(inp["norm_g"]).reshape(2, 8, 128).transpose(0, 2, 1)),
        "final_g": f(inp["final_g"]),
        "w_mod": f(inp["w_mod"]),
        "b_mod": f(inp["b_mod"]),
        "w_in": w_in,
        "w_krsw": f(w_in[:, :, 1344:1440][:, :, perm_k]),
        "w_out": f(inp["w_out"]),
        "w_uq": w_uq,
        "w_uqsw": f(w_uq[:, :, perm_q]),
        "qn_gT": f(f(inp["mla_qn_g"]).reshape(2, 2, 128).transpose(0, 2, 1)),
        "w_ukv": f(inp["mla_w_ukv"]),
        "kvn_gT": f(f(inp["mla_kvn_g"]).reshape(2, 1, 128).transpose(0, 2, 1)),
        "w_sT": f(f(inp["cm_w_s"]).transpose(0, 3, 1, 2)),
        "b_s": f(f(inp["cm_b_s"]).reshape(2, 512)),
        "ln_g": f(inp["cm_ln_g"]),
        "sc_wT": f(f(inp["sc_w"]).reshape(2, 3, 2, 128).transpose(0, 3, 2, 1)),
        "na_bias": nab,
    }
    return sh


_CACHE = {}


def kernel(x, c, ctx, c_ctx, norm_g, w_mod, b_mod, w_in, na_rpb, mla_qn_g, mla_w_uq, mla_kvn_g,
           mla_w_ukv, cm_ln_g, cm_w_s, cm_b_s, sc_w, w_out, final_g):
    inp = dict(x=x, c=c, ctx=ctx, c_ctx=c_ctx, norm_g=norm_g, w_mod=w_mod, b_mod=b_mod, w_in=w_in,
               na_rpb=na_rpb, mla_qn_g=mla_qn_g, mla_w_uq=mla_w_uq, mla_kvn_g=mla_kvn_g,
               mla_w_ukv=mla_w_ukv, cm_ln_g=cm_ln_g, cm_w_s=cm_w_s, cm_b_s=cm_b_s, sc_w=sc_w,
               w_out=w_out, final_g=final_g)
    sh = _prep_shared(inp)
    x = np.asarray(x, dtype=np.float32)
    ctx = np.asarray(ctx, dtype=np.float32)
    c = np.asarray(c, dtype=np.float32)
    c_ctx = np.asarray(c_ctx, dtype=np.float32)
    if "nc" not in _CACHE:
        _CACHE["nc"] = build_program()[0]
    nc = _CACHE["nc"]
    in_maps = []
    for i in range(NCORES):
        cc = np.stack([c[2 * i], c[2 * i + 1], c_ctx], axis=0)
        cT = np.ascontiguousarray(cc.reshape(3, 8, 128).transpose(2, 1, 0))
        m = dict(sh)
        m["x"] = np.ascontiguousarray(x[2 * i:2 * i + 2])
        m["ctx"] = np.ascontiguousarray(ctx[2 * i:2 * i + 2])
        m["cT"] = cT
        in_maps.append(m)
    res = run_bass_kernel_spmd(nc, in_maps, core_ids=list(range(NCORES)))
    return np.concatenate([np.asarray(r["out"], dtype=np.float32) for r in res.results], axis=0)
```

```python
import contextlib
import os
DBG = os.environ.get('DBG', '')
import numpy as np
import concourse.bass as bass
import concourse.mybir as mybir
from concourse.bass_utils import run_bass_kernel_spmd

F32 = mybir.dt.float32
BF16 = mybir.dt.bfloat16
AF = mybir.ActivationFunctionType
ALU = mybir.AluOpType
AX = mybir.AxisListType

NCORES = 8
D = 1024
KT = 8
SEQ = 2048
CTXL = 256
T = SEQ + CTXL
NT = T // 128
DIN = 3488
EPS = 1e-6
MASK_FILL = -100.0
NTAB = 21


class _Op:
    __slots__ = ("eng", "fn", "deps", "raw", "is_dma", "dsem", "dcount", "ms", "need_inc", "waits")


def _foot(ap):
    t = ap.tensor
    kind = type(t).__name__
    pat = ap.ap
    off = int(ap.offset)
    if kind.startswith("DRam"):
        ext = 1
        for st, cnt in pat:
            ext += (cnt - 1) * abs(st)
        return (t.name, 0, 1, off, off + ext)
    row = 1
    for d in list(t.shape)[1:]:
        row *= int(d)
    p0 = off // row
    lo = off % row
    npart = pat[0][1]
    ext = 1
    for st, cnt in pat[1:]:
        ext += (cnt - 1) * abs(st)
    return (t.name, p0, p0 + npart, lo, lo + ext)


class Prog:
    ENG = ("pe", "act", "dve", "pool", "sp")
    NDS = 8

    def __init__(self):
        self.ops = []
        self.acc = {}
        self.dcnt = {q: [0] * self.NDS for q in ("sp", "pool", "act")}
        self.drr = {q: 0 for q in ("sp", "pool", "act")}

    def _access(self, ap, idx, is_write, deps, eng, is_dma, raw):
        name, p0, p1, lo, hi = _foot(ap)
        rw = is_write
        q0, q1, l0, h0 = p0, p1, lo, hi
        if type(ap.tensor).__name__.startswith("PSum"):
            p0, p1, lo, hi, is_write = 0, 128, 0, 1 << 30, True
        lst = self.acc.setdefault(name, [])
        keep = []
        for e in lst:
            ov = not (e[1] <= p0 or p1 <= e[0] or e[3] <= lo or hi <= e[2])
            if ov and (is_write or e[5]):
                deps.add(e[4])
                if (not rw) and e[8] and not (e[10] <= q0 or q1 <= e[9] or e[12] <= l0 or h0 <= e[11]):
                    raw.add(e[4])
            if is_write and ov and e[0] >= p0 and e[1] <= p1 and e[2] >= lo and e[3] <= hi:
                continue
            if (not is_write) and (not e[5]) and (not is_dma) and e[6] == eng and (not e[7]) \
                    and e[0] == p0 and e[1] == p1 and e[2] == lo and e[3] == hi:
                continue
            keep.append(e)
        keep.append((p0, p1, lo, hi, idx, is_write, eng, is_dma, rw, q0, q1, l0, h0))
        self.acc[name] = keep

    def add(self, eng, fn, reads=(), writes=(), is_dma=False):
        op = _Op()
        op.eng = eng
        op.fn = fn
        op.is_dma = is_dma
        op.need_inc = is_dma
        op.ms = 0
        idx = len(self.ops)
        deps = set()
        raw = set()
        for ap in reads:
            if ap is not None and not isinstance(ap, (int, float)):
                self._access(ap, idx, False, deps, eng, is_dma, raw)
        for ap in writes:
            self._access(ap, idx, True, deps, eng, is_dma, raw)
        deps.discard(idx)
        raw.discard(idx)
        op.deps = deps
        op.raw = raw
        if is_dma:
            k = self.drr[eng]
            self.drr[eng] = (k + 1) % self.NDS
            op.dsem = (eng, k)
            self.dcnt[eng][k] += 1
            op.dcount = self.dcnt[eng][k]
        self.ops.append(op)
        return idx

    def mm(self, out, lhsT, rhs, start=True, stop=True, skip=False):
        self.add("pe", lambda e: e.matmul(out, lhsT=lhsT, rhs=rhs, start=start, stop=stop, skip_group_check=skip),
                 reads=[lhsT, rhs], writes=[out])

    def tr(self, out, in_, ident):
        self.add("pe", lambda e: e.transpose(out, in_, ident), reads=[in_, ident], writes=[out])

    def act(self, out, in_, func, bias=None, scale=None, accum_out=None):
        kw = {}
        if bias is not None:
            kw["bias"] = bias
        if scale is not None:
            kw["scale"] = scale
        if accum_out is not None:
            kw["accum_out"] = accum_out
        w = [out] + ([accum_out] if accum_out is not None else [])
        self.add("act", lambda e: e.activation(out, in_, func, **kw), reads=[in_, bias, scale], writes=w)

    def tt(self, eng, out, in0, in1, op):
        self.add(eng, lambda e: e.tensor_tensor(out, in0, in1, op), reads=[in0, in1], writes=[out])

    def ts(self, eng, out, in0, s1, s2=None, op0=ALU.mult, op1=None):
        if op1 is None:
            self.add(eng, lambda e: e.tensor_scalar(out, in0, s1, None, op0), reads=[in0, s1], writes=[out])
        else:
            self.add(eng, lambda e: e.tensor_scalar(out, in0, s1, s2, op0, op1), reads=[in0, s1, s2], writes=[out])

    def stt(self, eng, out, in0, scalar, in1, op0, op1):
        self.add(eng, lambda e: e.scalar_tensor_tensor(out, in0, scalar, in1, op0, op1),
                 reads=[in0, scalar, in1], writes=[out])

    def cp(self, eng, out, in_):
        if eng == "act":
            self.add("act", lambda e: e.activation(out, in_, AF.Copy), reads=[in_], writes=[out])
        else:
            self.add(eng, lambda e: e.tensor_copy(out, in_), reads=[in_], writes=[out])

    def memset(self, eng, out, val):
        self.add(eng, lambda e: e.memset(out, val), writes=[out])

    def recip(self, out, in_):
        self.add("dve", lambda e: e.reciprocal(out, in_), reads=[in_], writes=[out])

    def red(self, out, in_, op=ALU.add):
        self.add("dve", lambda e: e.tensor_reduce(out, in_, AX.X, op), reads=[in_], writes=[out])

    def dma(self, q, out, in_):
        self.add(q, lambda e: e.dma_start(out=out, in_=in_), reads=[in_], writes=[out], is_dma=True)

    def emit(self, nc):
        ops = self.ops
        for op in ops:
            for d in op.deps:
                D = ops[d]
                if not D.is_dma and (D.eng != op.eng or op.eng != "pe"):
                    D.need_inc = True
        cnt = {e: 0 for e in self.ENG}
        for op in ops:
            if not op.is_dma and op.need_inc:
                cnt[op.eng] += 1
                op.ms = cnt[op.eng]
        waited = {e: {} for e in self.ENG}
        for op in ops:
            need = {}
            for d in op.deps:
                D = ops[d]
                if D.is_dma:
                    key, val = ("d",) + D.dsem, 16 * D.dcount
                elif D.eng == op.eng and op.eng == "pe":
                    continue
                else:
                    key, val = ("e", D.eng), D.ms
                if need.get(key, 0) < val:
                    need[key] = val
            if op.is_dma and op.dcount > 1:
                key = ("d",) + op.dsem
                need[key] = max(need.get(key, 0), 16 * (op.dcount - 1))
            w = waited[op.eng]
            op.waits = []
            for key, val in need.items():
                if w.get(key, 0) < val:
                    w[key] = val
                    op.waits.append((key, val))
        per = {e: [op for op in ops if op.eng == e] for e in self.ENG}
        with contextlib.ExitStack() as es:
            sems = {}
            for e in self.ENG:
                sems[("e", e)] = es.enter_context(nc.semaphore("s_" + e))
            for q in self.dcnt:
                for k in range(self.NDS):
                    sems[("d", q, k)] = es.enter_context(nc.semaphore("d_%s%d" % (q, k)))
            block = es.enter_context(nc.Block())

            def runner(name, final_wait=False):
                def f(e):
                    for op in per[name]:
                        for key, val in op.waits:
                            e.wait_ge(sems[key], val)
                        ins = op.fn(e)
                        if op.is_dma:
                            ins.then_inc(sems[("d",) + op.dsem], 16)
                        elif op.need_inc:
                            ins.then_inc(sems[("e", name)], 1)
                    if final_wait:
                        for q in self.dcnt:
                            for k in range(self.NDS):
                                if self.dcnt[q][k] > 0:
                                    e.wait_ge(sems[("d", q, k)], 16 * self.dcnt[q][k])
                return f

            block.sync(runner("sp", True))
            block.scalar(runner("act"))
            block.vector(runner("dve"))
            block.gpsimd(runner("pool"))
            block.tensor(runner("pe"))


def _rope_perm32():
    p = np.zeros(32, dtype=np.int64)
    for a in range(2):
        for h in range(2):
            for f in range(8):
                p[a * 16 + h * 8 + f] = a * 16 + (1 - h) * 8 + f
    return p


def _rope_tables():
    t = np.arange(SEQ)
    row = (t // 64).astype(np.float32)
    col = (t % 64).astype(np.float32)
    nf = 8
    inv = (np.float32(10000.0) ** (-np.arange(nf, dtype=np.float32) / np.float32(nf))).astype(np.float32)
    C = np.zeros((32, SEQ), dtype=np.float32)
    S = np.zeros((32, SEQ), dtype=np.float32)
    for a in range(2):
        pos = row if a == 0 else col
        ang = (pos[None, :] * inv[:, None]).astype(np.float32)
        c = np.cos(ang).astype(np.float32)
        s = np.sin(ang).astype(np.float32)
        C[a * 16:a * 16 + 8] = c
        C[a * 16 + 8:a * 16 + 16] = c
        S[a * 16:a * 16 + 8] = -s
        S[a * 16 + 8:a * 16 + 16] = s
    return C, S


def _na_local_tiles(tq):
    rows = [2 * tq, 2 * tq + 1]
    ks = set()
    for qr in rows:
        rs = min(max(qr - 4, 0), 24)
        for kr in range(rs, rs + 8):
            ks.add(kr // 2)
    return sorted(ks)


def _na_table_base(tq):
    if 2 <= tq <= 13:
        return 0
    return {0: 5, 1: 9, 14: 13, 15: 17}[tq]


def _na_index_tables():
    ri = np.zeros((NTAB, 128, 128), dtype=np.int64)
    ci = np.zeros((NTAB, 128, 128), dtype=np.int64)
    inw = np.zeros((NTAB, 128, 128), dtype=bool)
    done = set()
    for tq in range(16):
        base = _na_table_base(tq)
        if base in done:
            continue
        done.add(base)
        for j, tk in enumerate(_na_local_tiles(tq)):
            ki = np.arange(128)[:, None]
            qi = np.arange(128)[None, :]
            kr = 2 * tk + ki // 64
            kc = ki % 64
            qr = 2 * tq + qi // 64
            qc = qi % 64
            rs = np.clip(qr - 4, 0, 24)
            cs = np.clip(qc - 8, 0, 48)
            win = (kr >= rs) & (kr < rs + 8) & (kc >= cs) & (kc < cs + 16)
            ri[base + j] = np.clip(kr - qr + 7, 0, 14)
            ci[base + j] = np.clip(kc - qc, -15, 15) + 15
            inw[base + j] = win
    return ri, ci, inw


class _Stop(Exception):
    pass


def build_program(n_layers=2, taps=(), stop=None):
    nc = bass.Bass("TRN2", target_bir_lowering=False)
    P = Prog()
    es = contextlib.ExitStack()

    def din(name, shape, dt=F32):
        return nc.dram_tensor(name, list(shape), dt, kind="ExternalInput").ap()

    x_d = din("x", [2, SEQ, D])
    ctx_d = din("ctx", [2, CTXL, D])
    cT_d = din("cT", [128, 8, 3])
    ident_d = din("ident", [128, 128])
    sel2_d = din("sel2", [2, 2])
    ropeC_d = din("ropeC", [32, SEQ])
    ropeS_d = din("ropeS", [32, SEQ])
    normg_d = din("norm_gT", [2, 128, 8])
    finalg_d = din("final_g", [D])
    wmod_d = din("w_mod", [2, D, 3 * D])
    bmod_d = din("b_mod", [2, 3 * D])
    win_d = din("w_in", [2, D, DIN])
    krsw_d = din("w_krsw", [2, D, 96])
    wout_d = din("w_out", [2, D, D])
    wuq_d = din("w_uq", [2, 256, 384])
    wuqsw_d = din("w_uqsw", [2, 256, 384])
    qng_d = din("qn_gT", [2, 128, 2])
    wukv_d = din("w_ukv", [2, 128, 512])
    kvng_d = din("kvn_gT", [2, 128, 1])
    wsT_d = din("w_sT", [2, 128, 4, 128])
    bs_d = din("b_s", [2, 512])
    lng_d = din("ln_g", [2, 256])
    scw_d = din("sc_wT", [2, 128, 2, 3])
    nab_d = din("na_bias", [2, 128, 4 * NTAB, 128])
    out_d = nc.dram_tensor("out", [2, SEQ, D], F32, kind="ExternalOutput").ap()
    xs_d = nc.dram_tensor("xs_scr", [2, SEQ, D], F32).ap()
    cs_d = nc.dram_tensor("cs_scr", [2, CTXL, D], F32).ap()
    grow_d = nc.dram_tensor("grow_scr", [2, 3, D], F32).ap()
    expm_d = nc.dram_tensor("expm_scr", [2, 128, 4 * NTAB, 128], BF16).ap()
    tap_out = {}

    def sb(name, shape, dt):
        return es.enter_context(nc.sbuf_tensor(name, list(shape), dt))

    def pst(name, shape, dt):
        return es.enter_context(nc.psum_tensor(name, list(shape), dt))

    hxT = sb("hxT", [128, KT, T], BF16)
    mixT = sb("mixT", [128, KT, T], BF16)
    wg = [sb("wg0", [128, KT, 1024], BF16), sb("wg1", [128, KT, 1024], BF16)]
    ident = sb("identb", [128, 128], BF16)
    ones_f = sb("ones_f", [128, 128], F32)
    epsT = sb("epsT", [128, 1], F32)
    siluT = sb("siluT", [128, 8, 3], F32)
    modT = sb("modT", [128, 24, 3], F32)
    sce = sb("sce", [128, 8, 3], F32)
    normgT = sb("normgT", [128, 8], F32)
    identf = sb("identf", [3, 4], F32)
    wuq = sb("wuq", [128, 2, 384], BF16)
    wuqsw = sb("wuqsw", [128, 2, 384], BF16)
    wukv = sb("wukv", [128, 512], BF16)
    krsw = sb("krsw", [128, KT, 96], BF16)
    wsT = sb("wsT", [128, 4, 128], BF16)
    bs2 = sb("bs2", [2, 512], BF16)
    bsh2 = sb("bsh2", [2, 512], BF16)
    ones_b = sb("ones_b", [2, 128], BF16)
    sel2 = sb("sel2s", [2, 2], F32)
    zeroT = sb("zeroT", [128, 1], F32)
    lngbc = sb("lngbc", [128, 256], F32)
    scw = sb("scw", [128, 2, 3], F32)
    qng = sb("qng", [128, 2], F32)
    kvng = sb("kvng", [128, 1], F32)
    stat = sb("stat", [128, 256], F32)
    AB = 28 * 1024 + 512
    AFN = 8 * 1024
    arb = sb("arena_b", [128, AB], BF16)
    arf = sb("arena_f", [128, AFN], F32)
    psF = [pst("psF%d" % i, [128, 512], F32) for i in range(6)]
    psT = [pst("psT%d" % i, [128, 1024], BF16) for i in range(2)]
    cnt = {"f": 0, "t": 0}

    def nbF():
        cnt["f"] += 1
        return psF[cnt["f"] % 4]

    def nbO():
        cnt["o"] = cnt.get("o", 0) + 1
        return psF[4 + cnt["o"] % 2]

    def nbT():
        cnt["t"] += 1
        return psT[cnt["t"] % 2]

    class Carver:
        def __init__(self, t, size):
            self.t, self.size, self.off = t, size, 0

        def reset(self, off=0):
            self.off = off

        def take(self, shape):
            n = 1
            for d_ in shape:
                n *= d_
            assert self.off + n <= self.size, (self.off, n, self.size)
            ap = self.t[:, self.off:self.off + n]
            self.off += n
            if len(shape) == 2:
                return ap.rearrange("p (a b) -> p a b", a=shape[0])
            if len(shape) == 3:
                return ap.rearrange("p (a b c) -> p a b c", a=shape[0], b=shape[1])
            return ap

    CB = Carver(arb, AB)
    CF = Carver(arf, AFN)

    def tap(name, ap):
        if name not in taps:
            return
        shp = list(ap.shape)
        dt = ap.dtype
        dtn = nc.dram_tensor("tap_" + name, shp, dt, kind="ExternalOutput").ap()
        tap_out[name] = dtn
        P.dma("sp", dtn, ap)

    TBLK = [(0, 256)] + [(256 + 512 * i, 512) for i in range(4)]
    evac_rr = {"i": 0}

    def evac_eng():
        evac_rr["i"] += 1
        return "act" if evac_rr["i"] % 2 else "dve"

    def fm_proj(w, c0, nft, evac, blks=TBLK):
        for ft in range(nft):
            for (t0, n) in blks:
                ps = nbF()
                for kt in range(KT):
                    P.mm(ps[:, 0:n], w[:, kt, c0 + ft * 128:c0 + (ft + 1) * 128], hxT[:, kt, t0:t0 + n],
                         start=(kt == 0), stop=(kt == KT - 1))
                evac(ft, t0, n, ps)

    def tm_proj(w, c0, ncols, tt, ps):
        for kt in range(KT):
            P.mm(ps[:, 0:ncols], hxT[:, kt, tt * 128:(tt + 1) * 128], w[:, kt, c0:c0 + ncols],
                 start=(kt == 0), stop=(kt == KT - 1))

    P.dma("pool", ident[:], ident_d[:, :])
    P.dma("sp", identf[0:3, 0:3], ident_d[0:3, 0:3])
    P.dma("sp", sel2[:], sel2_d[:, :])
    P.memset("dve", ones_f[:], 1.0)
    P.memset("dve", epsT[:], EPS)
    P.memset("dve", zeroT[:], 0.0)
    P.memset("dve", ones_b[:], 1.0)
    CF.reset()
    cTs = CF.take([8, 3])
    P.dma("sp", cTs, cT_d[:, :, :])
    P.act(siluT[:], cTs, AF.Silu)

    GRP = [(0, 1024), (1024, 672), (1696, 768), (2464, 1024)]
    wg_state = {"i": 0}

    def load_group(l, g):
        buf = wg[wg_state["i"] % 2]
        wg_state["i"] += 1
        c0, n = GRP[g]
        src = win_d[l].rearrange("(kt p) c -> p kt c", p=128)
        for k2 in range(0, KT, 2):
            P.dma("pool", buf[:, k2:k2 + 2, 0:n], src[:, k2:k2 + 2, c0:c0 + n])
        return buf

    def pipeline(n, stages):
        ns = len(stages)
        for step in range(n + ns - 1):
            for k, st in enumerate(stages):
                i = step - k
                if 0 <= i < n:
                    st(i)

    def rstd_from_ms(dst, src, n):
        P.act(dst, src, AF.Sqrt, bias=epsT[:, 0:1], scale=1.0)
        P.recip(dst, dst)

    try:
      for l in range(n_layers):
          upd = (l == 0)
          last = (l == n_layers - 1)
          CF.reset()
          CB.reset()
          wA_pref = [load_group(l, 0)]
          P.dma("sp", normgT[:], normg_d[l])
          CF.reset()
          grow = CF.take([1024])
          wm = [CF.take([8, 256]) for _ in range(2)]
          rowb = [CF.take([256]) for _ in range(2)]
          bch = [CF.take([256]) for _ in range(2)]
          mst = [CF.take([7, 128]) for _ in range(2)]
          mbf = [CB.take([7, 128]) for _ in range(2)]
          psm = nbF()
          wsrc = wmod_d[l].rearrange("(kt p) c -> p kt c", p=128)
          bsrc = bmod_d[l].rearrange("(o n) -> o n", o=1)
          for j in range(12):
              wmj = wm[j % 2]
              P.dma("sp", wmj, wsrc[:, :, j * 256:(j + 1) * 256])
              P.dma("sp", bch[j % 2][0:1, :], bsrc[:, j * 256:(j + 1) * 256])
              P.dma("sp", mst[j % 2], nab_d[l][:, j * 7:(j + 1) * 7, :])
              P.act(mbf[j % 2], mst[j % 2], AF.Exp)
              P.dma("act", expm_d[l][:, j * 7:(j + 1) * 7, :], mbf[j % 2])
              psr = nbO()
              for kt in range(KT):
                  P.mm(psr[0:3, 0:256], siluT[:, kt, :], wmj[:, kt, :], start=(kt == 0), stop=False)
              P.mm(psr[0:3, 0:256], ones_f[0:1, 0:3], bch[j % 2][0:1, :], start=False, stop=True)
              rb = rowb[j % 2]
              P.cp("dve", rb[0:3, :], psr[0:3, 0:256])
              if j >= 8:
                  P.cp("dve", grow[0:3, (j - 8) * 256:(j - 7) * 256], psr[0:3, 0:256])
              for m in range(2):
                  mt = j * 2 + m
                  P.tr(psm[:, mt * 3:mt * 3 + 3], rb[0:3, m * 128:(m + 1) * 128], identf[0:3, 0:3])
          P.dma("sp", grow_d[l], grow[0:3, :])
          psm3 = psm[:, 0:72].rearrange("p (m b) -> p m b", b=3)
          P.cp("dve", modT[:, :, :], psm3)
          for b in range(3):
              P.stt("dve", sce[:, :, b], modT[:, 8:16, b], 1.0, normgT[:], ALU.add, ALU.mult)
          tap("modT%d" % l, modT[:])
          CF.reset()
          P.dma("act", qng[:], qng_d[l])
          P.dma("act", kvng[:], kvng_d[l])
          P.dma("act", lngbc[:], lng_d[l].partition_broadcast(128))
          P.dma("act", scw[:], scw_d[l])
          P.dma("pool", wsT[:], wsT_d[l])
          P.dma("pool", krsw[:], krsw_d[l].rearrange("(kt p) c -> p kt c", p=128))
          wst = CF.take([2, 384])
          for (dst, src) in ((wuq, wuq_d), (wuqsw, wuqsw_d)):
              P.dma("act", wst, src[l].rearrange("(kt p) c -> p kt c", p=128))
              for k2 in range(2):
                  P.ts("dve", dst[:, k2, :], wst[:, k2, :], qng[:, k2:k2 + 1])
          wst2 = CF.take([512])
          P.dma("act", wst2, wukv_d[l])
          P.ts("dve", wukv[:], wst2, kvng[:, 0:1])

          bs2f = CF.take([512])
          bs2t = CF.take([512])
          P.dma("act", bs2f[0:2, :], bs_d[l].partition_broadcast(2))
          P.cp("dve", bsh2[0:2, :], bs2f[0:2, :])
          P.tt("dve", bs2f[0:2, :], bs2f[0:2, :], bsh2[0:2, :], ALU.subtract)
          P.ts("dve", bs2t[0:2, :], bsh2[0:2, :], sel2[0:2, 0:1])
          P.stt("dve", bs2[0:2, :], bs2f[0:2, :], sel2[0:2, 1:2], bs2t[0:2, :], ALU.mult, ALU.add)

          if stop == 'M':
              raise _Stop()
          for s in range(2):
              CF.reset()
              CB.reset()
              aqT = CB.take([2, T])
              akT = CB.take([2, T])
              Va = CB.take([NT, 4, 65])
              maskb = [CB.take([NTAB, 128]) for _ in range(2)]
              PT = [CB.take([7, 128]) for _ in range(3)]
              oa_off = CB.off
              oa = CB.take([NT, 256])
              NXS = 7
              xst = [CF.take([1024]) for _ in range(NXS)]
              xn = [arb[:, oa_off + i * 1024:oa_off + (i + 1) * 1024] for i in range(3)]
              junk = arb[:, oa_off + 3072:oa_off + 4096]
              ssq = stat[:, 0:NT]
              rsd = stat[:, 32:32 + NT]
              P.memset("dve", ssq, 0.0)
              xsrc = x_d if l == 0 else xs_d
              csrc = ctx_d if l == 0 else cs_d
              wcur = wA_pref[0]
              pTn = {}

              def n_sd(tt):
                  src = csrc[s, tt * 128:(tt + 1) * 128, :] if tt < 2 else xsrc[s, (tt - 2) * 128:(tt - 1) * 128, :]
                  P.dma("sp", xst[tt % NXS], src)

              def n_s0(tt):
                  P.act(junk, xst[tt % NXS], AF.Square, scale=1.0 / 32.0, accum_out=ssq[:, tt:tt + 1])

              def n_s1a(tt):
                  P.act(rsd[:, tt:tt + 1], ssq[:, tt:tt + 1], AF.Sqrt, bias=epsT[:, 0:1], scale=1.0)

              def n_s1b(tt):
                  P.recip(rsd[:, tt:tt + 1], rsd[:, tt:tt + 1])

              def n_s1(tt):
                  P.tt("pool", xn[tt % 3][:, 0:640], xst[tt % NXS][:, 0:640], rsd[:, tt:tt + 1].to_broadcast([128, 640]), ALU.mult)
                  P.ts("dve", xn[tt % 3][:, 640:1024], xst[tt % NXS][:, 640:1024], rsd[:, tt:tt + 1])

              def n_s2(tt):
                  pT = nbT()
                  pTn[tt] = pT
                  xb = xn[tt % 3]
                  for kt in range(KT):
                      P.tr(pT[:, kt * 128:(kt + 1) * 128], xb[:, kt * 128:(kt + 1) * 128], ident[:])

              def n_s3(tt):
                  b = 2 if tt < 2 else s
                  pT = pTn.pop(tt)
                  for kt in range(KT):
                      o = hxT[:, kt, tt * 128:(tt + 1) * 128]
                      i_ = pT[:, kt * 128:(kt + 1) * 128]
                      if tt % 3 == 0:
                          P.act(o, i_, AF.Identity, bias=modT[:, kt, b:b + 1], scale=sce[:, kt, b:b + 1])
                      else:
                          P.ts("dve", o, i_, sce[:, kt, b:b + 1], modT[:, kt, b:b + 1], ALU.mult, ALU.add)

              blk_done = {(t0 + n) // 128 - 1: (t0, n) for (t0, n) in TBLK}
              P.memset("pool", Va[:, :, :, 64:65], 1.0)

              aitems = []

              def a_item_fm(ft, c0, kind, t0, n):
                  def f():
                      ps = nbF()
                      for kt in range(KT):
                          P.mm(ps[:, 0:n], wcur[:, kt, c0 + ft * 128:c0 + (ft + 1) * 128], hxT[:, kt, t0:t0 + n],
                               start=(kt == 0), stop=(kt == KT - 1))
                      if kind == "g":
                          P.act(mixT[:, ft, t0:t0 + n], ps[:, 0:n], AF.Silu)
                      elif kind == "q":
                          P.cp("dve", aqT[:, ft, t0:t0 + n], ps[:, 0:n])
                      else:
                          P.cp("act", akT[:, ft, t0:t0 + n], ps[:, 0:n])
                  return f

              def a_item_v(t2):
                  def f():
                      ps = nbF()
                      tm_proj(wcur, 512, 256, t2, ps)
                      P.cp("dve", Va[:, t2, :, 0:64], ps[:, 0:256].rearrange("p (h d) -> p h d", h=4))
                  return f

              def n_s4(tt):
                  if tt in blk_done:
                      t0, n = blk_done[tt]
                      for ft in range(2):
                          for (c0, kind) in ((768, "g"), (0, "q"), (256, "k")):
                              aitems.append(a_item_fm(ft, c0, kind, t0, n))
                      for t2 in range(t0 // 128, (t0 + n) // 128):
                          aitems.append(a_item_v(t2))
                  for _ in range(4):
                      if aitems:
                          aitems.pop(0)()

              pipeline(NT, [n_sd, n_s0, n_s1a, n_s1b, n_s1, n_s2, n_s3, n_s4])
              while aitems:
                  aitems.pop(0)()
              if s == 0:
                  tap("hxT%d" % l, hxT[:])

              if stop == 'N':
                  raise _Stop()
              rden = stat[:, 64:72]
              wnext = load_group(l, 1)

              def ev_q(ft, t0, n, ps):
                  P.cp(evac_eng(), aqT[:, ft, t0:t0 + n], ps[:, 0:n])

              def ev_k(ft, t0, n, ps):
                  P.cp(evac_eng(), akT[:, ft, t0:t0 + n], ps[:, 0:n])

              def ev_gate(slot):
                  def f(ft, t0, n, ps):
                      P.act(mixT[:, slot + ft, t0:t0 + n], ps[:, 0:n], AF.Silu)
                  return f


              def na_stage1(h, qt, PTt, wi=0):
                  ft, pb = h // 2, (h % 2) * 64
                  if qt >= 2:
                      loc = _na_local_tiles(qt - 2)
                      slots = [0, 1] + [t_ + 2 for t_ in loc]
                      mask = (2, _na_table_base(qt - 2), len(loc))
                  else:
                      slots = [0, 1]
                      mask = None
                  ns = len(slots)
                  banks = [nbF(), nbF()] if ns > 4 else [nbF()]
                  for j, ktile in enumerate(slots):
                      bk = banks[j // 4]
                      P.mm(bk[:, (j % 4) * 128:(j % 4 + 1) * 128],
                           akT[pb:pb + 64, ft, ktile * 128:(ktile + 1) * 128],
                           aqT[pb:pb + 64, ft, qt * 128:(qt + 1) * 128])
                  for bi, bk in enumerate(banks):
                      n_here = min(4, ns - bi * 4)
                      P.act(PTt[:, bi * 4:bi * 4 + n_here, :],
                            bk[:, 0:n_here * 128].rearrange("p (j q) -> p j q", j=n_here),
                            AF.Exp, scale=0.125)
                  if mask is not None:
                      mj, mt0, mn = mask
                      P.tt("pool", PTt[:, mj:mj + mn, :], PTt[:, mj:mj + mn, :], maskb[h % 2][:, mt0:mt0 + mn, :], ALU.mult)
                  return slots

              def na_stage2(h, qt, PTt, slots):
                  ns = len(slots)
                  po = nbO()
                  for j, ktile in enumerate(slots):
                      P.mm(po[:, 0:65], PTt[:, j, :], Va[:, ktile, h, :], start=(j == 0), stop=(j == ns - 1))
                  P.recip(rden[:, 0:1], po[:, 64:65])
                  P.ts("dve", oa[:, qt, h * 64:(h + 1) * 64], po[:, 0:64], rden[:, 0:1])

              work = []
              for h in range(4):
                  for qt in (list(range(2, NT)) + ([0, 1] if upd else [])):
                      work.append((h, qt))
              pend = []
              lasth = -1
              for wi, (h, qt) in enumerate(work):
                  if h != lasth:
                      P.dma("sp", maskb[h % 2], expm_d[l][:, h * NTAB:(h + 1) * NTAB, :])
                      lasth = h
                  PTt = PT[wi % 3]
                  slots = na_stage1(h, qt, PTt, wi)
                  pend.append((h, qt, PTt, slots))
                  if len(pend) > 2:
                      na_stage2(*pend.pop(0))
              while pend:
                  na_stage2(*pend.pop(0))
              for qt in (list(range(2, NT)) + ([0, 1] if upd else [])):
                  pT = nbT()
                  for ft in range(2):
                      P.tr(pT[:, ft * 128:(ft + 1) * 128], oa[:, qt, ft * 128:(ft + 1) * 128], ident[:])
                  for ft in range(2):
                      o = mixT[:, ft, qt * 128:(qt + 1) * 128]
                      P.tt("dve", o, pT[:, ft * 128:(ft + 1) * 128], o, ALU.mult)

              if stop == 'A':
                  raise _Stop()
              wcur = wnext
              CB.reset()
              CF.reset()
              cT3 = CB.take([3, T])
              KTh = [CB.take([T]) for _ in range(2)]
              QTh = [CB.take([T]) for _ in range(2)]
              krT = KTh[0]
              Vb = CB.take([NT, 4, 65])
              PTb = [CB.take([512]) for _ in range(4)]
              ob = CB.take([NT, 256])
              cst = [CB.take([384]) for _ in range(3)]
              junkb = CB.take([256])
              junkb2 = CB.take([128])
              ropeC = CF.take([SEQ])
              ropeS = CF.take([SEQ])
              tmp1 = [CF.take([512]) for _ in range(2)]
              tmp2 = [CF.take([512]) for _ in range(2)]
              ms2 = stat[:, 80:82]
              rq = stat[:, 84:86]
              rden4 = stat[:, 88:92]
              rdenb = [stat[:, 88:92], stat[:, 92:96]]
              rbi = [0]
              P.dma("sp", ropeC[64:96, :], ropeC_d[:, :])
              P.dma("sp", ropeS[64:96, :], ropeS_d[:, :])
              wnext = load_group(l, 2)
              fm_proj(wcur, 416, 2, ev_gate(2))
              P.memset("pool", Vb[:, :, :, 64:65], 1.0)
              msb = stat[:, 96:96 + 2 * NT].rearrange("p (t k) -> p t k", k=2)
              rqb = stat[:, 136:136 + 2 * NT].rearrange("p (t k) -> p t k", k=2)
              P.memset("dve", stat[:, 96:96 + 2 * NT], 0.0)
              psb = {}
              pTb = {}

              def b_s0(tt):
                  ps = nbF()
                  psb[tt] = ps
                  tm_proj(wcur, 0, 384, tt, ps)
                  P.act(junkb[:, 0:256], ps[:, 0:256], AF.Square, scale=1.0 / 16.0, accum_out=msb[:, tt, 0:1])
                  P.act(junkb2[:, 0:128], ps[:, 256:384], AF.Square, scale=float(128.0 ** -0.5), accum_out=msb[:, tt, 1:2])

              def b_s1a(tt):
                  P.act(rqb[:, tt, :], msb[:, tt, :], AF.Sqrt, bias=epsT[:, 0:1], scale=1.0)

              def b_s1(tt):
                  ps = psb.pop(tt)
                  P.recip(rqb[:, tt, :], rqb[:, tt, :])
                  c_ = cst[tt % 3]
                  P.ts("dve", c_[:, 0:256], ps[:, 0:256], rqb[:, tt, 0:1])
                  P.ts("dve", c_[:, 256:384], ps[:, 256:384], rqb[:, tt, 1:2])

              def b_s2(tt):
                  pT = nbT()
                  pTb[tt] = pT
                  c_ = cst[tt % 3]
                  for j in range(3):
                      P.tr(pT[:, j * 128:(j + 1) * 128], c_[:, j * 128:(j + 1) * 128], ident[:])

              def b_s3(tt):
                  pT = pTb.pop(tt)
                  P.cp(evac_eng(), cT3[:, :, tt * 128:(tt + 1) * 128], pT[:, 0:384].rearrange("p (j t) -> p j t", j=3))

              pipeline(NT, [b_s0, b_s1a, b_s1, b_s2, b_s3])
              for tt in range(NT):
                  ps = nbF()
                  P.mm(ps[:, 0:512], cT3[:, 2, tt * 128:(tt + 1) * 128], wukv[:, :])
                  P.cp(evac_eng(), Vb[:, tt, :, 0:64], ps[:, 0:512].rearrange("p (h d) -> p h d", h=4)[:, :, 64:128])

              def rope_evac(dst, psA, psB, t0, n, ri):
                  if t0 < CTXL:
                      P.cp("dve", dst[64:96, t0:t0 + n], psA[64:96, 0:n])
                      return
                  p0 = t0 - CTXL
                  a, b_ = tmp1[ri % 2], tmp2[ri % 2]
                  P.tt("dve", a[64:96, 0:n], psA[64:96, 0:n], ropeC[64:96, p0:p0 + n], ALU.mult)
                  P.tt("dve", b_[64:96, 0:n], psB[64:96, 0:n], ropeS[64:96, p0:p0 + n], ALU.mult)
                  P.tt("pool", dst[64:96, t0:t0 + n], a[64:96, 0:n], b_[64:96, 0:n], ALU.add)

              for bi, (t0, n) in enumerate(TBLK):
                  psA, psB = nbF(), nbF()
                  for kt in range(KT):
                      P.mm(psA[0:96, 0:n], wcur[:, kt, 320:416], hxT[:, kt, t0:t0 + n], start=(kt == 0), stop=(kt == KT - 1))
                  if t0 >= CTXL:
                      for kt in range(KT):
                          P.mm(psB[0:96, 0:n], krsw[:, kt, :], hxT[:, kt, t0:t0 + n], start=(kt == 0), stop=(kt == KT - 1))
                  rope_evac(krT, psA, psB, t0, n, bi)
              P.cp("dve", KTh[1][64:96, :], KTh[0][64:96, :])

              sc_b = float(96.0 ** -0.5)

              def b_proj_gen(h):
                  Kh = KTh[h % 2]
                  Qh = QTh[h % 2]
                  for bi, (t0, n) in enumerate(TBLK):
                      ps = nbF()
                      P.mm(ps[0:64, 0:n], wukv[:, h * 128:h * 128 + 64], cT3[:, 2, t0:t0 + n])
                      P.cp("dve", Kh[0:64, t0:t0 + n], ps[0:64, 0:n])
                      yield
                  for bi, (t0, n) in enumerate(TBLK):
                      if t0 < CTXL and not upd:
                          continue
                      psA, psB = nbF(), nbF()
                      for k2 in range(2):
                          P.mm(psA[0:96, 0:n], wuq[:, k2, h * 96:(h + 1) * 96], cT3[:, k2, t0:t0 + n], start=(k2 == 0), stop=(k2 == 1))
                      if t0 >= CTXL:
                          for k2 in range(2):
                              P.mm(psB[0:96, 0:n], wuqsw[:, k2, h * 96:(h + 1) * 96], cT3[:, k2, t0:t0 + n], start=(k2 == 0), stop=(k2 == 1))
                      P.cp("dve", Qh[0:64, t0:t0 + n], psA[0:64, 0:n])
                      rope_evac(Qh, psA, psB, t0, n, bi)
                      yield

              def b_proj(h):
                  for _ in b_proj_gen(h):
                      pass

              qblks = [(256 + 512 * i, 512, NT) for i in range(4)] + ([(0, 256, 2)] if upd else [])
              items = []
              for h in range(4):
                  for bq, (q0, nq, nk) in enumerate(qblks):
                      for i in range(nk):
                          items.append((h, bq, q0, nq, nk, i))
              LA = 3
              pos = {}
              gen = [None]
              b_proj(0)
              for idx in range(len(items) + LA):
                  if idx < len(items):
                      h, bq, q0, nq, nk, i = items[idx]
                      if bq == 0 and i == 0:
                          if gen[0] is not None:
                              for _ in gen[0]:
                                  pass
                          gen[0] = b_proj_gen(h + 1) if h + 1 < 4 else None
                      elif gen[0] is not None and idx % 6 == 3:
                          try:
                              next(gen[0])
                          except StopIteration:
                              gen[0] = None
                      if i == 0:
                          pos[(h, bq)] = nbO()
                      ps = nbF()
                      P.mm(ps[:, 0:nq], KTh[h % 2][0:96, i * 128:(i + 1) * 128], QTh[h % 2][0:96, q0:q0 + nq])
                      P.act(PTb[idx % 4][:, 0:nq], ps[:, 0:nq], AF.Exp, scale=sc_b)
                  if idx >= LA:
                      h, bq, q0, nq, nk, k_ = items[idx - LA]
                      nqs = nq // 128
                      po = pos[(h, bq)]
                      for qs in range(nqs):
                          P.mm(po[:, qs * 65:(qs + 1) * 65], PTb[(idx - LA) % 4][:, qs * 128:(qs + 1) * 128], Vb[:, k_, h, :],
                               start=(k_ == 0 and qs == 0), stop=(k_ == nk - 1), skip=True)
                      if k_ == nk - 1:
                          po3 = po[:, 0:nqs * 65].rearrange("p (q d) -> p q d", d=65)
                          rd = rdenb[rbi[0] % 2]
                          rbi[0] += 1
                          P.recip(rd[:, 0:nqs], po3[:, :, 64])
                          for qs in range(nqs):
                              qt = q0 // 128 + qs
                              P.ts("dve", ob[:, qt, h * 64:(h + 1) * 64], po[:, qs * 65:qs * 65 + 64], rd[:, qs:qs + 1])
                          del pos[(h, bq)]
              for qt in (list(range(2, NT)) + ([0, 1] if upd else [])):
                  pT = nbT()
                  for ft in range(2):
                      P.tr(pT[:, ft * 128:(ft + 1) * 128], ob[:, qt, ft * 128:(ft + 1) * 128], ident[:])
                  for ft in range(2):
                      o = mixT[:, 2 + ft, qt * 128:(qt + 1) * 128]
                      P.tt("dve", o, pT[:, ft * 128:(ft + 1) * 128], o, ALU.mult)

              if stop == 'B':
                  raise _Stop()
              wcur = wnext
              CB.reset()
              CF.reset()
              ugT = CB.take([2, T])
              vn = CB.take([NT, 256])
              sgt = [CB.take([512]) for _ in range(2)]
              gv = CF.take([NT, 256])
              sq = CF.take([256])
              vt = [CF.take([256]) for _ in range(2)]
              s1 = stat[:, 96:96 + 72].rearrange("p (t g) -> p t g", g=4)
              s2 = stat[:, 168:168 + 72].rearrange("p (t g) -> p t g", g=4)
              wnext = load_group(l, 3)
              cblks = TBLK if upd else TBLK[1:]
              ctiles = list(range(NT)) if upd else list(range(2, NT))

              def ev_u(ft, t0, n, ps):
                  P.act(ugT[:, ft, t0:t0 + n], ps[:, 0:n], AF.Gelu)

              fm_proj(wcur, 0, 2, ev_u, cblks)
              for tt in ctiles:
                  ps = nbF()
                  tm_proj(wcur, 256, 256, tt, ps)
                  P.act(gv[:, tt, :], ps[:, 0:256], AF.Gelu)
                  P.red(s1[:, tt, :], gv[:, tt, :].rearrange("p (g c) -> p g c", g=4))
                  P.tt("pool", sq, gv[:, tt, :], gv[:, tt, :], ALU.mult)
                  P.red(s2[:, tt, :], sq.rearrange("p (g c) -> p g c", g=4))
              def ev_g(ft, t0, n, ps):
                  nonlocal_ri = ev_g.ri
                  ev_g.ri += 1
                  sg_ = sgt[nonlocal_ri % 2]
                  P.act(sg_[:, 0:n], ps[:, 0:n], AF.Silu)
                  P.tt("pool", ugT[:, ft, t0:t0 + n], ugT[:, ft, t0:t0 + n], sg_[:, 0:n], ALU.mult)
              ev_g.ri = 0
              fm_proj(wcur, 512, 2, ev_g, cblks)
              s1f = stat[:, 96:96 + 72]
              s2f = stat[:, 168:168 + 72]
              m2 = CF.take([72])
              P.ts("dve", s1f, s1f, 1.0 / 64.0)
              P.tt("dve", m2, s1f, s1f, ALU.mult)
              P.stt("dve", s2f, s2f, 1.0 / 64.0, m2, ALU.mult, ALU.subtract)
              rstd_from_ms(s2f, s2f, 72)
              P.stt("dve", s1f, s1f, -1.0, s2f, ALU.mult, ALU.mult)
              for tt in ctiles:
                  v_ = vt[tt % 2]
                  for g in range(4):
                      P.act(v_[:, g * 64:(g + 1) * 64], gv[:, tt, g * 64:(g + 1) * 64], AF.Identity,
                            bias=s1[:, tt, g:g + 1], scale=s2[:, tt, g:g + 1])
                  P.tt("pool", vn[:, tt, :], v_, lngbc[:], ALU.mult)
              for tt in ctiles:
                  ps = nbF()
                  for ft in range(2):
                      for hf in range(2):
                          g = 2 * ft + hf
                          o = ps[:, (ft * 2 + hf) * 128:(ft * 2 + hf + 1) * 128]
                          P.mm(o, vn[:, tt, ft * 128:(ft + 1) * 128], wsT[:, g, :], start=True, stop=False)
                          P.mm(o, ones_b[0:2, :], bs2[0:2, g * 128:(g + 1) * 128], start=False, stop=True)
                  for ft in range(2):
                      for hf in range(2):
                          pr = slice(hf * 64, (hf + 1) * 64)
                          P.tt("dve", mixT[pr, 4 + ft, tt * 128:(tt + 1) * 128],
                               ps[pr, (ft * 2 + hf) * 128:(ft * 2 + hf + 1) * 128],
                               ugT[pr, ft, tt * 128:(tt + 1) * 128], ALU.mult)

              if stop == 'C':
                  raise _Stop()
              wcur = wnext
              CB.reset()
              CF.reset()
              bg = CB.take([2, T])
              zl = CF.take([2, SEQ + 2])
              zc = CF.take([2, CTXL + 2])
              dcs = [CF.take([512]) for _ in range(2)]
              sgd = [CF.take([512]) for _ in range(2)]
              yt = [CF.take([512]) for _ in range(2)]
              P.memset("pool", zl[:, :, 0:1], 0.0)
              P.memset("pool", zl[:, :, SEQ + 1:SEQ + 2], 0.0)
              P.memset("pool", zc[:, :, 0:1], 0.0)
              P.memset("pool", zc[:, :, CTXL + 1:CTXL + 2], 0.0)
              if s == 0:
                  wA_pref[0] = load_group(l, 0)
              CBw = Carver(arb, AB)
              CBw.reset(2 * T)
              wo = CBw.take([KT, 1024])
              for k2 in range(0, KT, 2):
                  P.dma("pool", wo[:, k2:k2 + 2, :], wout_d[l].rearrange("(kt p) c -> p kt c", p=128)[:, k2:k2 + 2, :])
              di = 0
              yi = [0]

              def conv_block(ft, t0, n):
                  zb, zo = (zc, t0) if t0 < CTXL else (zl, t0 - CTXL)
                  y = yt[yi[0] % 2]
                  yi[0] += 1
                  P.act(y[:, 0:n], zb[:, ft, 1 + zo:1 + zo + n], AF.Identity, bias=zeroT[:, 0:1], scale=scw[:, ft, 1:2])
                  P.stt("dve", y[:, 0:n], zb[:, ft, zo:zo + n], scw[:, ft, 0:1], y[:, 0:n], ALU.mult, ALU.add)
                  P.stt("dve", y[:, 0:n], zb[:, ft, 2 + zo:2 + zo + n], scw[:, ft, 2:3], y[:, 0:n], ALU.mult, ALU.add)
                  P.tt("pool", mixT[:, 6 + ft, t0:t0 + n], y[:, 0:n], bg[:, ft, t0:t0 + n], ALU.mult)

              for ft in range(2):
                  prev = None
                  for (t0, n) in cblks:
                      ps_c, ps_h, ps_b, ps_g = nbF(), nbF(), nbF(), nbF()
                      for (ps, c0) in ((ps_c, 256), (ps_h, 512), (ps_b, 0), (ps_g, 768)):
                          for kt in range(KT):
                              P.mm(ps[:, 0:n], wcur[:, kt, c0 + ft * 128:c0 + (ft + 1) * 128], hxT[:, kt, t0:t0 + n],
                                   start=(kt == 0), stop=(kt == KT - 1))
                      d_, g_ = dcs[di % 2], sgd[di % 2]
                      di += 1
                      P.cp("act", d_[:, 0:n], ps_c[:, 0:n])
                      P.act(g_[:, 0:n], ps_g[:, 0:n], AF.Silu)
                      zdst = zc[:, ft, 1 + t0:1 + t0 + n] if t0 < CTXL else zl[:, ft, 1 + t0 - CTXL:1 + t0 - CTXL + n]
                      P.tt("dve", zdst, ps_h[:, 0:n], d_[:, 0:n], ALU.mult)
                      P.tt("dve", bg[:, ft, t0:t0 + n], ps_b[:, 0:n], g_[:, 0:n], ALU.mult)
                      if t0 < CTXL:
                          conv_block(ft, t0, n)
                      else:
                          if prev is not None:
                              conv_block(ft, *prev)
                          prev = (t0, n)
                  conv_block(ft, *prev)
              if s == 0:
                  tap("mixT%d" % l, mixT[:])

              if stop == 'D':
                  raise _Stop()
              CF.reset()
              gbc = CF.take([1024])
              gbcc = CF.take([1024])
              fg = CF.take([1024])
              xo = [CF.take([1024]) for _ in range(4)]
              tmps2 = [CF.take([512]) for _ in range(2)]
              P.dma("sp", gbc, grow_d[l, s].partition_broadcast(128))
              if upd:
                  P.dma("sp", gbcc, grow_d[l, 2].partition_broadcast(128))
              if last:
                  P.dma("sp", fg, finalg_d.partition_broadcast(128))
              sso = stat[:, 0:NT]
              rso = stat[:, 32:32 + NT]
              P.memset("dve", sso, 0.0)
              junko = CB.take([1024]) if False else arb[:, 0:1024]
              otiles = list(range(2, NT)) + ([0, 1] if upd else [])
              xo4 = xo
              pso = {}

              def o_src(tt):
                  return (xsrc[s, (tt - 2) * 128:(tt - 1) * 128, :] if tt >= 2 else csrc[s, tt * 128:(tt + 1) * 128, :])

              def o_s0(oi):
                  tt = otiles[oi]
                  P.dma("sp", xo4[oi % 4], o_src(tt))
                  banks = []
                  for hf in range(2):
                      ps = nbF()
                      banks.append(ps)
                      for kt in range(KT):
                          P.mm(ps[:, 0:512], mixT[:, kt, tt * 128:(tt + 1) * 128], wo[:, kt, hf * 512:(hf + 1) * 512],
                               start=(kt == 0), stop=(kt == KT - 1))
                  pso[oi] = banks

              def o_s1(oi):
                  tt = otiles[oi]
                  g_ = gbc if tt >= 2 else gbcc
                  xt = xo4[oi % 4]
                  banks = pso.pop(oi)
                  for hf in range(2):
                      tmpo = tmps2[hf]
                      P.tt("dve", tmpo[:, 0:512], banks[hf][:, 0:512], g_[:, hf * 512:(hf + 1) * 512], ALU.mult)
                      P.tt("pool", xt[:, hf * 512:(hf + 1) * 512], tmpo[:, 0:512], xt[:, hf * 512:(hf + 1) * 512], ALU.add)
                  if last and tt >= 2:
                      P.act(junko, xt, AF.Square, scale=1.0 / 32.0, accum_out=sso[:, tt:tt + 1])

              def o_s2(oi):
                  tt = otiles[oi]
                  xt = xo4[oi % 4]
                  if not last:
                      dst = (xs_d[s, (tt - 2) * 128:(tt - 1) * 128, :] if tt >= 2 else cs_d[s, tt * 128:(tt + 1) * 128, :])
                      P.dma("act", dst, xt)
                  elif tt >= 2:
                      rstd_from_ms(rso[:, tt:tt + 1], sso[:, tt:tt + 1], 1)
                      P.stt("dve", xt, xt, rso[:, tt:tt + 1], fg, ALU.mult, ALU.mult)
                      P.dma("act", out_d[s, (tt - 2) * 128:(tt - 1) * 128, :], xt)

              pipeline(len(otiles), [o_s0, o_s1, o_s2])

    except _Stop:
        pass
    P.emit(nc)
    es.close()
    return nc, list(tap_out.keys())


def _prep_shared(inp):
    f = lambda a: np.ascontiguousarray(np.asarray(a, dtype=np.float32))
    C, S = _rope_tables()
    pr = _rope_perm32()
    perm_q = np.concatenate([np.concatenate([np.arange(64), 64 + pr]) + 96 * h for h in range(4)])
    perm_k = np.concatenate([np.arange(64), 64 + pr])
    ri, ci, inw = _na_index_tables()
    rpb = f(inp["na_rpb"])
    nab = rpb[:, :, ri, ci]
    nab = np.where(inw[None, None], nab, np.float32(MASK_FILL)).astype(np.float32)
    nab = np.ascontiguousarray(nab.transpose(0, 3, 1, 2, 4).reshape(2, 128, 4 * NTAB, 128))
    w_in = f(inp["w_in"])
    w_uq = f(inp["mla_w_uq"])
    sh = {
        "ident": np.eye(128, dtype=np.float32),
        "sel2": np.eye(2, dtype=np.float32),
        "ropeC": C, "ropeS": S,
        "norm_gT": f(f(inp["norm_g"]).reshape(2, 8, 128).transpose(0, 2, 1)),
        "final_g": f(inp["final_g"]),
        "w_mod": f(inp["w_mod"]),
        "b_mod": f(inp["b_mod"]),
        "w_in": w_in,
        "w_krsw": f(w_in[:, :, 1344:1440][:, :, perm_k]),
        "w_out": f(inp["w_out"]),
        "w_uq": w_uq,
        "w_uqsw": f(w_uq[:, :, perm_q]),
        "qn_gT": f(f(inp["mla_qn_g"]).reshape(2, 2, 128).transpose(0, 2, 1)),
        "w_ukv": f(inp["mla_w_ukv"]),
        "kvn_gT": f(f(inp["mla_kvn_g"]).reshape(2, 1, 128).transpose(0, 2, 1)),
        "w_sT": f(f(inp["cm_w_s"]).transpose(0, 3, 1, 2)),
        "b_s": f(f(inp["cm_b_s"]).reshape(2, 512)),
        "ln_g": f(inp["cm_ln_g"]),
        "sc_wT": f(f(inp["sc_w"]).reshape(2, 3, 2, 128).transpose(0, 3, 2, 1)),
        "na_bias": nab,
    }
    return sh


_CACHE = {}


def kernel(x, c, ctx, c_ctx, norm_g, w_mod, b_mod, w_in, na_rpb, mla_qn_g, mla_w_uq, mla_kvn_g,
           mla_w_ukv, cm_ln_g, cm_w_s, cm_b_s, sc_w, w_out, final_g):
    inp = dict(x=x, c=c, ctx=ctx, c_ctx=c_ctx, norm_g=norm_g, w_mod=w_mod, b_mod=b_mod, w_in=w_in,
               na_rpb=na_rpb, mla_qn_g=mla_qn_g, mla_w_uq=mla_w_uq, mla_kvn_g=mla_kvn_g,
               mla_w_ukv=mla_w_ukv, cm_ln_g=cm_ln_g, cm_w_s=cm_w_s, cm_b_s=cm_b_s, sc_w=sc_w,
               w_out=w_out, final_g=final_g)
    sh = _prep_shared(inp)
    x = np.asarray(x, dtype=np.float32)
    ctx = np.asarray(ctx, dtype=np.float32)
    c = np.asarray(c, dtype=np.float32)
    c_ctx = np.asarray(c_ctx, dtype=np.float32)
    if "nc" not in _CACHE:
        _CACHE["nc"] = build_program()[0]
    nc = _CACHE["nc"]
    in_maps = []
    for i in range(NCORES):
        cc = np.stack([c[2 * i], c[2 * i + 1], c_ctx], axis=0)
        cT = np.ascontiguousarray(cc.reshape(3, 8, 128).transpose(2, 1, 0))
        m = dict(sh)
        m["x"] = np.ascontiguousarray(x[2 * i:2 * i + 2])
        m["ctx"] = np.ascontiguousarray(ctx[2 * i:2 * i + 2])
        m["cT"] = cT
        in_maps.append(m)
    res = run_bass_kernel_spmd(nc, in_maps, core_ids=list(range(NCORES)))
    return np.concatenate([np.asarray(r["out"], dtype=np.float32) for r in res.results], axis=0)
```

```python
import contextlib
import os
DBG = os.environ.get('DBG', '')
import numpy as np
import concourse.bass as bass
import concourse.mybir as mybir
from concourse.bass_utils import run_bass_kernel_spmd

F32 = mybir.dt.float32
BF16 = mybir.dt.bfloat16
AF = mybir.ActivationFunctionType
ALU = mybir.AluOpType
AX = mybir.AxisListType

NCORES = 8
D = 1024
KT = 8
SEQ = 2048
CTXL = 256
T = SEQ + CTXL
NT = T // 128
DIN = 3488
EPS = 1e-6
MASK_FILL = -100.0
NTAB = 21


class _Op:
    __slots__ = ("eng", "fn", "deps", "raw", "is_dma", "dsem", "dcount", "ms", "need_inc", "waits")


def _foot(ap):
    t = ap.tensor
    kind = type(t).__name__
    pat = ap.ap
    off = int(ap.offset)
    if kind.startswith("DRam"):
        ext = 1
        for st, cnt in pat:
            ext += (cnt - 1) * abs(st)
        return (t.name, 0, 1, off, off + ext)
    row = 1
    for d in list(t.shape)[1:]:
        row *= int(d)
    p0 = off // row
    lo = off % row
    npart = pat[0][1]
    ext = 1
    for st, cnt in pat[1:]:
        ext += (cnt - 1) * abs(st)
    return (t.name, p0, p0 + npart, lo, lo + ext)


class Prog:
    ENG = ("pe", "act", "dve", "pool", "sp")
    NDS = 8

    def __init__(self):
        self.ops = []
        self.acc = {}
        self.dcnt = {q: [0] * self.NDS for q in ("sp", "pool", "act")}
        self.drr = {q: 0 for q in ("sp", "pool", "act")}

    def _access(self, ap, idx, is_write, deps, eng, is_dma, raw):
        name, p0, p1, lo, hi = _foot(ap)
        rw = is_write
        q0, q1, l0, h0 = p0, p1, lo, hi
        if type(ap.tensor).__name__.startswith("PSum"):
            p0, p1, lo, hi, is_write = 0, 128, 0, 1 << 30, True
        lst = self.acc.setdefault(name, [])
        keep = []
        for e in lst:
            ov = not (e[1] <= p0 or p1 <= e[0] or e[3] <= lo or hi <= e[2])
            if ov and (is_write or e[5]):
                deps.add(e[4])
                if (not rw) and e[8] and not (e[10] <= q0 or q1 <= e[9] or e[12] <= l0 or h0 <= e[11]):
                    raw.add(e[4])
            if is_write and ov and e[0] >= p0 and e[1] <= p1 and e[2] >= lo and e[3] <= hi:
                continue
            if (not is_write) and (not e[5]) and (not is_dma) and e[6] == eng and (not e[7]) \
                    and e[0] == p0 and e[1] == p1 and e[2] == lo and e[3] == hi:
                continue
            keep.append(e)
        keep.append((p0, p1, lo, hi, idx, is_write, eng, is_dma, rw, q0, q1, l0, h0))
        self.acc[name] = keep

    def add(self, eng, fn, reads=(), writes=(), is_dma=False):
        op = _Op()
        op.eng = eng
        op.fn = fn
        op.is_dma = is_dma
        op.need_inc = is_dma
        op.ms = 0
        idx = len(self.ops)
        deps = set()
        raw = set()
        for ap in reads:
            if ap is not None and not isinstance(ap, (int, float)):
                self._access(ap, idx, False, deps, eng, is_dma, raw)
        for ap in writes:
            self._access(ap, idx, True, deps, eng, is_dma, raw)
        deps.discard(idx)
        raw.discard(idx)
        op.deps = deps
        op.raw = raw
        if is_dma:
            k = self.drr[eng]
            self.drr[eng] = (k + 1) % self.NDS
            op.dsem = (eng, k)
            self.dcnt[eng][k] += 1
            op.dcount = self.dcnt[eng][k]
        self.ops.append(op)
        return idx

    def mm(self, out, lhsT, rhs, start=True, stop=True, skip=False):
        self.add("pe", lambda e: e.matmul(out, lhsT=lhsT, rhs=rhs, start=start, stop=stop, skip_group_check=skip),
                 reads=[lhsT, rhs], writes=[out])

    def tr(self, out, in_, ident):
        self.add("pe", lambda e: e.transpose(out, in_, ident), reads=[in_, ident], writes=[out])

    def act(self, out, in_, func, bias=None, scale=None, accum_out=None):
        kw = {}
        if bias is not None:
            kw["bias"] = bias
        if scale is not None:
            kw["scale"] = scale
        if accum_out is not None:
            kw["accum_out"] = accum_out
        w = [out] + ([accum_out] if accum_out is not None else [])
        self.add("act", lambda e: e.activation(out, in_, func, **kw), reads=[in_, bias, scale], writes=w)

    def tt(self, eng, out, in0, in1, op):
        self.add(eng, lambda e: e.tensor_tensor(out, in0, in1, op), reads=[in0, in1], writes=[out])

    def ts(self, eng, out, in0, s1, s2=None, op0=ALU.mult, op1=None):
        if op1 is None:
            self.add(eng, lambda e: e.tensor_scalar(out, in0, s1, None, op0), reads=[in0, s1], writes=[out])
        else:
            self.add(eng, lambda e: e.tensor_scalar(out, in0, s1, s2, op0, op1), reads=[in0, s1, s2], writes=[out])

    def stt(self, eng, out, in0, scalar, in1, op0, op1):
        self.add(eng, lambda e: e.scalar_tensor_tensor(out, in0, scalar, in1, op0, op1),
                 reads=[in0, scalar, in1], writes=[out])

    def cp(self, eng, out, in_):
        if eng == "act":
            self.add("act", lambda e: e.activation(out, in_, AF.Copy), reads=[in_], writes=[out])
        else:
            self.add(eng, lambda e: e.tensor_copy(out, in_), reads=[in_], writes=[out])

    def memset(self, eng, out, val):
        self.add(eng, lambda e: e.memset(out, val), writes=[out])

    def recip(self, out, in_):
        self.add("dve", lambda e: e.reciprocal(out, in_), reads=[in_], writes=[out])

    def red(self, out, in_, op=ALU.add):
        self.add("dve", lambda e: e.tensor_reduce(out, in_, AX.X, op), reads=[in_], writes=[out])

    def dma(self, q, out, in_):
        self.add(q, lambda e: e.dma_start(out=out, in_=in_), reads=[in_], writes=[out], is_dma=True)

    def emit(self, nc):
        ops = self.ops
        for op in ops:
            for d in op.deps:
                D = ops[d]
                if not D.is_dma and (D.eng != op.eng or op.eng != "pe"):
                    D.need_inc = True
        cnt = {e: 0 for e in self.ENG}
        for op in ops:
            if not op.is_dma and op.need_inc:
                cnt[op.eng] += 1
                op.ms = cnt[op.eng]
        waited = {e: {} for e in self.ENG}
        for op in ops:
            need = {}
            for d in op.deps:
                D = ops[d]
                if D.is_dma:
                    key, val = ("d",) + D.dsem, 16 * D.dcount
                elif D.eng == op.eng and op.eng == "pe":
                    continue
                else:
                    key, val = ("e", D.eng), D.ms
                if need.get(key, 0) < val:
                    need[key] = val
            if op.is_dma and op.dcount > 1:
                key = ("d",) + op.dsem
                need[key] = max(need.get(key, 0), 16 * (op.dcount - 1))
            w = waited[op.eng]
            op.waits = []
            for key, val in need.items():
                if w.get(key, 0) < val:
                    w[key] = val
                    op.waits.append((key, val))
        per = {e: [op for op in ops if op.eng == e] for e in self.ENG}
        with contextlib.ExitStack() as es:
            sems = {}
            for e in self.ENG:
                sems[("e", e)] = es.enter_context(nc.semaphore("s_" + e))
            for q in self.dcnt:
                for k in range(self.NDS):
                    sems[("d", q, k)] = es.enter_context(nc.semaphore("d_%s%d" % (q, k)))
            block = es.enter_context(nc.Block())

            def runner(name, final_wait=False):
                def f(e):
                    for op in per[name]:
                        for key, val in op.waits:
                            e.wait_ge(sems[key], val)
                        ins = op.fn(e)
                        if op.is_dma:
                            ins.then_inc(sems[("d",) + op.dsem], 16)
                        elif op.need_inc:
                            ins.then_inc(sems[("e", name)], 1)
                    if final_wait:
                        for q in self.dcnt:
                            for k in range(self.NDS):
                                if self.dcnt[q][k] > 0:
                                    e.wait_ge(sems[("d", q, k)], 16 * self.dcnt[q][k])
                return f

            block.sync(runner("sp", True))
            block.scalar(runner("act"))
            block.vector(runner("dve"))
            block.gpsimd(runner("pool"))
            block.tensor(runner("pe"))


def _rope_perm32():
    p = np.zeros(32, dtype=np.int64)
    for a in range(2):
        for h in range(2):
            for f in range(8):
                p[a * 16 + h * 8 + f] = a * 16 + (1 - h) * 8 + f
    return p


def _rope_tables():
    t = np.arange(SEQ)
    row = (t // 64).astype(np.float32)
    col = (t % 64).astype(np.float32)
    nf = 8
    inv = (np.float32(10000.0) ** (-np.arange(nf, dtype=np.float32) / np.float32(nf))).astype(np.float32)
    C = np.zeros((32, SEQ), dtype=np.float32)
    S = np.zeros((32, SEQ), dtype=np.float32)
    for a in range(2):
        pos = row if a == 0 else col
        ang = (pos[None, :] * inv[:, None]).astype(np.float32)
        c = np.cos(ang).astype(np.float32)
        s = np.sin(ang).astype(np.float32)
        C[a * 16:a * 16 + 8] = c
        C[a * 16 + 8:a * 16 + 16] = c
        S[a * 16:a * 16 + 8] = -s
        S[a * 16 + 8:a * 16 + 16] = s
    return C, S


def _na_local_tiles(tq):
    rows = [2 * tq, 2 * tq + 1]
    ks = set()
    for qr in rows:
        rs = min(max(qr - 4, 0), 24)
        for kr in range(rs, rs + 8):
            ks.add(kr // 2)
    return sorted(ks)


def _na_table_base(tq):
    if 2 <= tq <= 13:
        return 0
    return {0: 5, 1: 9, 14: 13, 15: 17}[tq]


def _na_index_tables():
    ri = np.zeros((NTAB, 128, 128), dtype=np.int64)
    ci = np.zeros((NTAB, 128, 128), dtype=np.int64)
    inw = np.zeros((NTAB, 128, 128), dtype=bool)
    done = set()
    for tq in range(16):
        base = _na_table_base(tq)
        if base in done:
            continue
        done.add(base)
        for j, tk in enumerate(_na_local_tiles(tq)):
            ki = np.arange(128)[:, None]
            qi = np.arange(128)[None, :]
            kr = 2 * tk + ki // 64
            kc = ki % 64
            qr = 2 * tq + qi // 64
            qc = qi % 64
            rs = np.clip(qr - 4, 0, 24)
            cs = np.clip(qc - 8, 0, 48)
            win = (kr >= rs) & (kr < rs + 8) & (kc >= cs) & (kc < cs + 16)
            ri[base + j] = np.clip(kr - qr + 7, 0, 14)
            ci[base + j] = np.clip(kc - qc, -15, 15) + 15
            inw[base + j] = win
    return ri, ci, inw


class _Stop(Exception):
    pass


def build_program(n_layers=2, taps=(), stop=None):
    nc = bass.Bass("TRN2", target_bir_lowering=False)
    P = Prog()
    es = contextlib.ExitStack()

    def din(name, shape, dt=F32):
        return nc.dram_tensor(name, list(shape), dt, kind="ExternalInput").ap()

    x_d = din("x", [2, SEQ, D])
    ctx_d = din("ctx", [2, CTXL, D])
    cT_d = din("cT", [128, 8, 3])
    ident_d = din("ident", [128, 128])
    sel2_d = din("sel2", [2, 2])
    ropeC_d = din("ropeC", [32, SEQ])
    ropeS_d = din("ropeS", [32, SEQ])
    normg_d = din("norm_gT", [2, 128, 8])
    finalg_d = din("final_g", [D])
    wmod_d = din("w_mod", [2, D, 3 * D])
    bmod_d = din("b_mod", [2, 3 * D])
    win_d = din("w_in", [2, D, DIN])
    krsw_d = din("w_krsw", [2, D, 96])
    wout_d = din("w_out", [2, D, D])
    wuq_d = din("w_uq", [2, 256, 384])
    wuqsw_d = din("w_uqsw", [2, 256, 384])
    qng_d = din("qn_gT", [2, 128, 2])
    wukv_d = din("w_ukv", [2, 128, 512])
    kvng_d = din("kvn_gT", [2, 128, 1])
    wsT_d = din("w_sT", [2, 128, 4, 128])
    bs_d = din("b_s", [2, 512])
    lng_d = din("ln_g", [2, 256])
    scw_d = din("sc_wT", [2, 128, 2, 3])
    nab_d = din("na_bias", [2, 128, 4 * NTAB, 128])
    out_d = nc.dram_tensor("out", [2, SEQ, D], F32, kind="ExternalOutput").ap()
    xs_d = nc.dram_tensor("xs_scr", [2, SEQ, D], F32).ap()
    cs_d = nc.dram_tensor("cs_scr", [2, CTXL, D], F32).ap()
    grow_d = nc.dram_tensor("grow_scr", [2, 3, D], F32).ap()
    expm_d = nc.dram_tensor("expm_scr", [2, 128, 4 * NTAB, 128], BF16).ap()
    tap_out = {}

    def sb(name, shape, dt):
        return es.enter_context(nc.sbuf_tensor(name, list(shape), dt))

    def pst(name, shape, dt):
        return es.enter_context(nc.psum_tensor(name, list(shape), dt))

    hxT = sb("hxT", [128, KT, T], BF16)
    mixT = sb("mixT", [128, KT, T], BF16)
    wg = [sb("wg0", [128, KT, 1024], BF16), sb("wg1", [128, KT, 1024], BF16)]
    ident = sb("identb", [128, 128], BF16)
    ones_f = sb("ones_f", [128, 128], F32)
    epsT = sb("epsT", [128, 1], F32)
    siluT = sb("siluT", [128, 8, 3], F32)
    modT = sb("modT", [128, 24, 3], F32)
    sce = sb("sce", [128, 8, 3], F32)
    normgT = sb("normgT", [128, 8], F32)
    identf = sb("identf", [3, 4], F32)
    wuq = sb("wuq", [128, 2, 384], BF16)
    wuqsw = sb("wuqsw", [128, 2, 384], BF16)
    wukv = sb("wukv", [128, 512], BF16)
    krsw = sb("krsw", [128, KT, 96], BF16)
    wsT = sb("wsT", [128, 4, 128], BF16)
    bs2 = sb("bs2", [2, 512], BF16)
    bsh2 = sb("bsh2", [2, 512], BF16)
    ones_b = sb("ones_b", [2, 128], BF16)
    sel2 = sb("sel2s", [2, 2], F32)
    zeroT = sb("zeroT", [128, 1], F32)
    lngbc = sb("lngbc", [128, 256], F32)
    scw = sb("scw", [128, 2, 3], F32)
    qng = sb("qng", [128, 2], F32)
    kvng = sb("kvng", [128, 1], F32)
    stat = sb("stat", [128, 256], F32)
    AB = 28 * 1024 + 512
    AFN = 8 * 1024
    arb = sb("arena_b", [128, AB], BF16)
    arf = sb("arena_f", [128, AFN], F32)
    psF = [pst("psF%d" % i, [128, 512], F32) for i in range(6)]
    psT = [pst("psT%d" % i, [128, 1024], BF16) for i in range(2)]
    cnt = {"f": 0, "t": 0}

    def nbF():
        cnt["f"] += 1
        return psF[cnt["f"] % 4]

    def nbO():
        cnt["o"] = cnt.get("o", 0) + 1
        return psF[4 + cnt["o"] % 2]

    def nbT():
        cnt["t"] += 1
        return psT[cnt["t"] % 2]

    class Carver:
        def __init__(self, t, size):
            self.t, self.size, self.off = t, size, 0

        def reset(self, off=0):
            self.off = off

        def take(self, shape):
            n = 1
            for d_ in shape:
                n *= d_
            assert self.off + n <= self.size, (self.off, n, self.size)
            ap = self.t[:, self.off:self.off + n]
            self.off += n
            if len(shape) == 2:
                return ap.rearrange("p (a b) -> p a b", a=shape[0])
            if len(shape) == 3:
                return ap.rearrange("p (a b c) -> p a b c", a=shape[0], b=shape[1])
            return ap

    CB = Carver(arb, AB)
    CF = Carver(arf, AFN)

    def tap(name, ap):
        if name not in taps:
            return
        shp = list(ap.shape)
        dt = ap.dtype
        dtn = nc.dram_tensor("tap_" + name, shp, dt, kind="ExternalOutput").ap()
        tap_out[name] = dtn
        P.dma("sp", dtn, ap)

    TBLK = [(0, 256)] + [(256 + 512 * i, 512) for i in range(4)]
    evac_rr = {"i": 0}

    def evac_eng():
        evac_rr["i"] += 1
        return "act" if evac_rr["i"] % 2 else "dve"

    def fm_proj(w, c0, nft, evac, blks=TBLK):
        for ft in range(nft):
            for (t0, n) in blks:
                ps = nbF()
                for kt in range(KT):
                    P.mm(ps[:, 0:n], w[:, kt, c0 + ft * 128:c0 + (ft + 1) * 128], hxT[:, kt, t0:t0 + n],
                         start=(kt == 0), stop=(kt == KT - 1))
                evac(ft, t0, n, ps)

    def tm_proj(w, c0, ncols, tt, ps):
        for kt in range(KT):
            P.mm(ps[:, 0:ncols], hxT[:, kt, tt * 128:(tt + 1) * 128], w[:, kt, c0:c0 + ncols],
                 start=(kt == 0), stop=(kt == KT - 1))

    P.dma("pool", ident[:], ident_d[:, :])
    P.dma("sp", identf[0:3, 0:3], ident_d[0:3, 0:3])
    P.dma("sp", sel2[:], sel2_d[:, :])
    P.memset("dve", ones_f[:], 1.0)
    P.memset("dve", epsT[:], EPS)
    P.memset("dve", zeroT[:], 0.0)
    P.memset("dve", ones_b[:], 1.0)
    CF.reset()
    cTs = CF.take([8, 3])
    P.dma("sp", cTs, cT_d[:, :, :])
    P.act(siluT[:], cTs, AF.Silu)

    GRP = [(0, 1024), (1024, 672), (1696, 768), (2464, 1024)]
    wg_state = {"i": 0}

    def load_group(l, g):
        buf = wg[wg_state["i"] % 2]
        wg_state["i"] += 1
        c0, n = GRP[g]
        src = win_d[l].rearrange("(kt p) c -> p kt c", p=128)
        for k2 in range(0, KT, 2):
            P.dma("pool", buf[:, k2:k2 + 2, 0:n], src[:, k2:k2 + 2, c0:c0 + n])
        return buf

    def pipeline(n, stages):
        ns = len(stages)
        for step in range(n + ns - 1):
            for k, st in enumerate(stages):
                i = step - k
                if 0 <= i < n:
                    st(i)

    def rstd_from_ms(dst, src, n):
        P.act(dst, src, AF.Sqrt, bias=epsT[:, 0:1], scale=1.0)
        P.recip(dst, dst)

    try:
      for l in range(n_layers):
          upd = (l == 0)
          last = (l == n_layers - 1)
          CF.reset()
          CB.reset()
          wA_pref = [load_group(l, 0)]
          P.dma("sp", normgT[:], normg_d[l])
          CF.reset()
          grow = CF.take([1024])
          wm = [CF.take([8, 256]) for _ in range(2)]
          rowb = [CF.take([256]) for _ in range(2)]
          bch = [CF.take([256]) for _ in range(2)]
          mst = [CF.take([7, 128]) for _ in range(2)]
          mbf = [CB.take([7, 128]) for _ in range(2)]
          psm = nbF()
          wsrc = wmod_d[l].rearrange("(kt p) c -> p kt c", p=128)
          bsrc = bmod_d[l].rearrange("(o n) -> o n", o=1)
          for j in range(12):
              wmj = wm[j % 2]
              P.dma("sp", wmj, wsrc[:, :, j * 256:(j + 1) * 256])
              P.dma("sp", bch[j % 2][0:1, :], bsrc[:, j * 256:(j + 1) * 256])
              P.dma("sp", mst[j % 2], nab_d[l][:, j * 7:(j + 1) * 7, :])
              P.act(mbf[j % 2], mst[j % 2], AF.Exp)
              P.dma("act", expm_d[l][:, j * 7:(j + 1) * 7, :], mbf[j % 2])
              psr = nbO()
              for kt in range(KT):
                  P.mm(psr[0:3, 0:256], siluT[:, kt, :], wmj[:, kt, :], start=(kt == 0), stop=False)
              P.mm(psr[0:3, 0:256], ones_f[0:1, 0:3], bch[j % 2][0:1, :], start=False, stop=True)
              rb = rowb[j % 2]
              P.cp("dve", rb[0:3, :], psr[0:3, 0:256])
              if j >= 8:
                  P.cp("dve", grow[0:3, (j - 8) * 256:(j - 7) * 256], psr[0:3, 0:256])
              for m in range(2):
                  mt = j * 2 + m
                  P.tr(psm[:, mt * 3:mt * 3 + 3], rb[0:3, m * 128:(m + 1) * 128], identf[0:3, 0:3])
          P.dma("sp", grow_d[l], grow[0:3, :])
          psm3 = psm[:, 0:72].rearrange("p (m b) -> p m b", b=3)
          P.cp("dve", modT[:, :, :], psm3)
          for b in range(3):
              P.stt("dve", sce[:, :, b], modT[:, 8:16, b], 1.0, normgT[:], ALU.add, ALU.mult)
          tap("modT%d" % l, modT[:])
          CF.reset()
          P.dma("act", qng[:], qng_d[l])
          P.dma("act", kvng[:], kvng_d[l])
          P.dma("act", lngbc[:], lng_d[l].partition_broadcast(128))
          P.dma("act", scw[:], scw_d[l])
          P.dma("pool", wsT[:], wsT_d[l])
          P.dma("pool", krsw[:], krsw_d[l].rearrange("(kt p) c -> p kt c", p=128))
          wst = CF.take([2, 384])
          for (dst, src) in ((wuq, wuq_d), (wuqsw, wuqsw_d)):
              P.dma("act", wst, src[l].rearrange("(kt p) c -> p kt c", p=128))
              for k2 in range(2):
                  P.ts("dve", dst[:, k2, :], wst[:, k2, :], qng[:, k2:k2 + 1])
          wst2 = CF.take([512])
          P.dma("act", wst2, wukv_d[l])
          P.ts("dve", wukv[:], wst2, kvng[:, 0:1])

          bs2f = CF.take([512])
          bs2t = CF.take([512])
          P.dma("act", bs2f[0:2, :], bs_d[l].partition_broadcast(2))
          P.cp("dve", bsh2[0:2, :], bs2f[0:2, :])
          P.tt("dve", bs2f[0:2, :], bs2f[0:2, :], bsh2[0:2, :], ALU.subtract)
          P.ts("dve", bs2t[0:2, :], bsh2[0:2, :], sel2[0:2, 0:1])
          P.stt("dve", bs2[0:2, :], bs2f[0:2, :], sel2[0:2, 1:2], bs2t[0:2, :], ALU.mult, ALU.add)

          if stop == 'M':
              raise _Stop()
          for s in range(2):
              CF.reset()
              CB.reset()
              aqT = CB.take([2, T])
              akT = CB.take([2, T])
              Va = CB.take([NT, 4, 65])
              maskb = [CB.take([NTAB, 128]) for _ in range(2)]
              PT = [CB.take([7, 128]) for _ in range(3)]
              oa_off = CB.off
              oa = CB.take([NT, 256])
              NXS = 7
              xst = [CF.take([1024]) for _ in range(NXS)]
              xn = [arb[:, oa_off + i * 1024:oa_off + (i + 1) * 1024] for i in range(3)]
              junk = arb[:, oa_off + 3072:oa_off + 4096]
              ssq = stat[:, 0:NT]
              rsd = stat[:, 32:32 + NT]
              P.memset("dve", ssq, 0.0)
              xsrc = x_d if l == 0 else xs_d
              csrc = ctx_d if l == 0 else cs_d
              wcur = wA_pref[0]
              pTn = {}

              def n_sd(tt):
                  src = csrc[s, tt * 128:(tt + 1) * 128, :] if tt < 2 else xsrc[s, (tt - 2) * 128:(tt - 1) * 128, :]
                  P.dma("sp", xst[tt % NXS], src)

              def n_s0(tt):
                  P.act(junk, xst[tt % NXS], AF.Square, scale=1.0 / 32.0, accum_out=ssq[:, tt:tt + 1])

              def n_s1a(tt):
                  P.act(rsd[:, tt:tt + 1], ssq[:, tt:tt + 1], AF.Sqrt, bias=epsT[:, 0:1], scale=1.0)

              def n_s1b(tt):
                  P.recip(rsd[:, tt:tt + 1], rsd[:, tt:tt + 1])

              def n_s1(tt):
                  P.tt("pool", xn[tt % 3][:, 0:640], xst[tt % NXS][:, 0:640], rsd[:, tt:tt + 1].to_broadcast([128, 640]), ALU.mult)
                  P.ts("dve", xn[tt % 3][:, 640:1024], xst[tt % NXS][:, 640:1024], rsd[:, tt:tt + 1])

              def n_s2(tt):
                  pT = nbT()
                  pTn[tt] = pT
                  xb = xn[tt % 3]
                  for kt in range(KT):
                      P.tr(pT[:, kt * 128:(kt + 1) * 128], xb[:, kt * 128:(kt + 1) * 128], ident[:])

              def n_s3(tt):
                  b = 2 if tt < 2 else s
                  pT = pTn.pop(tt)
                  for kt in range(KT):
                      o = hxT[:, kt, tt * 128:(tt + 1) * 128]
                      i_ = pT[:, kt * 128:(kt + 1) * 128]
                      if tt % 3 == 0:
                          P.act(o, i_, AF.Identity, bias=modT[:, kt, b:b + 1], scale=sce[:, kt, b:b + 1])
                      else:
                          P.ts("dve", o, i_, sce[:, kt, b:b + 1], modT[:, kt, b:b + 1], ALU.mult, ALU.add)

              blk_done = {(t0 + n) // 128 - 1: (t0, n) for (t0, n) in TBLK}
              P.memset("pool", Va[:, :, :, 64:65], 1.0)

              aitems = []

              def a_item_fm(ft, c0, kind, t0, n):
                  def f():
                      ps = nbF()
                      for kt in range(KT):
                          P.mm(ps[:, 0:n], wcur[:, kt, c0 + ft * 128:c0 + (ft + 1) * 128], hxT[:, kt, t0:t0 + n],
                               start=(kt == 0), stop=(kt == KT - 1))
                      if kind == "g":
                          P.act(mixT[:, ft, t0:t0 + n], ps[:, 0:n], AF.Silu)
                      elif kind == "q":
                          P.cp("dve", aqT[:, ft, t0:t0 + n], ps[:, 0:n])
                      else:
                          P.cp("act", akT[:, ft, t0:t0 + n], ps[:, 0:n])
                  return f

              def a_item_v(t2):
                  def f():
                      ps = nbF()
                      tm_proj(wcur, 512, 256, t2, ps)
                      P.cp("dve", Va[:, t2, :, 0:64], ps[:, 0:256].rearrange("p (h d) -> p h d", h=4))
                  return f

              def n_s4(tt):
                  if tt in blk_done:
                      t0, n = blk_done[tt]
                      for ft in range(2):
                          for (c0, kind) in ((768, "g"), (0, "q"), (256, "k")):
                              aitems.append(a_item_fm(ft, c0, kind, t0, n))
                      for t2 in range(t0 // 128, (t0 + n) // 128):
                          aitems.append(a_item_v(t2))
                  for _ in range(4):
                      if aitems:
                          aitems.pop(0)()

              pipeline(NT, [n_sd, n_s0, n_s1a, n_s1b, n_s1, n_s2, n_s3, n_s4])
              while aitems:
                  aitems.pop(0)()
              if s == 0:
                  tap("hxT%d" % l, hxT[:])

              if stop == 'N':
                  raise _Stop()
              rden = stat[:, 64:72]
              wnext = load_group(l, 1)

              def ev_q(ft, t0, n, ps):
                  P.cp(evac_eng(), aqT[:, ft, t0:t0 + n], ps[:, 0:n])

              def ev_k(ft, t0, n, ps):
                  P.cp(evac_eng(), akT[:, ft, t0:t0 + n], ps[:, 0:n])

              def ev_gate(slot):
                  def f(ft, t0, n, ps):
                      P.act(mixT[:, slot + ft, t0:t0 + n], ps[:, 0:n], AF.Silu)
                  return f


              def na_stage1(h, qt, PTt, wi=0):
                  ft, pb = h // 2, (h % 2) * 64
                  if qt >= 2:
                      loc = _na_local_tiles(qt - 2)
                      slots = [0, 1] + [t_ + 2 for t_ in loc]
                      mask = (2, _na_table_base(qt - 2), len(loc))
                  else:
                      slots = [0, 1]
                      mask = None
                  ns = len(slots)
                  banks = [nbF(), nbF()] if ns > 4 else [nbF()]
                  for j, ktile in enumerate(slots):
                      bk = banks[j // 4]
                      P.mm(bk[:, (j % 4) * 128:(j % 4 + 1) * 128],
                           akT[pb:pb + 64, ft, ktile * 128:(ktile + 1) * 128],
                           aqT[pb:pb + 64, ft, qt * 128:(qt + 1) * 128])
                  for bi, bk in enumerate(banks):
                      n_here = min(4, ns - bi * 4)
                      P.act(PTt[:, bi * 4:bi * 4 + n_here, :],
                            bk[:, 0:n_here * 128].rearrange("p (j q) -> p j q", j=n_here),
                            AF.Exp, scale=0.125)
                  if mask is not None:
                      mj, mt0, mn = mask
                      P.tt("pool", PTt[:, mj:mj + mn, :], PTt[:, mj:mj + mn, :], maskb[h % 2][:, mt0:mt0 + mn, :], ALU.mult)
                  return slots

              def na_stage2(h, qt, PTt, slots):
                  ns = len(slots)
                  po = nbO()
                  for j, ktile in enumerate(slots):
                      P.mm(po[:, 0:65], PTt[:, j, :], Va[:, ktile, h, :], start=(j == 0), stop=(j == ns - 1))
                  P.recip(rden[:, 0:1], po[:, 64:65])
                  P.ts("dve", oa[:, qt, h * 64:(h + 1) * 64], po[:, 0:64], rden[:, 0:1])

              work = []
              for h in range(4):
                  for qt in (list(range(2, NT)) + ([0, 1] if upd else [])):
                      work.append((h, qt))
              pend = []
              lasth = -1
              for wi, (h, qt) in enumerate(work):
                  if h != lasth:
                      P.dma("sp", maskb[h % 2], expm_d[l][:, h * NTAB:(h + 1) * NTAB, :])
                      lasth = h
                  PTt = PT[wi % 3]
                  slots = na_stage1(h, qt, PTt, wi)
                  pend.append((h, qt, PTt, slots))
                  if len(pend) > 2:
                      na_stage2(*pend.pop(0))
              while pend:
                  na_stage2(*pend.pop(0))
              for qt in (list(range(2, NT)) + ([0, 1] if upd else [])):
                  pT = nbT()
                  for ft in range(2):
                      P.tr(pT[:, ft * 128:(ft + 1) * 128], oa[:, qt, ft * 128:(ft + 1) * 128], ident[:])
                  for ft in range(2):
                      o = mixT[:, ft, qt * 128:(qt + 1) * 128]
                      P.tt("dve", o, pT[:, ft * 128:(ft + 1) * 128], o, ALU.mult)

              if stop == 'A':
                  raise _Stop()
              wcur = wnext
              CB.reset()
              CF.reset()
              cT3 = CB.take([3, T])
              KTh = [CB.take([T]) for _ in range(2)]
              QTh = [CB.take([T]) for _ in range(2)]
              krT = KTh[0]
              Vb = CB.take([NT, 4, 65])
              PTb = [CB.take([512]) for _ in range(4)]
              ob = CB.take([NT, 256])
              cst = [CB.take([384]) for _ in range(3)]
              junkb = CB.take([256])
              junkb2 = CB.take([128])
              ropeC = CF.take([SEQ])
              ropeS = CF.take([SEQ])
              tmp1 = [CF.take([512]) for _ in range(2)]
              tmp2 = [CF.take([512]) for _ in range(2)]
              ms2 = stat[:, 80:82]
              rq = stat[:, 84:86]
              rden4 = stat[:, 88:92]
              rdenb = [stat[:, 88:92], stat[:, 92:96]]
              rbi = [0]
              P.dma("sp", ropeC[64:96, :], ropeC_d[:, :])
              P.dma("sp", ropeS[64:96, :], ropeS_d[:, :])
              wnext = load_group(l, 2)
              fm_proj(wcur, 416, 2, ev_gate(2))
              P.memset("pool", Vb[:, :, :, 64:65], 1.0)
              msb = stat[:, 96:96 + 2 * NT].rearrange("p (t k) -> p t k", k=2)
              rqb = stat[:, 136:136 + 2 * NT].rearrange("p (t k) -> p t k", k=2)
              P.memset("dve", stat[:, 96:96 + 2 * NT], 0.0)
              psb = {}
              pTb = {}

              def b_s0(tt):
                  ps = nbF()
                  psb[tt] = ps
                  tm_proj(wcur, 0, 384, tt, ps)
                  P.act(junkb[:, 0:256], ps[:, 0:256], AF.Square, scale=1.0 / 16.0, accum_out=msb[:, tt, 0:1])
                  P.act(junkb2[:, 0:128], ps[:, 256:384], AF.Square, scale=float(128.0 ** -0.5), accum_out=msb[:, tt, 1:2])

              def b_s1a(tt):
                  P.act(rqb[:, tt, :], msb[:, tt, :], AF.Sqrt, bias=epsT[:, 0:1], scale=1.0)

              def b_s1(tt):
                  ps = psb.pop(tt)
                  P.recip(rqb[:, tt, :], rqb[:, tt, :])
                  c_ = cst[tt % 3]
                  P.ts("dve", c_[:, 0:256], ps[:, 0:256], rqb[:, tt, 0:1])
                  P.ts("dve", c_[:, 256:384], ps[:, 256:384], rqb[:, tt, 1:2])

              def b_s2(tt):
                  pT = nbT()
                  pTb[tt] = pT
                  c_ = cst[tt % 3]
                  for j in range(3):
                      P.tr(pT[:, j * 128:(j + 1) * 128], c_[:, j * 128:(j + 1) * 128], ident[:])

              def b_s3(tt):
                  pT = pTb.pop(tt)
                  P.cp(evac_eng(), cT3[:, :, tt * 128:(tt + 1) * 128], pT[:, 0:384].rearrange("p (j t) -> p j t", j=3))

              pipeline(NT, [b_s0, b_s1a, b_s1, b_s2, b_s3])
              for tt in range(NT):
                  ps = nbF()
                  P.mm(ps[:, 0:512], cT3[:, 2, tt * 128:(tt + 1) * 128], wukv[:, :])
                  P.cp(evac_eng(), Vb[:, tt, :, 0:64], ps[:, 0:512].rearrange("p (h d) -> p h d", h=4)[:, :, 64:128])

              def rope_evac(dst, psA, psB, t0, n, ri):
                  if t0 < CTXL:
                      P.cp("dve", dst[64:96, t0:t0 + n], psA[64:96, 0:n])
                      return
                  p0 = t0 - CTXL
                  a, b_ = tmp1[ri % 2], tmp2[ri % 2]
                  P.tt("dve", a[64:96, 0:n], psA[64:96, 0:n], ropeC[64:96, p0:p0 + n], ALU.mult)
                  P.tt("dve", b_[64:96, 0:n], psB[64:96, 0:n], ropeS[64:96, p0:p0 + n], ALU.mult)
                  P.tt("pool", dst[64:96, t0:t0 + n], a[64:96, 0:n], b_[64:96, 0:n], ALU.add)

              for bi, (t0, n) in enumerate(TBLK):
                  psA, psB = nbF(), nbF()
                  for kt in range(KT):
                      P.mm(psA[0:96, 0:n], wcur[:, kt, 320:416], hxT[:, kt, t0:t0 + n], start=(kt == 0), stop=(kt == KT - 1))
                  if t0 >= CTXL:
                      for kt in range(KT):
                          P.mm(psB[0:96, 0:n], krsw[:, kt, :], hxT[:, kt, t0:t0 + n], start=(kt == 0), stop=(kt == KT - 1))
                  rope_evac(krT, psA, psB, t0, n, bi)
              P.cp("dve", KTh[1][64:96, :], KTh[0][64:96, :])

              sc_b = float(96.0 ** -0.5)

              def b_proj_gen(h):
                  Kh = KTh[h % 2]
                  Qh = QTh[h % 2]
                  for bi, (t0, n) in enumerate(TBLK):
                      ps = nbF()
                      P.mm(ps[0:64, 0:n], wukv[:, h * 128:h * 128 + 64], cT3[:, 2, t0:t0 + n])
                      P.cp("dve", Kh[0:64, t0:t0 + n], ps[0:64, 0:n])
                      yield
                  for bi, (t0, n) in enumerate(TBLK):
                      if t0 < CTXL and not upd:
                          continue
                      psA, psB = nbF(), nbF()
                      for k2 in range(2):
                          P.mm(psA[0:96, 0:n], wuq[:, k2, h * 96:(h + 1) * 96], cT3[:, k2, t0:t0 + n], start=(k2 == 0), stop=(k2 == 1))
                      if t0 >= CTXL:
                          for k2 in range(2):
                              P.mm(psB[0:96, 0:n], wuqsw[:, k2, h * 96:(h + 1) * 96], cT3[:, k2, t0:t0 + n], start=(k2 == 0), stop=(k2 == 1))
                      P.cp("dve", Qh[0:64, t0:t0 + n], psA[0:64, 0:n])
                      rope_evac(Qh, psA, psB, t0, n, bi)
                      yield

              def b_proj(h):
                  for _ in b_proj_gen(h):
                      pass

              qblks = [(256 + 512 * i, 512, NT) for i in range(4)] + ([(0, 256, 2)] if upd else [])
              items = []
              for h in range(4):
                  for bq, (q0, nq, nk) in enumerate(qblks):
                      for i in range(nk):
                          items.append((h, bq, q0, nq, nk, i))
              LA = 3
              pos = {}
              gen = [None]
              b_proj(0)
              for idx in range(len(items) + LA):
                  if idx < len(items):
                      h, bq, q0, nq, nk, i = items[idx]
                      if bq == 0 and i == 0:
                          if gen[0] is not None:
                              for _ in gen[0]:
                                  pass
                          gen[0] = b_proj_gen(h + 1) if h + 1 < 4 else None
                      elif gen[0] is not None and idx % 6 == 3:
                          try:
                              next(gen[0])
                          except StopIteration:
                              gen[0] = None
                      if i == 0:
                          pos[(h, bq)] = nbO()
                      ps = nbF()
                      P.mm(ps[:, 0:nq], KTh[h % 2][0:96, i * 128:(i + 1) * 128], QTh[h % 2][0:96, q0:q0 + nq])
                      P.act(PTb[idx % 4][:, 0:nq], ps[:, 0:nq], AF.Exp, scale=sc_b)
                  if idx >= LA:
                      h, bq, q0, nq, nk, k_ = items[idx - LA]
                      nqs = nq // 128
                      po = pos[(h, bq)]
                      for qs in range(nqs):
                          P.mm(po[:, qs * 65:(qs + 1) * 65], PTb[(idx - LA) % 4][:, qs * 128:(qs + 1) * 128], Vb[:, k_, h, :],
                               start=(k_ == 0 and qs == 0), stop=(k_ == nk - 1), skip=True)
                      if k_ == nk - 1:
                          po3 = po[:, 0:nqs * 65].rearrange("p (q d) -> p q d", d=65)
                          rd = rdenb[rbi[0] % 2]
                          rbi[0] += 1
                          P.recip(rd[:, 0:nqs], po3[:, :, 64])
                          for qs in range(nqs):
                              qt = q0 // 128 + qs
                              P.ts("dve", ob[:, qt, h * 64:(h + 1) * 64], po[:, qs * 65:qs * 65 + 64], rd[:, qs:qs + 1])
                          del pos[(h, bq)]
              for qt in (list(range(2, NT)) + ([0, 1] if upd else [])):
                  pT = nbT()
                  for ft in range(2):
                      P.tr(pT[:, ft * 128:(ft + 1) * 128], ob[:, qt, ft * 128:(ft + 1) * 128], ident[:])
                  for ft in range(2):
                      o = mixT[:, 2 + ft, qt * 128:(qt + 1) * 128]
                      P.tt("dve", o, pT[:, ft * 128:(ft + 1) * 128], o, ALU.mult)

              if stop == 'B':
                  raise _Stop()
              wcur = wnext
              CB.reset()
              CF.reset()
              ugT = CB.take([2, T])
              vn = CB.take([NT, 256])
              sgt = [CB.take([512]) for _ in range(2)]
              gv = CF.take([NT, 256])
              sq = CF.take([256])
              vt = [CF.take([256]) for _ in range(2)]
              s1 = stat[:, 96:96 + 72].rearrange("p (t g) -> p t g", g=4)
              s2 = stat[:, 168:168 + 72].rearrange("p (t g) -> p t g", g=4)
              wnext = load_group(l, 3)
              cblks = TBLK if upd else TBLK[1:]
              ctiles = list(range(NT)) if upd else list(range(2, NT))

              def ev_u(ft, t0, n, ps):
                  P.act(ugT[:, ft, t0:t0 + n], ps[:, 0:n], AF.Gelu)

              for tt in ctiles:
                  ps = nbF()
                  tm_proj(wcur, 256, 256, tt, ps)
                  P.act(gv[:, tt, :], ps[:, 0:256], AF.Gelu)
                  P.red(s1[:, tt, :], gv[:, tt, :].rearrange("p (g c) -> p g c", g=4))
                  P.tt("pool", sq, gv[:, tt, :], gv[:, tt, :], ALU.mult)
                  P.red(s2[:, tt, :], sq.rearrange("p (g c) -> p g c", g=4))
              s1f = stat[:, 96:96 + 72]
              s2f = stat[:, 168:168 + 72]
              m2 = CF.take([72])
              P.ts("dve", s1f, s1f, 1.0 / 64.0)
              P.tt("dve", m2, s1f, s1f, ALU.mult)
              P.stt("dve", s2f, s2f, 1.0 / 64.0, m2, ALU.mult, ALU.subtract)
              rstd_from_ms(s2f, s2f, 72)
              P.stt("dve", s1f, s1f, -1.0, s2f, ALU.mult, ALU.mult)

              sgi = [0]

              def c_fm(ft, t0, n, c0, kind):
                  ps = nbF()
                  for kt in range(KT):
                      P.mm(ps[:, 0:n], wcur[:, kt, c0 + ft * 128:c0 + (ft + 1) * 128], hxT[:, kt, t0:t0 + n],
                           start=(kt == 0), stop=(kt == KT - 1))
                  if kind == "u":
                      P.act(ugT[:, ft, t0:t0 + n], ps[:, 0:n], AF.Gelu)
                  else:
                      sg_ = sgt[sgi[0] % 2]
                      sgi[0] += 1
                      P.act(sg_[:, 0:n], ps[:, 0:n], AF.Silu)
                      P.tt("pool", ugT[:, ft, t0:t0 + n], ugT[:, ft, t0:t0 + n], sg_[:, 0:n], ALU.mult)

              def c_norm(tt):
                  v_ = vt[tt % 2]
                  for g in range(4):
                      P.act(v_[:, g * 64:(g + 1) * 64], gv[:, tt, g * 64:(g + 1) * 64], AF.Identity,
                            bias=s1[:, tt, g:g + 1], scale=s2[:, tt, g:g + 1])
                  P.tt("pool", vn[:, tt, :], v_, lngbc[:], ALU.mult)

              fitems = [(ft, t0, n, 0, "u") for ft in range(2) for (t0, n) in cblks] + \
                       [(ft, t0, n, 512, "g") for ft in range(2) for (t0, n) in cblks]
              nitems = list(ctiles)
              while fitems or nitems:
                  if fitems:
                      c_fm(*fitems.pop(0))
                  if nitems:
                      c_norm(nitems.pop(0))
              for tt in ctiles:
                  ps = nbF()
                  for ft in range(2):
                      for hf in range(2):
                          g = 2 * ft + hf
                          o = ps[:, (ft * 2 + hf) * 128:(ft * 2 + hf + 1) * 128]
                          P.mm(o, vn[:, tt, ft * 128:(ft + 1) * 128], wsT[:, g, :], start=True, stop=False)
                          P.mm(o, ones_b[0:2, :], bs2[0:2, g * 128:(g + 1) * 128], start=False, stop=True)
                  for ft in range(2):
                      for hf in range(2):
                          pr = slice(hf * 64, (hf + 1) * 64)
                          P.tt("dve", mixT[pr, 4 + ft, tt * 128:(tt + 1) * 128],
                               ps[pr, (ft * 2 + hf) * 128:(ft * 2 + hf + 1) * 128],
                               ugT[pr, ft, tt * 128:(tt + 1) * 128], ALU.mult)

              if stop == 'C':
                  raise _Stop()
              wcur = wnext
              CB.reset()
              CF.reset()
              bg = CB.take([2, T])
              zl = CF.take([2, SEQ + 2])
              zc = CF.take([2, CTXL + 2])
              dcs = [CF.take([512]) for _ in range(2)]
              sgd = [CF.take([512]) for _ in range(2)]
              yt = [CF.take([512]) for _ in range(2)]
              P.memset("pool", zl[:, :, 0:1], 0.0)
              P.memset("pool", zl[:, :, SEQ + 1:SEQ + 2], 0.0)
              P.memset("pool", zc[:, :, 0:1], 0.0)
              P.memset("pool", zc[:, :, CTXL + 1:CTXL + 2], 0.0)
              if s == 0:
                  wA_pref[0] = load_group(l, 0)
              CBw = Carver(arb, AB)
              CBw.reset(2 * T)
              wo = CBw.take([KT, 1024])
              for k2 in range(0, KT, 2):
                  P.dma("pool", wo[:, k2:k2 + 2, :], wout_d[l].rearrange("(kt p) c -> p kt c", p=128)[:, k2:k2 + 2, :])
              di = 0
              yi = [0]

              def conv_block(ft, t0, n):
                  zb, zo = (zc, t0) if t0 < CTXL else (zl, t0 - CTXL)
                  y = yt[yi[0] % 2]
                  yi[0] += 1
                  P.act(y[:, 0:n], zb[:, ft, 1 + zo:1 + zo + n], AF.Identity, bias=zeroT[:, 0:1], scale=scw[:, ft, 1:2])
                  P.stt("dve", y[:, 0:n], zb[:, ft, zo:zo + n], scw[:, ft, 0:1], y[:, 0:n], ALU.mult, ALU.add)
                  P.stt("dve", y[:, 0:n], zb[:, ft, 2 + zo:2 + zo + n], scw[:, ft, 2:3], y[:, 0:n], ALU.mult, ALU.add)
                  P.tt("pool", mixT[:, 6 + ft, t0:t0 + n], y[:, 0:n], bg[:, ft, t0:t0 + n], ALU.mult)

              for ft in range(2):
                  prev = None
                  for (t0, n) in cblks:
                      ps_c, ps_h, ps_b, ps_g = nbF(), nbF(), nbF(), nbF()
                      for (ps, c0) in ((ps_c, 256), (ps_h, 512), (ps_b, 0), (ps_g, 768)):
                          for kt in range(KT):
                              P.mm(ps[:, 0:n], wcur[:, kt, c0 + ft * 128:c0 + (ft + 1) * 128], hxT[:, kt, t0:t0 + n],
                                   start=(kt == 0), stop=(kt == KT - 1))
                      d_, g_ = dcs[di % 2], sgd[di % 2]
                      di += 1
                      P.cp("act", d_[:, 0:n], ps_c[:, 0:n])
                      P.act(g_[:, 0:n], ps_g[:, 0:n], AF.Silu)
                      zdst = zc[:, ft, 1 + t0:1 + t0 + n] if t0 < CTXL else zl[:, ft, 1 + t0 - CTXL:1 + t0 - CTXL + n]
                      P.tt("dve", zdst, ps_h[:, 0:n], d_[:, 0:n], ALU.mult)
                      P.tt("dve", bg[:, ft, t0:t0 + n], ps_b[:, 0:n], g_[:, 0:n], ALU.mult)
                      if t0 < CTXL:
                          conv_block(ft, t0, n)
                      else:
                          if prev is not None:
                              conv_block(ft, *prev)
                          prev = (t0, n)
                  conv_block(ft, *prev)
              if s == 0:
                  tap("mixT%d" % l, mixT[:])

              if stop == 'D':
                  raise _Stop()
              CF.reset()
              gbc = CF.take([1024])
              gbcc = CF.take([1024])
              fg = CF.take([1024])
              xo = [CF.take([1024]) for _ in range(4)]
              tmps2 = [CF.take([512]) for _ in range(2)]
              P.dma("sp", gbc, grow_d[l, s].partition_broadcast(128))
              if upd:
                  P.dma("sp", gbcc, grow_d[l, 2].partition_broadcast(128))
              if last:
                  P.dma("sp", fg, finalg_d.partition_broadcast(128))
              sso = stat[:, 0:NT]
              rso = stat[:, 32:32 + NT]
              P.memset("dve", sso, 0.0)
              junko = CB.take([1024]) if False else arb[:, 0:1024]
              otiles = list(range(2, NT)) + ([0, 1] if upd else [])
              xo4 = xo
              pso = {}

              def o_src(tt):
                  return (xsrc[s, (tt - 2) * 128:(tt - 1) * 128, :] if tt >= 2 else csrc[s, tt * 128:(tt + 1) * 128, :])

              def o_s0(oi):
                  tt = otiles[oi]
                  P.dma("sp", xo4[oi % 4], o_src(tt))
                  banks = []
                  for hf in range(2):
                      ps = nbF()
                      banks.append(ps)
                      for kt in range(KT):
                          P.mm(ps[:, 0:512], mixT[:, kt, tt * 128:(tt + 1) * 128], wo[:, kt, hf * 512:(hf + 1) * 512],
                               start=(kt == 0), stop=(kt == KT - 1))
                  pso[oi] = banks

              def o_s1(oi):
                  tt = otiles[oi]
                  g_ = gbc if tt >= 2 else gbcc
                  xt = xo4[oi % 4]
                  banks = pso.pop(oi)
                  for hf in range(2):
                      tmpo = tmps2[hf]
                      P.tt("dve", tmpo[:, 0:512], banks[hf][:, 0:512], g_[:, hf * 512:(hf + 1) * 512], ALU.mult)
                      P.tt("pool", xt[:, hf * 512:(hf + 1) * 512], tmpo[:, 0:512], xt[:, hf * 512:(hf + 1) * 512], ALU.add)
                  if last and tt >= 2:
                      P.act(junko, xt, AF.Square, scale=1.0 / 32.0, accum_out=sso[:, tt:tt + 1])

              def o_s2(oi):
                  tt = otiles[oi]
                  xt = xo4[oi % 4]
                  if not last:
                      dst = (xs_d[s, (tt - 2) * 128:(tt - 1) * 128, :] if tt >= 2 else cs_d[s, tt * 128:(tt + 1) * 128, :])
                      P.dma("act", dst, xt)
                  elif tt >= 2:
                      rstd_from_ms(rso[:, tt:tt + 1], sso[:, tt:tt + 1], 1)
                      P.stt("dve", xt, xt, rso[:, tt:tt + 1], fg, ALU.mult, ALU.mult)
                      P.dma("act", out_d[s, (tt - 2) * 128:(tt - 1) * 128, :], xt)

              pipeline(len(otiles), [o_s0, o_s1, o_s2])

    except _Stop:
        pass
    P.emit(nc)
    es.close()
    return nc, list(tap_out.keys())


def _prep_shared(inp):
    f = lambda a: np.ascontiguousarray(np.asarray(a, dtype=np.float32))
    C, S = _rope_tables()
    pr = _rope_perm32()
    perm_q = np.concatenate([np.concatenate([np.arange(64), 64 + pr]) + 96 * h for h in range(4)])
    perm_k = np.concatenate([np.arange(64), 64 + pr])
    ri, ci, inw = _na_index_tables()
    rpb = f(inp["na_rpb"])
    nab = rpb[:, :, ri, ci]
    nab = np.where(inw[None, None], nab, np.float32(MASK_FILL)).astype(np.float32)
    nab = np.ascontiguousarray(nab.transpose(0, 3, 1, 2, 4).reshape(2, 128, 4 * NTAB, 128))
    w_in = f(inp["w_in"])
    w_uq = f(inp["mla_w_uq"])
    sh = {
        "ident": np.eye(128, dtype=np.float32),
        "sel2": np.eye(2, dtype=np.float32),
        "ropeC": C, "ropeS": S,
        "norm_gT": f(f(inp["norm_g"]).reshape(2, 8, 128).transpose(0, 2, 1)),
        "final_g": f(inp["final_g"]),
        "w_mod": f(inp["w_mod"]),
        "b_mod": f(inp["b_mod"]),
        "w_in": w_in,
        "w_krsw": f(w_in[:, :, 1344:1440][:, :, perm_k]),
        "w_out": f(inp["w_out"]),
        "w_uq": w_uq,
        "w_uqsw": f(w_uq[:, :, perm_q]),
        "qn_gT": f(f(inp["mla_qn_g"]).reshape(2, 2, 128).transpose(0, 2, 1)),
        "w_ukv": f(inp["mla_w_ukv"]),
        "kvn_gT": f(f(inp["mla_kvn_g"]).reshape(2, 1, 128).transpose(0, 2, 1)),
        "w_sT": f(f(inp["cm_w_s"]).transpose(0, 3, 1, 2)),
        "b_s": f(f(inp["cm_b_s"]).reshape(2, 512)),
        "ln_g": f(inp["cm_ln_g"]),
        "sc_wT": f(f(inp["sc_w"]).reshape(2, 3, 2, 128).transpose(0, 3, 2, 1)),
        "na_bias": nab,
    }
    return sh


_CACHE = {}


def kernel(x, c, ctx, c_ctx, norm_g, w_mod, b_mod, w_in, na_rpb, mla_qn_g, mla_w_uq, mla_kvn_g,
           mla_w_ukv, cm_ln_g, cm_w_s, cm_b_s, sc_w, w_out, final_g):
    inp = dict(x=x, c=c, ctx=ctx, c_ctx=c_ctx, norm_g=norm_g, w_mod=w_mod, b_mod=b_mod, w_in=w_in,
               na_rpb=na_rpb, mla_qn_g=mla_qn_g, mla_w_uq=mla_w_uq, mla_kvn_g=mla_kvn_g,
               mla_w_ukv=mla_w_ukv, cm_ln_g=cm_ln_g, cm_w_s=cm_w_s, cm_b_s=cm_b_s, sc_w=sc_w,
               w_out=w_out, final_g=final_g)
    sh = _prep_shared(inp)
    x = np.asarray(x, dtype=np.float32)
    ctx = np.asarray(ctx, dtype=np.float32)
    c = np.asarray(c, dtype=np.float32)
    c_ctx = np.asarray(c_ctx, dtype=np.float32)
    if "nc" not in _CACHE:
        _CACHE["nc"] = build_program()[0]
    nc = _CACHE["nc"]
    in_maps = []
    for i in range(NCORES):
        cc = np.stack([c[2 * i], c[2 * i + 1], c_ctx], axis=0)
        cT = np.ascontiguousarray(cc.reshape(3, 8, 128).transpose(2, 1, 0))
        m = dict(sh)
        m["x"] = np.ascontiguousarray(x[2 * i:2 * i + 2])
        m["ctx"] = np.ascontiguousarray(ctx[2 * i:2 * i + 2])
        m["cT"] = cT
        in_maps.append(m)
    res = run_bass_kernel_spmd(nc, in_maps, core_ids=list(range(NCORES)))
    return np.concatenate([np.asarray(r["out"], dtype=np.float32) for r in res.results], axis=0)
```
